# Optimizing a Trainium2 kernel written in Bass

```python
import math
import jax
import jax.numpy as jnp
from jax import lax
import numpy as np

D_MODEL = 2048
BATCH = 2
SEQ = 4096
DEPTH = 2
DEC_BATCH = 32
DEC_SEQ = 8
PAST_LEN = 8192
PAGE_SIZE = 128

HEAD_DIM = 64
D_CONV = D_MODEL // 4
D_SSM = D_MODEL // 4
D_ATTN = D_MODEL - D_CONV - D_SSM
CONV_A_WIDTH = 3
SSM_HEADS = D_SSM // HEAD_DIM
SSM_GROUPS = 2
SSM_STATE = 128
SSM_CONV_WIDTH = 4
SSM_CHUNK = 128
SSM_CONV_DIM = D_SSM + 2 * SSM_GROUPS * SSM_STATE
ATTN_HEADS = D_ATTN // HEAD_DIM
KV_HEADS = 4
Q_PER_KV = ATTN_HEADS // KV_HEADS
CMP_BLOCK = 32
CMP_STRIDE = 16
CMP_SPAN = CMP_BLOCK // CMP_STRIDE
CMP_HIDDEN = 128
SEL_BLOCK = 64
CMP_PER_SEL = SEL_BLOCK // CMP_STRIDE
TOP_BLOCKS = 16
WINDOW = 512
Q_BLOCK = 128
D_FF = ((8 * D_MODEL // 3 + 255) // 256) * 256
COLS_A = 3 * D_CONV
COLS_B = D_SSM + SSM_CONV_DIM + SSM_HEADS
KV_COLS = 3 * 2 * KV_HEADS * HEAD_DIM
COLS_C = D_ATTN + KV_COLS + 3 * ATTN_HEADS
D_IN_PROJ = COLS_A + COLS_B + COLS_C
RMS_EPS = 1e-6
NEG_INF = -1e30
TINY = 1e-30
FORCE_BONUS = 1e3
ATTN_SCALE = HEAD_DIM ** -0.5

kernel_name = 'hymba_shortconv_ssd_nsa_step'


def rms_norm(x, gain):
    xf = x.astype(jnp.float32)
    y = xf * lax.rsqrt(jnp.mean(xf * xf, axis=-1, keepdims=True) + RMS_EPS)
    return (y * gain.astype(jnp.float32)).astype(x.dtype)


def masked_softmax(s, mask):
    s = jnp.where(mask, s, NEG_INF)
    p = jnp.exp(s - jnp.max(s, axis=-1, keepdims=True)) * mask
    return p / jnp.maximum(jnp.sum(p, axis=-1, keepdims=True), TINY)


def causal_depthwise_conv(u, prefix, w):
    width = w.shape[0]
    t = u.shape[1]
    ext = jnp.concatenate([prefix.astype(u.dtype), u], axis=1)
    y = ext[:, 0:t] * w[0]
    for k in range(1, width):
        y = y + ext[:, k:k + t] * w[k]
    return y, ext[:, t:]


def short_conv_mixer(u_a, conv_prefix, conv_w):
    b_gate, c_gate, x_in = jnp.split(u_a, 3, axis=-1)
    y, new_prefix = causal_depthwise_conv(c_gate * x_in, conv_prefix, conv_w)
    return b_gate * y, new_prefix


def ssd_chunked_scan(x, dt, a_log, bm, cm, h0):
    bsz, t, h, p = x.shape
    n = bm.shape[-1]
    lc = math.gcd(t, SSM_CHUNK)
    nc = t // lc
    f32 = jnp.float32
    log_a = dt * (-jnp.exp(a_log.astype(f32)))
    xdt = (x.astype(f32) * dt[..., None]).reshape(bsz, nc, lc, h, p)
    bc = bm.astype(f32).reshape(bsz, nc, lc, h, n)
    cc = cm.astype(f32).reshape(bsz, nc, lc, h, n)
    cum = jnp.cumsum(log_a.reshape(bsz, nc, lc, h), axis=2)
    causal = jnp.tril(jnp.ones((lc, lc), dtype=bool))[None, None, :, :, None]
    seg = cum[:, :, :, None, :] - cum[:, :, None, :, :]
    decay = jnp.exp(jnp.where(causal, seg, NEG_INF))
    scores = jnp.einsum('bclhn,bcshn->bclsh', cc, bc) * decay
    y_intra = jnp.einsum('bclsh,bcshp->bclhp', scores, xdt)
    to_end = jnp.exp(cum[:, :, -1:, :] - cum)
    chunk_states = jnp.einsum('bclhn,bclh,bclhp->bchpn', bc, to_end, xdt)
    chunk_decay = jnp.exp(cum[:, :, -1, :])

    def step(state, inp):
        st, dec = inp
        return state * dec[:, :, None, None] + st, state

    h_final, h_enter = lax.scan(step, h0.astype(f32),
                                (jnp.moveaxis(chunk_states, 1, 0), jnp.moveaxis(chunk_decay, 1, 0)))
    h_enter = jnp.moveaxis(h_enter, 0, 1)
    y_inter = jnp.einsum('bclhn,bchpn,bclh->bclhp', cc, h_enter, jnp.exp(cum))
    return (y_intra + y_inter).reshape(bsz, t, h, p), h_final


def ssd_mixer(u_b, conv_prefix, h0, conv_w, conv_bias, dt_bias, a_log, d_skip):
    bsz, t, _ = u_b.shape
    z, xbc, dt_raw = jnp.split(u_b, [D_SSM, D_SSM + SSM_CONV_DIM], axis=-1)
    xbc_c, new_prefix = causal_depthwise_conv(xbc, conv_prefix, conv_w)
    xbc_c = jax.nn.silu(xbc_c + conv_bias)
    xs, bm, cm = jnp.split(xbc_c, [D_SSM, D_SSM + SSM_GROUPS * SSM_STATE], axis=-1)
    xs = xs.reshape(bsz, t, SSM_HEADS, HEAD_DIM)
    rep = SSM_HEADS // SSM_GROUPS
    bm = jnp.repeat(bm.reshape(bsz, t, SSM_GROUPS, SSM_STATE), rep, axis=2)
    cm = jnp.repeat(cm.reshape(bsz, t, SSM_GROUPS, SSM_STATE), rep, axis=2)
    dt = jax.nn.softplus(dt_raw.astype(jnp.float32) + dt_bias.astype(jnp.float32))
    y, h_final = ssd_chunked_scan(xs, dt, a_log, bm, cm, h0)
    y = y + d_skip.astype(jnp.float32)[:, None] * xs.astype(jnp.float32)
    y = y.reshape(bsz, t, D_SSM).astype(u_b.dtype) * jax.nn.silu(z)
    return y, new_prefix, h_final.astype(h0.dtype)


def compress_blocks(rows, pos_emb, w1, w2):
    bsz, lp, g, dh = rows.shape
    n_chunks = lp // CMP_STRIDE
    n_blocks = n_chunks - CMP_SPAN + 1
    chunks = rows.reshape(bsz, n_chunks, CMP_STRIDE, g, dh)
    blocks = jnp.concatenate([chunks[:, k:k + n_blocks] for k in range(CMP_SPAN)], axis=2)
    blocks = blocks + pos_emb[:, None, :]
    flat = jnp.transpose(blocks, (0, 1, 3, 2, 4)).reshape(bsz, n_blocks, g, CMP_BLOCK * dh)
    return jax.nn.gelu(flat @ w1) @ w2


def nsa_mixer(u_c, q0, k_past, v_past, win_k_prefix, win_v_prefix, n_win_keep,
              q_gain, k_gain, cmp_pos, cmp_w1, cmp_w2):
    bsz, t, _ = u_c.shape
    q_raw, kv, g_raw = jnp.split(u_c, [D_ATTN, D_ATTN + KV_COLS], axis=-1)
    q = rms_norm(q_raw.reshape(bsz, t, ATTN_HEADS, HEAD_DIM), q_gain)
    q = q.reshape(bsz, t, KV_HEADS, Q_PER_KV, HEAD_DIM)
    kv = kv.reshape(bsz, t, 3, 2, KV_HEADS, HEAD_DIM)
    gates = jax.nn.sigmoid(g_raw.reshape(bsz, t, KV_HEADS, Q_PER_KV, 3))
    k_sel_new = rms_norm(kv[:, :, 1, 0], k_gain[1])
    k_win_new = rms_norm(kv[:, :, 2, 0], k_gain[2])
    new_k_rows = jnp.stack([kv[:, :, 0, 0], k_sel_new], axis=2)
    new_v_rows = jnp.stack([kv[:, :, 0, 1], kv[:, :, 1, 1]], axis=2)

    k_full = jnp.concatenate([k_past.astype(u_c.dtype), new_k_rows], axis=1)
    v_full = jnp.concatenate([v_past.astype(u_c.dtype), new_v_rows], axis=1)
    length = k_full.shape[1]
    pad = (-length) % SEL_BLOCK
    k_full = jnp.pad(k_full, ((0, 0), (0, pad), (0, 0), (0, 0), (0, 0)))
    v_full = jnp.pad(v_full, ((0, 0), (0, pad), (0, 0), (0, 0), (0, 0)))
    lp = length + pad
    n_sel = lp // SEL_BLOCK
    topk = min(TOP_BLOCKS, n_sel)

    k_cmp = rms_norm(compress_blocks(k_full[:, :, 0], cmp_pos[0], cmp_w1[0], cmp_w2[0]), k_gain[0])
    v_cmp = compress_blocks(v_full[:, :, 0], cmp_pos[1], cmp_w1[1], cmp_w2[1])
    n_cmp = k_cmp.shape[1]
    cmp_end = jnp.arange(n_cmp, dtype=jnp.int32) * CMP_STRIDE + CMP_BLOCK - 1

    def to_blocks(r):
        r = r.reshape(bsz, n_sel, SEL_BLOCK, KV_HEADS, HEAD_DIM)
        return jnp.transpose(r, (0, 3, 1, 2, 4)).reshape(bsz, KV_HEADS, n_sel, SEL_BLOCK * HEAD_DIM)

    k_sel_blocks = to_blocks(k_full[:, :, 1])
    v_sel_blocks = to_blocks(v_full[:, :, 1])

    ext_k = jnp.concatenate([win_k_prefix.astype(u_c.dtype), k_win_new], axis=1)
    ext_v = jnp.concatenate([win_v_prefix.astype(u_c.dtype), kv[:, :, 2, 1]], axis=1)

    slopes = jnp.exp2(-8.0 * jnp.arange(1, ATTN_HEADS + 1, dtype=jnp.float32) / ATTN_HEADS)
    slopes = slopes.reshape(KV_HEADS, Q_PER_KV)
    qb_size = math.gcd(t, Q_BLOCK)
    n_qb = t // qb_size
    b_ix = jnp.arange(bsz)[:, None, None, None]
    g_ix = jnp.arange(KV_HEADS)[None, :, None, None]
    blk = jnp.arange(n_sel, dtype=jnp.int32)
    f32 = jnp.float32

    def block_fn(i):
        qs = i * qb_size
        qb = lax.dynamic_slice_in_dim(q, qs, qb_size, axis=1)
        gb = lax.dynamic_slice_in_dim(gates, qs, qb_size, axis=1)
        tpos = q0 + qs + jnp.arange(qb_size, dtype=jnp.int32)
        dist_c = tpos[:, None] - cmp_end[None, :]
        s_c = (jnp.einsum('bqgrd,bngd->bgrqn', qb, k_cmp).astype(f32) * ATTN_SCALE
               - slopes[:, :, None, None] * dist_c.astype(f32))
        p_c = masked_softmax(s_c, dist_c >= 0)
        o_c = jnp.einsum('bgrqn,bngd->bqgrd', p_c.astype(v_cmp.dtype), v_cmp)
        imp = jnp.pad(jnp.sum(p_c, axis=2), ((0, 0), (0, 0), (0, 0), (0, 1)))
        imp = jnp.sum(imp.reshape(bsz, KV_HEADS, qb_size, n_sel, CMP_PER_SEL), axis=-1)
        cur = tpos // SEL_BLOCK
        forced = (blk[None, :] == 0) | (blk[None, :] == cur[:, None]) | (blk[None, :] == cur[:, None] - 1)
        valid = blk[None, :] <= cur[:, None]
        imp = jnp.where(valid, imp + FORCE_BONUS * forced, NEG_INF)
        _, idx = lax.top_k(imp, topk)
        kg = k_sel_blocks[b_ix, g_ix, idx].reshape(bsz, KV_HEADS, qb_size, topk * SEL_BLOCK, HEAD_DIM)
        vg = v_sel_blocks[b_ix, g_ix, idx].reshape(bsz, KV_HEADS, qb_size, topk * SEL_BLOCK, HEAD_DIM)
        key_pos = (idx[..., None] * SEL_BLOCK + jnp.arange(SEL_BLOCK, dtype=jnp.int32))
        key_pos = key_pos.reshape(bsz, KV_HEADS, qb_size, topk * SEL_BLOCK)
        dist_s = tpos[None, None, :, None] - key_pos
        s_s = (jnp.einsum('bqgrd,bgqkd->bgrqk', qb, kg).astype(f32) * ATTN_SCALE
               - slopes[None, :, :, None, None] * dist_s[:, :, None].astype(f32))
        p_s = masked_softmax(s_s, (dist_s >= 0)[:, :, None])
        o_s = jnp.einsum('bgrqk,bgqkd->bqgrd', p_s.astype(vg.dtype), vg)
        kw = lax.dynamic_slice_in_dim(ext_k, qs, WINDOW + qb_size, axis=1)
        vw = lax.dynamic_slice_in_dim(ext_v, qs, WINDOW + qb_size, axis=1)
        key_pos_w = q0 - WINDOW + qs + jnp.arange(WINDOW + qb_size, dtype=jnp.int32)
        dist_w = tpos[:, None] - key_pos_w[None, :]
        mask_w = (dist_w >= 0) & (dist_w <= WINDOW) & (key_pos_w[None, :] >= 0)
        s_w = (jnp.einsum('bqgrd,bkgd->bgrqk', qb, kw).astype(f32) * ATTN_SCALE
               - slopes[:, :, None, None] * dist_w.astype(f32))
        p_w = masked_softmax(s_w, mask_w)
        o_w = jnp.einsum('bgrqk,bkgd->bqgrd', p_w.astype(vw.dtype), vw)
        return gb[..., 0:1] * o_c + gb[..., 1:2] * o_s + gb[..., 2:3] * o_w

    o = lax.map(block_fn, jnp.arange(n_qb, dtype=jnp.int32))
    o = jnp.moveaxis(o, 0, 1).reshape(bsz, t, D_ATTN)
    return o, new_k_rows, new_v_rows, ext_k[:, -n_win_keep:], ext_v[:, -n_win_keep:]


def decoder_layer(x, q0, k_past, v_past, win_k_prefix, win_v_prefix, n_win_keep,
                  conv_a_prefix, conv_b_prefix, ssm_h0, params):
    (g_mix, w_in_l, conv_a_w_l, conv_b_w_l, conv_b_bias_l, dt_bias_l, a_log_l, d_skip_l,
     q_gain, k_gain, cmp_pos_l, cmp_w1_l, cmp_w2_l, g_out, w_out_l, g_ffn,
     w_gate_l, w_up_l, w_down_l) = params
    u = rms_norm(x, g_mix) @ w_in_l
    u_a, u_b, u_c = jnp.split(u, [COLS_A, COLS_A + COLS_B], axis=-1)
    y_a, new_conv_a = short_conv_mixer(u_a, conv_a_prefix, conv_a_w_l)
    y_b, new_conv_b, new_ssm = ssd_mixer(u_b, conv_b_prefix, ssm_h0, conv_b_w_l, conv_b_bias_l,
                                         dt_bias_l, a_log_l, d_skip_l)
    y_c, new_k, new_v, new_win_k, new_win_v = nsa_mixer(
        u_c, q0, k_past, v_past, win_k_prefix, win_v_prefix, n_win_keep,
        q_gain, k_gain, cmp_pos_l, cmp_w1_l, cmp_w2_l)
    merged = jnp.concatenate([
        rms_norm(y_a, g_out[:D_CONV]),
        rms_norm(y_b, g_out[D_CONV:D_CONV + D_SSM]),
        rms_norm(y_c, g_out[D_CONV + D_SSM:])], axis=-1)
    x = x + merged @ w_out_l
    h = rms_norm(x, g_ffn)
    x = x + (jax.nn.silu(h @ w_gate_l) * (h @ w_up_l)) @ w_down_l
    return x, (new_k, new_v, new_win_k, new_win_v, new_conv_a, new_conv_b, new_ssm)


def setup_inputs(seed: int = 0) -> dict:
    key = jax.random.key(seed)
    ks = jax.random.split(key, 32)
    f32 = jnp.float32
    n_pages = PAST_LEN // PAGE_SIZE
    n_used = DEC_BATCH * n_pages
    n_pool = n_used + (n_used + 3) // 4
    win_buf = min(WINDOW, PAST_LEN)

    def normal(k, shape, scale):
        return jax.random.normal(k, shape, f32) * scale

    def gain(k, shape):
        return 1.0 + normal(k, shape, 0.02)

    dt_init = jnp.exp(jax.random.uniform(ks[12], (DEPTH, SSM_HEADS), f32, math.log(1e-3), math.log(1e-1)))
    return {
        'x_prompt': normal(ks[0], (BATCH, SEQ, D_MODEL), 1.0),
        'x_sample': normal(ks[1], (DEC_BATCH, DEC_SEQ, D_MODEL), 1.0),
        'cache_k': normal(ks[2], (DEPTH, n_pool, PAGE_SIZE, 2, KV_HEADS, HEAD_DIM), 1.0),
        'cache_v': normal(ks[3], (DEPTH, n_pool, PAGE_SIZE, 2, KV_HEADS, HEAD_DIM), 1.0),
        'cache_win_k': normal(ks[4], (DEPTH, DEC_BATCH, win_buf, KV_HEADS, HEAD_DIM), 1.0),
        'cache_win_v': normal(ks[5], (DEPTH, DEC_BATCH, win_buf, KV_HEADS, HEAD_DIM), 1.0),
        'state_conv_a': normal(ks[6], (DEPTH, DEC_BATCH, CONV_A_WIDTH - 1, D_CONV), 1.0),
        'state_conv_b': normal(ks[7], (DEPTH, DEC_BATCH, SSM_CONV_WIDTH - 1, SSM_CONV_DIM), 1.0),
        'state_ssm': normal(ks[8], (DEPTH, DEC_BATCH, SSM_HEADS, HEAD_DIM, SSM_STATE), 0.5),
        'page_table': jax.random.permutation(ks[9], n_pool)[:n_used].reshape(DEC_BATCH, n_pages).astype(jnp.int32),
        'norm_mix': gain(ks[10], (DEPTH, D_MODEL)),
        'w_in': normal(ks[11], (DEPTH, D_MODEL, D_IN_PROJ), D_MODEL ** -0.5),
        'conv_a_w': normal(ks[13], (DEPTH, CONV_A_WIDTH, D_CONV), CONV_A_WIDTH ** -0.5),
        'conv_b_w': normal(ks[14], (DEPTH, SSM_CONV_WIDTH, SSM_CONV_DIM), SSM_CONV_WIDTH ** -0.5),
        'conv_b_bias': normal(ks[15], (DEPTH, SSM_CONV_DIM), 0.02),
        'dt_bias': dt_init + jnp.log(-jnp.expm1(-dt_init)),
        'a_log': jnp.log(jax.random.uniform(ks[16], (DEPTH, SSM_HEADS), f32, 1.0, 16.0)),
        'd_skip': 1.0 + normal(ks[17], (DEPTH, SSM_HEADS), 0.1),
        'q_norm': gain(ks[18], (DEPTH, HEAD_DIM)),
        'k_norm': gain(ks[19], (DEPTH, 3, HEAD_DIM)),
        'cmp_pos': normal(ks[20], (DEPTH, 2, CMP_BLOCK, HEAD_DIM), 0.1),
        'cmp_w1': normal(ks[21], (DEPTH, 2, CMP_BLOCK * HEAD_DIM, CMP_HIDDEN), (CMP_BLOCK * HEAD_DIM) ** -0.5),
        'cmp_w2': normal(ks[22], (DEPTH, 2, CMP_HIDDEN, HEAD_DIM), CMP_HIDDEN ** -0.5),
        'norm_out': gain(ks[23], (DEPTH, D_MODEL)),
        'w_out': normal(ks[24], (DEPTH, D_MODEL, D_MODEL), D_MODEL ** -0.5),
        'norm_ffn': gain(ks[25], (DEPTH, D_MODEL)),
        'w_gate': normal(ks[26], (DEPTH, D_MODEL, D_FF), D_MODEL ** -0.5),
        'w_up': normal(ks[27], (DEPTH, D_MODEL, D_FF), D_MODEL ** -0.5),
        'w_down': normal(ks[28], (DEPTH, D_FF, D_MODEL), D_FF ** -0.5),
    }


def reference(x_prompt, x_sample, cache_k, cache_v, cache_win_k, cache_win_v,
              state_conv_a, state_conv_b, state_ssm, page_table,
              norm_mix, w_in, conv_a_w, conv_b_w, conv_b_bias, dt_bias, a_log, d_skip,
              q_norm, k_norm, cmp_pos, cmp_w1, cmp_w2, norm_out, w_out, norm_ffn,
              w_gate, w_up, w_down):
    bsz, t_prompt, _ = x_prompt.shape
    dec_b = x_sample.shape[0]
    dtype = x_prompt.dtype
    n_pages = page_table.shape[1]
    past_len = n_pages * PAGE_SIZE
    win_buf = cache_win_k.shape[2]
    win_pad = WINDOW - win_buf
    hp, hs = x_prompt, x_sample
    prompt_states = []
    sample_states = []
    for l in range(DEPTH):
        params = (norm_mix[l], w_in[l], conv_a_w[l], conv_b_w[l], conv_b_bias[l], dt_bias[l],
                  a_log[l], d_skip[l], q_norm[l], k_norm[l], cmp_pos[l], cmp_w1[l], cmp_w2[l],
                  norm_out[l], w_out[l], norm_ffn[l], w_gate[l], w_up[l], w_down[l])
        hp, st_p = decoder_layer(
            hp, 0,
            jnp.zeros((bsz, 0, 2, KV_HEADS, HEAD_DIM), dtype),
            jnp.zeros((bsz, 0, 2, KV_HEADS, HEAD_DIM), dtype),
            jnp.zeros((bsz, WINDOW, KV_HEADS, HEAD_DIM), dtype),
            jnp.zeros((bsz, WINDOW, KV_HEADS, HEAD_DIM), dtype),
            min(WINDOW, t_prompt),
            jnp.zeros((bsz, CONV_A_WIDTH - 1, D_CONV), dtype),
            jnp.zeros((bsz, SSM_CONV_WIDTH - 1, SSM_CONV_DIM), dtype),
            jnp.zeros((bsz, SSM_HEADS, HEAD_DIM, SSM_STATE), dtype),
            params)
        k_past = cache_k[l][page_table].reshape(dec_b, past_len, 2, KV_HEADS, HEAD_DIM)
        v_past = cache_v[l][page_table].reshape(dec_b, past_len, 2, KV_HEADS, HEAD_DIM)
        wk = jnp.pad(cache_win_k[l], ((0, 0), (win_pad, 0), (0, 0), (0, 0)))
        wv = jnp.pad(cache_win_v[l], ((0, 0), (win_pad, 0), (0, 0), (0, 0)))
        hs, st_s = decoder_layer(hs, past_len, k_past, v_past, wk, wv, win_buf,
                                 state_conv_a[l], state_conv_b[l], state_ssm[l], params)
        prompt_states.append(st_p)
        sample_states.append(st_s)

    def stack_state(states, j):
        return jnp.stack([s[j] for s in states], axis=0)

    new_k_prompt = stack_state(prompt_states, 0)
    new_v_prompt = stack_state(prompt_states, 1)
    new_win_k_prompt = stack_state(prompt_states, 2)
    new_win_v_prompt = stack_state(prompt_states, 3)
    new_conv_a_prompt = stack_state(prompt_states, 4)
    new_conv_b_prompt = stack_state(prompt_states, 5)
    new_ssm_prompt = stack_state(prompt_states, 6)
    new_k_sample = stack_state(sample_states, 0)
    new_v_sample = stack_state(sample_states, 1)
    new_win_k_sample = stack_state(sample_states, 2)
    new_win_v_sample = stack_state(sample_states, 3)
    new_conv_a_sample = stack_state(sample_states, 4)
    new_conv_b_sample = stack_state(sample_states, 5)
    new_ssm_sample = stack_state(sample_states, 6)
    return (hp, hs, new_k_prompt, new_v_prompt, new_win_k_prompt, new_win_v_prompt,
            new_conv_a_prompt, new_conv_b_prompt, new_ssm_prompt,
            new_k_sample, new_v_sample, new_win_k_sample, new_win_v_sample,
            new_conv_a_sample, new_conv_b_sample, new_ssm_sample)
```

```python
import math
from contextlib import ExitStack

import numpy as np
import concourse.bass as bass
import concourse.mybir as mybir
from concourse.bass_utils import run_bass_kernel_spmd

F32 = mybir.dt.float32
BF16 = mybir.dt.bfloat16
I32 = mybir.dt.int32
U32 = mybir.dt.uint32
AF = mybir.ActivationFunctionType
ALU = mybir.AluOpType
AX = mybir.AxisListType

D = 2048
KT = D // 128
DEPTH = 2
HD = 64
D_CONV = 512
D_SSM = 512
D_ATTN = 1024
SSM_H = 8
SSM_N = 128
XBC = 1024
NH = 16
NG = 4
RPG = 4
D_FF = 5632
FFT = D_FF // 128
COLS_A = 1536
COLS_B = 512 + 1024 + 8
KV_COLS = 1536
COLS_C = 1024 + 1536 + 48
D_IN = COLS_A + COLS_B + COLS_C
OFF_B = COLS_A
OFF_C = COLS_A + COLS_B
EPS = 1e-6
NEG = -30000.0
TB = 512
WIN = 512
ND = 12


class Trk:
    def __init__(self, nc, es):
        self.nc = nc
        self.eng = {"pe": nc.tensor, "act": nc.scalar, "dve": nc.vector,
                    "pool": nc.gpsimd, "sp": nc.sync}
        self.sem = {k: es.enter_context(nc.semaphore("s_" + k)) for k in self.eng}
        self.cnt = {k: 0 for k in self.eng}
        self.seen = {k: {} for k in self.eng}
        self.res = {}
        self.dq = ("sp", "pool", "act")
        self.dsem = {q: [es.enter_context(nc.semaphore("d_%s%d" % (q, i))) for i in range(ND)]
                     for q in self.dq}
        self.dval = {q: [0] * ND for q in self.dq}
        self.dnext = {q: 0 for q in self.dq}
        self.nwait = 0
        self.excl = set(["F%d" % i for i in range(8)] + ["B%d" % i for i in range(8)])

    def _wait(self, e, tok, raw):
        if tok is None:
            return
        key, h, v, src = tok
        if src == e and e == "pe":
            return
        if self.seen[e].get(key, 0) >= v:
            return
        self.eng[e].wait_ge(h, v)
        self.nwait += 1
        self.seen[e][key] = v

    def _deps(self, e, reads, writes):
        for k in reads:
            r = self.res.get(k)
            if r is not None:
                self._wait(e, r[0], True)
                if isinstance(k, str) and k in self.excl:
                    for t in r[1].values():
                        if t[3] != e:
                            self._wait(e, t, False)
        for k in writes:
            r = self.res.get(k)
            if r is not None:
                self._wait(e, r[0], False)
                for t in r[1].values():
                    self._wait(e, t, False)

    def _record(self, tok, reads, writes):
        for k in reads:
            r = self.res.setdefault(k, [None, {}])
            r[1][tok[0]] = tok
        for k in writes:
            self.res[k] = [tok, {}]

    def op(self, e, reads, writes, emit, sig=True):
        self._deps(e, reads, writes)
        ins = emit()
        if sig:
            self.cnt[e] += 1
            ins.then_inc(self.sem[e], 1)
            tok = (e, self.sem[e], self.cnt[e], e)
        else:
            tok = (e, self.sem[e], self.cnt[e] + 1, e)
        self._record(tok, reads, writes)
        return tok

    def dma(self, q, out, in_, reads, writes, indirect=None, **kw):
        self._deps(q, reads, writes)
        i = self.dnext[q]
        self.dnext[q] = (i + 1) % ND
        h = self.dsem[q][i]
        v = self.dval[q][i]
        if v > 0:
            self._wait(q, ((q, i), h, v, None), True)
        if indirect is not None:
            ins = self.eng[q].indirect_dma_start(out=out, out_offset=None, in_=in_,
                                                 in_offset=indirect, **kw)
        else:
            ins = self.eng[q].dma_start(out=out, in_=in_, **kw)
        ins.then_inc(h, 16)
        self.dval[q][i] = v + 16
        tok = ((q, i), h, v + 16, None)
        self._record(tok, reads, writes)
        return tok

    def all_tokens(self):
        toks = []
        for e in self.eng:
            if self.cnt[e] > 0:
                toks.append((e, self.sem[e], self.cnt[e], e))
        for q in self.dq:
            for i in range(ND):
                if self.dval[q][i] > 0:
                    toks.append(((q, i), self.dsem[q][i], self.dval[q][i], None))
        return toks

    def barrier(self, scratch):
        for t in self.all_tokens():
            if t[3] != "pool":
                self._wait("pool", t, True)
        if self.cnt["pool"] > 0:
            self._wait("pool", ("pool", self.sem["pool"], self.cnt["pool"], "pool"), True)
        self.cnt["pool"] += 1
        self.nc.gpsimd.memset(scratch, 0.0).then_inc(self.sem["pool"], 1)
        tok = ("pool", self.sem["pool"], self.cnt["pool"], "pool")
        for e in self.eng:
            if e != "pool":
                self._wait(e, tok, True)
        self.res = {}

    def finish(self):
        for t in self.all_tokens():
            if t[3] != "sp":
                self._wait("sp", t, True)


class Cfg:
    def __init__(self, seq=4096, past=8192, ns=4, npool=2560):
        self.SEQ = seq
        self.PAST = past
        self.NS = ns
        self.NPOOL = npool
        self.NBLK = seq // TB
        self.NPG = past // 128
        self.NTILE = seq // 128
        self.NCMP_P = seq // 16
        self.NCMP_S = past // 16 - 1
        self.NSEL_S = past // 64


def host_consts(cfg):
    c = {}
    c["c_ident"] = np.eye(128, dtype=np.float32)
    t = np.arange(128)
    c["c_tri"] = (t[:, None] <= t[None, :]).astype(np.float32)
    c["c_negtri"] = np.where(t[None, :] < t[:, None], NEG, 0.0).astype(np.float32)
    nd = max(cfg.NTILE, cfg.NPG) + 2
    c["c_kp"] = (t[:, None] - 64 - 128 * np.arange(nd)[None, :]).astype(np.float32)
    slopes = np.exp2(-8.0 * np.arange(1, NH + 1, dtype=np.float32) / NH).astype(np.float32)
    c["c_slope"] = np.broadcast_to(slopes[None, :, None], (128, NH, 128)).astype(np.float32).copy()
    nkk = max(cfg.SEQ, cfg.PAST)
    kk = np.arange(nkk)
    c["c_ee"] = (kk[None, :] // 64 == np.arange(128)[:, None]).astype(np.float32)
    c["c_eed"] = (kk[None, :] // 64 == (np.arange(128) % 64)[:, None]).astype(np.float32)
    ncol_p = ((cfg.NCMP_P + 127) // 128) * 128
    col = np.arange(ncol_p)
    sp = ((col[:, None] - 1) // 4 == np.arange(128)[None, :]) & (col[:, None] >= 1)
    c["c_selp"] = sp.astype(np.float32).reshape(ncol_p // 128, 128, 128).transpose(1, 0, 2).copy()
    ncol_s = ((cfg.NCMP_S + 127) // 128) * 128
    col = np.arange(ncol_s)
    ss = (col[:, None] // 4 == np.arange(128)[None, :])
    c["c_sels"] = ss.astype(np.float32).reshape(ncol_s // 128, 128, 128).transpose(1, 0, 2).copy()
    return c


import os
CUT = int(os.environ.get("KCUT", "0"))


CUTKIND = os.environ.get("KCUTK", "ps")
CUTI = int(os.environ.get("KCUTI", "0"))
CUTG = int(os.environ.get("KCUTG", "1"))


def CUTK(b):
    v = os.environ.get("KCUTP" if b.kind == "p" else "KCUTS")
    if v is not None:
        return int(v)
    return CUT if b.kind in CUTKIND else 0


class Cut(Exception):
    pass


def cutpoint(n):
    if CUT == n:
        raise Cut()


class Blk:
    def __init__(self, cfg, l, kind, bi):
        self.l = l
        self.kind = kind
        self.bi = bi
        if kind == "p":
            self.nt = TB // 128
            self.L = 128
            self.nseq = 1
            self.Lseq = TB
        else:
            self.nt = cfg.NS
            self.L = 8
            self.nseq = cfg.NS
            self.Lseq = 8
        self.TT = self.nt * self.L


def build(cfg, stage=99, dbg_names=()):
    nc = bass.Bass("TRN2", target_bir_lowering=False)
    es = ExitStack()
    SEQ, NS, NPOOL, NPG = cfg.SEQ, cfg.NS, cfg.NPOOL, cfg.NPG

    def dram(name, shape, dtype=F32, kind="ExternalInput"):
        return nc.dram_tensor(name, list(shape), dtype, kind=kind).ap()

    I = {}
    I["xp"] = dram("xp", [SEQ, D])
    I["xs"] = dram("xs", [NS * 8, D])
    I["ck"] = dram("ck", [DEPTH, NPOOL * 128, 512])
    I["cv"] = dram("cv", [DEPTH, NPOOL * 128, 512])
    I["cwk"] = dram("cwk", [DEPTH, NS, WIN, 256])
    I["cwv"] = dram("cwv", [DEPTH, NS, WIN, 256])
    I["sca"] = dram("sca", [DEPTH, NS, 2, D_CONV])
    I["scb"] = dram("scb", [DEPTH, NS, 3, XBC])
    I["ssm"] = dram("ssm", [DEPTH, NS, D_SSM, SSM_N])
    I["pt"] = dram("pt", [NS, NPG], I32)
    wshapes = {
        "norm_mix": [DEPTH, D], "w_in": [DEPTH, D, D_IN], "conv_a_w": [DEPTH, 3, D_CONV],
        "conv_b_w": [DEPTH, 4, XBC], "conv_b_bias": [DEPTH, XBC], "dt_bias": [DEPTH, SSM_H],
        "a_log": [DEPTH, SSM_H], "d_skip": [DEPTH, SSM_H], "q_norm": [DEPTH, HD],
        "k_norm": [DEPTH, 3, HD], "cmp_pos": [DEPTH, 2, 32, HD], "cmp_w1": [DEPTH, 2, 2048, 128],
        "cmp_w2": [DEPTH, 2, 128, HD], "norm_out": [DEPTH, D], "w_out": [DEPTH, D, D],
        "norm_ffn": [DEPTH, D], "w_gate": [DEPTH, D, D_FF], "w_up": [DEPTH, D, D_FF],
        "w_down": [DEPTH, D_FF, D],
    }
    for k, s in wshapes.items():
        I[k] = dram(k, s)
    hc = host_consts(cfg)
    for k, v in hc.items():
        I[k] = dram(k, list(v.shape))

    O = {}

    def out(name, shape):
        O[name] = dram(name, shape, kind="ExternalOutput")

    out("yp", [SEQ, D])
    out("ys", [NS * 8, D])
    out("nkp", [DEPTH, SEQ, 512])
    out("nvp", [DEPTH, SEQ, 512])
    out("nwkp", [DEPTH, WIN, 256])
    out("nwvp", [DEPTH, WIN, 256])
    out("ncap", [DEPTH, 2, D_CONV])
    out("ncbp", [DEPTH, 3, XBC])
    out("nssp", [DEPTH, D_SSM, SSM_N])
    out("nks", [DEPTH, NS * 8, 512])
    out("nvs", [DEPTH, NS * 8, 512])
    out("nwks", [DEPTH, NS, WIN, 256])
    out("nwvs", [DEPTH, NS, WIN, 256])
    out("ncas", [DEPTH, NS, 2, D_CONV])
    out("ncbs", [DEPTH, NS, 3, XBC])
    out("nsss", [DEPTH, NS, D_SSM, SSM_N])
    x1p = nc.dram_tensor("x1p", [SEQ, D], F32, kind="Internal").ap()
    x1s = nc.dram_tensor("x1s", [NS * 8, D], F32, kind="Internal").ap()
    DBG = {}

    T = Trk(nc, es)
    V, S_, G_, PE = nc.vector, nc.scalar, nc.gpsimd, nc.tensor

    uniq = {"n": 0}

    def sb(ctx, name, shape, dtype=F32):
        uniq["n"] += 1
        return ctx.enter_context(nc.sbuf_tensor("%s_%d" % (name, uniq["n"]), list(shape), dtype))

    def mm(out_, lhsT, rhs, start, stop, R, W):
        return T.op("pe", R, W, lambda: PE.matmul(out_, lhsT, rhs, start=start, stop=stop), sig=stop)

    def tr(out_, in_, ident, R, W):
        return T.op("pe", R, W, lambda: PE.transpose(out_, in_, ident))

    def trf(out_, in_, R, W):
        K_ = in_.shape[0]
        return T.op("pe", R, W, lambda: PE.matmul(out_, in_, ident_f[:K_, :K_], start=True, stop=True))

    def tr_hilo(out_, in_, hl, R, W):
        cp("act", hl[:, 0, :], in_, R, ["hl_hi"])
        tt("dve", hl[:, 1, :], in_, hl[:, 0, :], ALU.subtract, R + ["hl_hi"], ["hl_lo"])
        psb, pbk = nextB()
        tr(psb[:, 0:128], hl[:, 0, :], ident_b[:, :], ["hl_hi", "ident_b"], [pbk])
        tr(psb[:, 128:256], hl[:, 1, :], ident_b[:, :], ["hl_lo", "ident_b"], [pbk])
        cp("act", out_, psb[:, 0:128], [pbk], W)
        tt("dve", out_, out_, psb[:, 128:256], ALU.add, [pbk] + W, W)

    def act(out_, in_, func, R, W, **kw):
        return T.op("act", R, W, lambda: S_.activation(out=out_, in_=in_, func=func, **kw))

    def tt(e, out_, in0, in1, op, R, W):
        return T.op(e, R, W, lambda: T.eng[e].tensor_tensor(out=out_, in0=in0, in1=in1, op=op))

    def ts(e, out_, in0, s1, s2, op0, op1, R, W):
        if op1 is None:
            return T.op(e, R, W, lambda: T.eng[e].tensor_scalar(out=out_, in0=in0, scalar1=s1, scalar2=None, op0=op0))
        return T.op(e, R, W, lambda: T.eng[e].tensor_scalar(out=out_, in0=in0, scalar1=s1, scalar2=s2, op0=op0, op1=op1))

    def stt(e, out_, in0, scalar, in1, op0, op1, R, W):
        return T.op(e, R, W, lambda: T.eng[e].scalar_tensor_tensor(out=out_, in0=in0, scalar=scalar, in1=in1, op0=op0, op1=op1))

    def cp(e, out_, in_, R, W):
        if e == "act":
            return T.op("act", R, W, lambda: S_.copy(out=out_, in_=in_))
        return T.op(e, R, W, lambda: T.eng[e].tensor_copy(out=out_, in_=in_))

    def rsq(out_, in_, scale, R, W):
        act(out_, in_, AF.Sqrt, R, W, scale=scale, bias=EPS)
        return T.op("dve", W, W, lambda: V.reciprocal(out=out_, in_=out_))

    def ms(e, ap, val, W):
        return T.op(e, [], W, lambda: T.eng[e].memset(ap, val))

    def dma(q, out_, in_, R, W, **kw):
        return T.dma(q, out_, in_, R, W, **kw)

    psF = [es.enter_context(nc.psum_tensor("psF%d" % i, [128, 512], F32)) for i in range(6)]
    psB = [es.enter_context(nc.psum_tensor("psB%d" % i, [128, 1024], BF16)) for i in range(2)]
    rr = {"F": 0, "B": 0}

    def nextF(banks=(0, 1)):
        i = banks[rr["F"] % len(banks)]
        rr["F"] += 1
        return psF[i], "F%d" % i

    def nextB():
        i = rr["B"] % 2
        rr["B"] += 1
        return psB[i], "B%d" % i

    ident_f = sb(es, "ident_f", [128, 128])
    ident_b = sb(es, "ident_b", [128, 128], BF16)
    kp = sb(es, "kp", [128, hc["c_kp"].shape[1]])
    slope = sb(es, "slope", [128, NH])
    ones_b = sb(es, "ones_b", [128, 128], BF16)
    ones_f = sb(es, "ones_f", [128, 128])
    bsc = sb(es, "bsc", [128, 8])
    dma("sp", ident_f[:], I["c_ident"], [], ["ident_f"])
    dma("pool", ident_b[:], I["c_ident"], [], ["ident_b"])
    tri_b = sb(es, "tri_b", [128, 128], BF16)
    negtri_b = sb(es, "negtri_b", [128, 128], BF16)
    dma("pool", tri_b[:], I["c_tri"], [], ["tri_b"])
    dma("pool", negtri_b[:], I["c_negtri"], [], ["negtri_b"])
    dma("sp", kp[:], I["c_kp"], [], ["kp"])
    dma("sp", slope[:], I["c_slope"][:, :, 0], [], ["slope"], allow_slow_non_contiguous=True)
    maskC = sb(es, "maskC", [128, 128])
    maskW = sb(es, "maskW", [128, 128])
    dma("sp", maskC[:], I["c_negtri"], [], ["maskCW"])
    dma("sp", maskW[:], I["c_negtri"].rearrange("a b -> b a"), [], ["maskCW"], allow_slow_non_contiguous=True)
    ms("dve", ones_b[:], 1.0, ["ones_b"])
    ms("dve", ones_f[:], 1.0, ["ones_f"])

    fill_reg = G_.to_reg(NEG)

    def barrier():
        T.barrier(bsc[:, 0:1])

    def dbg(name, ap, R):
        if name not in dbg_names:
            return
        shape = list(ap.shape)
        DBG[name] = dram("dbg_" + name, shape, ap.dtype, kind="ExternalOutput")
        dma("sp", DBG[name], ap, R, [("dbg", name)])

    NRING = 2
    ring = [None] * NRING
    pstate = {"n": 0}

    def ring_alloc(ctx):
        for i_ in range(NRING):
            ring[i_] = sb(ctx, "ring%d" % i_, [128, 8192], BF16)

    class Panel:
        def __init__(self, src, kt, pc, srcs=None):
            self.src, self.kt, self.pc = src, kt, pc
            self.srcs = srcs
            self.slot = None

        def issue(self):
            if self.slot is not None:
                return
            self.slot = pstate["n"] % NRING
            pstate["n"] += 1
            v = ring[self.slot][:, 0:self.kt * self.pc].rearrange("p (k c) -> p k c", c=self.pc)
            self.view = v
            self.key = ("ring", self.slot)
            if self.srcs is None:
                dma("pool", v, self.src.rearrange("(k p) c -> p k c", p=128), [], [self.key])
            else:
                w_ = self.pc // len(self.srcs)
                for n_, src_ in enumerate(self.srcs):
                    dma("pool", v[:, :, n_ * w_:(n_ + 1) * w_], src_.rearrange("(k p) c -> p k c", p=128), [], [self.key])

    def run_panels(panels, consume):
        for j, p in enumerate(panels):
            p.issue()
            if j + 1 < len(panels) and NRING > 1:
                panels[j + 1].issue()
            consume(j, p)

    def col_panels(w_ap, c0, c1, kt, width=512):
        ps_ = []
        c = c0
        while c < c1:
            pc = min(width, c1 - c)
            ps_.append((c, Panel(w_ap[:, c:c + pc], kt, pc)))
            c += pc
        return ps_

    def load_layer_params(ctx, l):
        Pm = {}
        Pm["g_out_fm"] = sb(ctx, "g_out_fm", [128, 4])
        dma("sp", Pm["g_out_fm"][:], I["norm_out"][l][0:512].rearrange("(m p) -> p m", p=128),
            [], ["g_out_fm"], allow_slow_non_contiguous=True)
        Pm["wA"] = sb(ctx, "wA", [128, 4, 3])
        for m in range(4):
            dma("sp", Pm["wA"][:, m, :], I["conv_a_w"][l][:, m * 128:(m + 1) * 128].rearrange("k p -> p k"),
                [], ["wA"], allow_slow_non_contiguous=True)
        Pm["wB"] = sb(ctx, "wB", [128, 8, 4])
        for m in range(8):
            dma("sp", Pm["wB"][:, m, :], I["conv_b_w"][l][:, m * 128:(m + 1) * 128].rearrange("k p -> p k"),
                [], ["wB"], allow_slow_non_contiguous=True)
        Pm["bB"] = sb(ctx, "bB", [128, 8])
        dma("sp", Pm["bB"][:], I["conv_b_bias"][l].rearrange("(m p) -> p m", p=128),
            [], ["bB"], allow_slow_non_contiguous=True)
        for nm in ("dt_bias", "a_log", "d_skip"):
            Pm[nm] = sb(ctx, nm, [128, SSM_H])
            dma("sp", Pm[nm][:], I[nm][l].partition_broadcast(128), [], [nm])
        Pm["negA"] = sb(ctx, "negA", [128, SSM_H])
        act(Pm["negA"][:], Pm["a_log"][:], AF.Exp, ["a_log"], ["negA"])
        ts("dve", Pm["negA"][:], Pm["negA"][:], -1.0, None, ALU.mult, None, ["negA"], ["negA"])
        Pm["dskip"] = sb(ctx, "dskip", [128, D_SSM])
        cp("dve", Pm["dskip"][:].rearrange("p (h d) -> p h d", d=HD),
           Pm["d_skip"][:].unsqueeze(2).to_broadcast([128, SSM_H, HD]), ["d_skip"], ["dskip"])
        Pm["qg"] = sb(ctx, "qg", [64, 1])
        dma("sp", Pm["qg"][:], I["q_norm"][l].rearrange("(d o) -> d o", o=1), [], ["qg"],
            allow_slow_non_contiguous=True)
        ts("dve", Pm["qg"][:], Pm["qg"][:], 0.125, None, ALU.mult, None, ["qg"], ["qg"])
        Pm["kg"] = sb(ctx, "kg", [128, 3, HD])
        dma("sp", Pm["kg"][:], I["k_norm"][l].partition_broadcast(128), [], ["kg"])
        Pm["kg0"] = sb(ctx, "kg0", [64, 1])
        dma("sp", Pm["kg0"][:], I["k_norm"][l][0].rearrange("(d o) -> d o", o=1), [], ["kg0"],
            allow_slow_non_contiguous=True)
        Pm["w1"] = sb(ctx, "w1", [64, 2, 32, 128], BF16)
        for kv in range(2):
            dma("pool", Pm["w1"][:, kv], I["cmp_w1"][l][kv].rearrange("(j d) h -> d j h", d=HD),
                [], [("w1", kv)])
        Pm["posT"] = sb(ctx, "posT", [64, 2, 32], BF16)
        for kv in range(2):
            dma("pool", Pm["posT"][:, kv, :], I["cmp_pos"][l][kv].rearrange("j d -> d j"), [], ["posT"],
                allow_slow_non_contiguous=True)
        Pm["w2"] = sb(ctx, "w2", [128, 2, HD], BF16)
        dma("pool", Pm["w2"][:], I["cmp_w2"][l].rearrange("v h d -> h v d"), [], ["w2"])
        Pm["posb"] = sb(ctx, "posb", [128, 2])
        for kv in range(2):
            ps, pk = nextF()
            for j in range(32):
                mm(ps[:, 0:1], Pm["w1"][:, kv, j, :], Pm["posT"][:, kv, j:j + 1], j == 0, j == 31,
                   [("w1", kv), "posT"], [pk])
            cp("dve", Pm["posb"][:, kv:kv + 1], ps[:, 0:1], [pk], [("posb", kv)])
        return Pm

    def x_src(l, b, i):
        if b.kind == "p":
            base = b.bi * TB + i * 128
            src = I["xp"] if l == 0 else x1p
            return src[base:base + 128, :]
        src = I["xs"] if l == 0 else x1s
        return src[i * 8:(i + 1) * 8, :]

    def x_dst(l, b, i):
        if b.kind == "p":
            base = b.bi * TB + i * 128
            dst = x1p if l == 0 else O["yp"]
            return dst[base:base + 128, :]
        dst = x1s if l == 0 else O["ys"]
        return dst[i * 8:(i + 1) * 8, :]

    def norm_T(b, x_sb, xkey, gain, gkey, xT, tkey, wk):
        L, nt = b.L, b.nt
        for i in range(nt):
            act(wk["junk"][:L, :], x_sb[:L, i, :], AF.Square, [(xkey, i)], ["xn", "nst"],
                accum_out=wk["st"][:L, 0:1])
            rsq(wk["st"][:L, 1:2], wk["st"][:L, 0:1], 1.0 / D, ["nst"], ["nst2"])
            stt("dve", wk["xn"][:L, :], x_sb[:L, i, :], wk["st"][:L, 1:2], gain[:L, :], ALU.mult, ALU.mult,
                [(xkey, i), "nst2", gkey], ["xn"])
            for h in range(2):
                ps, pk = nextB()
                for k in range(8):
                    kt = h * 8 + k
                    tr(ps[:, k * 128:k * 128 + L], wk["xn"][:L, kt * 128:(kt + 1) * 128], ident_b[:L, :L],
                       ["xn", "ident_b"], [pk])
                src = ps[:, :].rearrange("p (k c) -> p k c", c=128)[:, :, 0:L]
                e = "act" if h == 0 else "dve"
                cp(e, xT[:, h * 8:(h + 1) * 8, i * L:(i + 1) * L], src, [pk], [(tkey, i, h)])

    def xT_keys(tkey, b):
        return [(tkey, i, h) for i in range(b.nt) for h in range(2)]

    def fm_cols(b, panels, xT, tkey, mw, evac):
        rk = xT_keys(tkey, b)

        def consume(j, cp_):
            c0, p = panels[j]
            for m0 in range(0, p.pc, mw):
                ps, pk = nextF()
                for kt in range(p.kt):
                    mm(ps[:mw, :b.TT], p.view[:, kt, m0:m0 + mw], xT[:, kt, :b.TT], kt == 0, kt == p.kt - 1,
                       [p.key] + rk, [pk])
                evac(c0 + m0, ps, pk)
        run_panels([p for _, p in panels], consume)

    def tm_cols(b, panels, xT, tkey, evac):
        def consume(j, cp_):
            c0, p = panels[j]
            for i in range(b.nt):
                ps, pk = nextF()
                for kt in range(p.kt):
                    mm(ps[:b.L, :p.pc], xT[:, kt, i * b.L:(i + 1) * b.L], p.view[:, kt, :], kt == 0, kt == p.kt - 1,
                       [p.key, (tkey, i, 0), (tkey, i, 1)], [pk])
                evac(i, c0, p.pc, ps, pk)
        run_panels([p for _, p in panels], consume)

    def mixer_a(l, b, Pm, xT, mergedT, carry):
        L, nt, TT, nseq, Ls = b.L, b.nt, b.TT, b.nseq, b.Lseq
        with ExitStack() as ctx:
            ring_alloc(ctx)
            uA = sb(ctx, "uA", [128, 12, TT])
            ext = sb(ctx, "extA", [128, 4, nseq, 2 + Ls])
            acc = sb(ctx, "accA", [128, TT])
            sq = sb(ctx, "sqA", [128, 4, TT], BF16)
            ya = sb(ctx, "yA", [128, 4, TT])
            rstd = sb(ctx, "rstdA", [128, TT])
            wcol = I["w_in"][l][:, 0:COLS_A]

            def evac(c, ps, pk):
                m = c // 128
                cp("act", uA[:, m, :], ps[:, :TT], [pk], [("uA", m)])
            fm_cols(b, col_panels(wcol, 0, COLS_A, KT), xT, "xT", 128, evac)
            if b.kind == "p":
                if b.bi == 0:
                    ms("dve", ext[:, :, :, 0:2], 0.0, ["extA_pre"])
                else:
                    cp("dve", ext[:, :, 0, 0:2], carry["A"][:, :, :], ["carryA"], ["extA_pre"])
            else:
                for m in range(4):
                    for s_ in range(nseq):
                        dma("sp", ext[:, m, s_, 0:2],
                            I["sca"][l][s_][:, m * 128:(m + 1) * 128].rearrange("t c -> c t"),
                            [], [("extA_prem", m, s_)], allow_slow_non_contiguous=True)
            pre_keys = ["extA_pre"] + ([("extA_prem", m, s_) for m in range(4) for s_ in range(nseq)] if b.kind == "s" else [])
            for m in range(4):
                v3 = lambda ap: ap.rearrange("p (s t) -> p s t", t=Ls)
                tt("dve", ext[:, m, :, 2:2 + Ls], v3(uA[:, 4 + m, :]), v3(uA[:, 8 + m, :]), ALU.mult,
                   [("uA", 4 + m), ("uA", 8 + m)] + pre_keys, [("extA", m)])
                a3 = v3(acc[:, :])
                ts("dve", a3, ext[:, m, :, 0:Ls], Pm["wA"][:, m, 0:1], None, ALU.mult, None,
                   [("extA", m), "wA"] + pre_keys, ["accA"])
                stt("dve", a3, ext[:, m, :, 1:1 + Ls], Pm["wA"][:, m, 1:2], a3, ALU.mult, ALU.add,
                    [("extA", m), "accA"], ["accA"])
                stt("dve", a3, ext[:, m, :, 2:2 + Ls], Pm["wA"][:, m, 2:3], a3, ALU.mult, ALU.add,
                    [("extA", m), "accA"], ["accA"])
                tt("dve", ya[:, m, :], acc[:, :], uA[:, m, :], ALU.mult, ["accA", ("uA", m)], [("yA", m)])
                act(sq[:, m, :], ya[:, m, :], AF.Square, [("yA", m)], [("sqA", m)])
            if b.kind == "p":
                cp("dve", carry["A"][:, :, :], ext[:, :, 0, Ls:Ls + 2], [("extA", m) for m in range(4)], ["carryA"])
                if b.bi == cfg.NBLK - 1:
                    for m in range(4):
                        dma("sp", O["ncap"][l][:, m * 128:(m + 1) * 128].rearrange("t c -> c t"),
                            ext[:, m, 0, Ls:Ls + 2], [("extA", m)], [("o_ncap", l, m)], allow_slow_non_contiguous=True)
            else:
                for m in range(4):
                    for s_ in range(nseq):
                        dma("sp", O["ncas"][l][s_][:, m * 128:(m + 1) * 128].rearrange("t c -> c t"),
                            ext[:, m, s_, Ls:Ls + 2], [("extA", m)], [("o_ncas", l, m, s_)], allow_slow_non_contiguous=True)
            ps, pk = nextF()
            for m in range(4):
                mm(ps[:, :TT], ones_b[:, :], sq[:, m, :], m == 0, m == 3, [("sqA", m), "ones_b"], [pk])
            rsq(rstd[:, :], ps[:, :TT], 1.0 / D_CONV, [pk], ["rstdA"])
            for m in range(4):
                stt("dve", mergedT[:, m, :TT], ya[:, m, :], Pm["g_out_fm"][:, m:m + 1], rstd[:, :], ALU.mult, ALU.mult,
                    [("yA", m), "rstdA", "g_out_fm"], [("mT", m)])
            dbg("yA_%d_%s%d" % (l, b.kind, b.bi), ya[:], [("yA", m) for m in range(4)])
            barrier()

    def mixer_b(l, b, Pm, xT, mergedT, carry):
        L, nt, TT, nseq, Ls = b.L, b.nt, b.TT, b.nseq, b.Lseq
        with ExitStack() as ctx:
            ring_alloc(ctx)
            goutB = sb(ctx, "goutB", [128, D_SSM])
            dma("sp", goutB[:], I["norm_out"][l][512:1024].partition_broadcast(128), [], ["goutB"])
            z = sb(ctx, "zB", [128, nt, D_SSM])
            dtr = sb(ctx, "dtr", [128, nt, SSM_H])
            dtv = sb(ctx, "dtv", [128, nt, SSM_H])
            la = sb(ctx, "la", [128, nt, SSM_H])
            ext = sb(ctx, "extB", [128, 8, nseq, 3 + Ls])
            acc = sb(ctx, "accB", [128, TT])
            xcb = sb(ctx, "xcb", [128, 8, TT], BF16)
            BT = xcb[:, 4:6, :]
            CT = xcb[:, 6:8, :]
            lahl = sb(ctx, "lahl", [128, 2, SSM_H], BF16)
            la_bch = sb(ctx, "la_bch", [128, 2, SSM_H, 128], BF16)
            sthl = sb(ctx, "sthl", [128, 2, 128], BF16)
            xs_bf = sb(ctx, "xs_bf", [128, nt, D_SSM], BF16)
            Btok = sb(ctx, "Btok", [128, nt, 256], BF16)
            yb = sb(ctx, "yb", [128, nt, D_SSM])
            cum = sb(ctx, "cumB", [128, 16])
            sm = sb(ctx, "smB", [128, 5, SSM_H])
            decT = [sb(ctx, "decT%d" % k, [128, 128]) for k in range(2)]
            scT = sb(ctx, "scT", [128, SSM_H, 128], BF16)
            tmp = sb(ctx, "tmpB", [128, D_SSM])
            tmp2 = sb(ctx, "tmp2B", [128, D_SSM])
            xw = sb(ctx, "xwB", [128, D_SSM], BF16)
            ynb = sb(ctx, "ynb", [128, D_SSM], BF16)
            stio = sb(ctx, "stio", [128, 4, 128])
            nst = sb(ctx, "nstB", [128, 4])
            hT, hT_bf = carry["hT"], carry["hT_bf"]
            wbase = I["w_in"][l]
            v3 = lambda ap: ap.rearrange("p (s t) -> p s t", t=Ls)

            def ev_z(i, c0, pc, ps, pk):
                cp("act", z[:L, i, :], ps[:L, :pc], [pk], [("zB", i)])
            tm_cols(b, col_panels(wbase, OFF_B, OFF_B + 512, KT), xT, "xT", ev_z)

            def ev_x(c, ps, pk):
                m = (c - (OFF_B + 512)) // 128
                cp("act", ext[:, m, :, 3:3 + Ls], v3(ps[:, :TT]), [pk], [("extB", m)])
            fm_cols(b, col_panels(wbase, OFF_B + 512, OFF_B + 512 + XBC, KT), xT, "xT", 128, ev_x)

            def ev_dt(i, c0, pc, ps, pk):
                cp("act", dtr[:L, i, :], ps[:L, :pc], [pk], [("dtr", i)])
            tm_cols(b, col_panels(wbase, OFF_B + 512 + XBC, OFF_B + 512 + XBC + SSM_H, KT), xT, "xT", ev_dt)

            if CUT == 1:
                barrier()
                return
            if b.kind == "p":
                if b.bi == 0:
                    ms("dve", ext[:, :, :, 0:3], 0.0, ["extB_pre"])
                else:
                    cp("dve", ext[:, :, 0, 0:3], carry["B"][:, :, :], ["carryB"], ["extB_pre"])
                pre_keys = ["extB_pre"]
            else:
                pre_keys = []
                for m in range(8):
                    for s_ in range(nseq):
                        dma("sp", ext[:, m, s_, 0:3],
                            I["scb"][l][s_][:, m * 128:(m + 1) * 128].rearrange("t c -> c t"),
                            [], [("extB_prem", m, s_)], allow_slow_non_contiguous=True)
                        pre_keys.append(("extB_prem", m, s_))
            for m in range(8):
                a3 = v3(acc[:, :])
                ts("dve", a3, ext[:, m, :, 0:Ls], Pm["wB"][:, m, 0:1], None, ALU.mult, None,
                   [("extB", m), "wB"] + pre_keys, ["accB"])
                for k in range(1, 4):
                    stt("dve", a3, ext[:, m, :, k:k + Ls], Pm["wB"][:, m, k:k + 1], a3, ALU.mult, ALU.add,
                        [("extB", m), "accB"], ["accB"])
                act(xcb[:, m, :], acc[:, :], AF.Silu, ["accB", "bB"], [("xcb", m)], bias=Pm["bB"][:, m:m + 1])
            ek = [("extB", m) for m in range(8)]
            if b.kind == "p":
                cp("dve", carry["B"][:, :, :], ext[:, :, 0, Ls:Ls + 3], ek, ["carryB"])
                if b.bi == cfg.NBLK - 1:
                    for m in range(8):
                        dma("sp", O["ncbp"][l][:, m * 128:(m + 1) * 128].rearrange("t c -> c t"),
                            ext[:, m, 0, Ls:Ls + 3], ek, [("o_ncbp", l, m)], allow_slow_non_contiguous=True)
            else:
                for m in range(8):
                    for s_ in range(nseq):
                        dma("sp", O["ncbs"][l][s_][:, m * 128:(m + 1) * 128].rearrange("t c -> c t"),
                            ext[:, m, s_, Ls:Ls + 3], ek, [("o_ncbs", l, m, s_)], allow_slow_non_contiguous=True)
            if CUT == 2:
                barrier()
                return
            if CUT == 10:
                barrier()
                return
            for i in range(nt):
                if CUT == 11 and i == 1:
                    barrier()
                    return
                psb, pbk = nextB()
                for m in range(4):
                    tr(psb[:L, m * 128:(m + 1) * 128], xcb[:, m, i * L:(i + 1) * L], ident_b[:, :],
                       [("xcb", m), "ident_b"], [pbk])
                for m in range(2):
                    tr(psb[:L, 512 + m * 128:512 + (m + 1) * 128], BT[:, m, i * L:(i + 1) * L], ident_b[:, :],
                       [("xcb", 4 + m), "ident_b"], [pbk])
                cp("dve", xs_bf[:L, i, :], psb[:L, 0:512], [pbk], [("xs_bf", i)])
                cp("act", Btok[:L, i, :], psb[:L, 512:768], [pbk], [("Btok", i)])
            if CUT == 3:
                barrier()
                return
            for i in range(nt):
                tt("dve", dtv[:L, i, :], dtr[:L, i, :], Pm["dt_bias"][:L, :], ALU.add, [("dtr", i), "dt_bias"], [("dtv", i)])
                act(dtv[:L, i, :], dtv[:L, i, :], AF.Exp, [("dtv", i)], [("dtv", i)])
                act(dtv[:L, i, :], dtv[:L, i, :], AF.Ln, [("dtv", i)], [("dtv", i)], bias=1.0)
                tt("dve", la[:L, i, :], dtv[:L, i, :], Pm["negA"][:L, :], ALU.mult, [("dtv", i), "negA"], [("la", i)])

            if CUT == 4:
                barrier()
                return
            for i in range(nt):
                off = i * L
                first = (b.kind == "p" and b.bi == 0 and i == 0)
                if b.kind == "s":
                    dma("sp", stio[:, :, :], I["ssm"][l][i].rearrange("(q p) n -> p q n", p=128), [], ["stio"])
                    for q in range(4):
                        tr_hilo(hT[:, q * 128:(q + 1) * 128], stio[:, q, :], sthl, ["stio"], ["hT"])
                    cp("act", hT_bf[:, :], hT[:, :], ["hT"], ["hT_bf"])
                elif first:
                    ms("dve", hT[:, :], 0.0, ["hT"])
                    ms("dve", hT_bf[:, :], 0.0, ["hT_bf"])
                psc, pck = psF[2], "F2"
                cp("act", lahl[:L, 0, :], la[:L, i, :], [("la", i)], ["lahi"])
                tt("dve", lahl[:L, 1, :], la[:L, i, :], lahl[:L, 0, :], ALU.subtract, [("la", i), "lahi"], ["lalo"])
                for k in range(2):
                    mm(psc[:L, 0:8], tri_b[:L, :L], lahl[:L, k, :], k == 0, k == 1, ["tri_b", "lahi", "lalo"], [pck])
                for k in range(2):
                    mm(psc[:, 8:16], ones_b[:L, :], lahl[:L, k, :], k == 0, k == 1, ["ones_b", "lahi", "lalo"], [pck])
                cp("dve", cum[:, :], psc[:, 0:16], [pck], ["cumB"])
                negcum, expcum, w_, dec, t5 = (sm[:, k, :] for k in range(5))
                ts("dve", negcum[:L, :], cum[:L, 0:8], -1.0, None, ALU.mult, None, ["cumB"], ["negcum"])
                act(expcum[:L, :], cum[:L, 0:8], AF.Exp, ["cumB"], ["expcum"])
                act(dec[:, :], cum[:, 8:16], AF.Exp, ["cumB"], ["decB"])
                tt("dve", t5[:L, :], cum[:L, 8:16], cum[:L, 0:8], ALU.subtract, ["cumB"], ["t5B"])
                act(t5[:L, :], t5[:L, :], AF.Exp, ["t5B"], ["t5B"])
                tt("dve", w_[:L, :], t5[:L, :], dtv[:L, i, :], ALU.mult, ["t5B", ("dtv", i)], ["wB_"])
                for k in range(2):
                    cp("dve", la_bch[:L, k, :, :L], lahl[:L, k, :].unsqueeze(2).to_broadcast([L, SSM_H, L]),
                       ["lahi", "lalo"], [("la_bch", k)])
                if CUT == 5:
                    barrier()
                    return
                psg, pgk = psF[3], "F3"
                for g in range(2):
                    mm(psg[:L, g * 128:g * 128 + L], BT[:, g, off:off + L], CT[:, g, off:off + L], True, True,
                       [("xcb", 4 + g), ("xcb", 6 + g)], [pgk])
                for h in range(SSM_H):
                    g = h // 4
                    psr, prk = nextF()
                    for k in range(2):
                        mm(psr[:L, :L], la_bch[:L, k, h, :L], tri_b[:L, :L], k == 0, False, [("la_bch", k), "tri_b"], [prk])
                    mm(psr[:L, :L], ident_b[:L, :L], negtri_b[:L, :L], False, True, ["ident_b", "negtri_b"], [prk])
                    dT = decT[h % 2]
                    act(dT[:L, :L], psr[:L, :L], AF.Exp, [prk, "negcum"], [("decT", h % 2)], bias=negcum[:L, h:h + 1])
                    stt("dve", scT[:L, h, :L], psg[:L, g * 128:g * 128 + L], dtv[:L, i, h:h + 1], dT[:L, :L],
                        ALU.mult, ALU.mult, [pgk, ("dtv", i), ("decT", h % 2)], [("scT", h)])
                if CUT == 6:
                    barrier()
                    return
                psy, pyk = psF[4], "F4"
                psy2, py2k = psF[5], "F5"
                for h in range(SSM_H):
                    g = h // 4
                    hs = slice(h * HD, (h + 1) * HD)
                    mm(psy[:L, hs], scT[:L, h, :L], xs_bf[:L, i, hs], True, True, [("scT", h), ("xs_bf", i)], [pyk])
                    mm(psy2[:L, hs], CT[:, g, off:off + L], hT_bf[:, hs], True, True, [("xcb", 6 + g), "hT_bf"], [py2k])
                h3 = lambda ap: ap.rearrange("p (h d) -> p h d", d=HD)
                tt("dve", h3(tmp[:L, :]), h3(psy2[:L, :]), expcum[:L, :].unsqueeze(2).to_broadcast([L, SSM_H, HD]),
                   ALU.mult, [py2k, "expcum"], ["tmpB"])
                tt("dve", yb[:L, i, :], tmp[:L, :], psy[:L, :], ALU.add, ["tmpB", pyk], [("yb", i)])
                tt("dve", tmp2[:L, :], xs_bf[:L, i, :], Pm["dskip"][:L, :], ALU.mult, [("xs_bf", i), "dskip"], ["tmp2B"])
                tt("dve", yb[:L, i, :], yb[:L, i, :], tmp2[:L, :], ALU.add, [("yb", i), "tmp2B"], [("yb", i)])
                act(tmp[:L, :], z[:L, i, :], AF.Silu, [("zB", i)], ["tmpB"])
                tt("dve", yb[:L, i, :], yb[:L, i, :], tmp[:L, :], ALU.mult, [("yb", i), "tmpB"], [("yb", i)])
                if CUT == 7:
                    barrier()
                    return
                tt("dve", h3(xw[:L, :]), h3(xs_bf[:L, i, :]), w_[:L, :].unsqueeze(2).to_broadcast([L, SSM_H, HD]),
                   ALU.mult, [("xs_bf", i), "wB_"], ["xwB"])
                psc2, pc2k = psF[2], "F2"
                for g in range(2):
                    mm(psc2[:, g * 256:(g + 1) * 256], Btok[:L, i, g * 128:(g + 1) * 128], xw[:L, g * 256:(g + 1) * 256],
                       True, True, [("Btok", i), "xwB"], [pc2k])
                tt("dve", h3(hT[:, :]), h3(hT[:, :]), dec[:, :].unsqueeze(2).to_broadcast([128, SSM_H, HD]), ALU.mult,
                   ["hT", "decB"], ["hT"])
                tt("dve", hT[:, :], hT[:, :], psc2[:, :], ALU.add, ["hT", pc2k], ["hT"])
                cp("act", hT_bf[:, :], hT[:, :], ["hT"], ["hT_bf"])
                if CUT == 8:
                    barrier()
                    return
                last = (b.kind == "p" and b.bi == cfg.NBLK - 1 and i == nt - 1)
                if b.kind == "s" or last:
                    for q in range(4):
                        tr_hilo(stio[:, q, :], hT[:, q * 128:(q + 1) * 128], sthl, ["hT"], ["stio"])
                    dst = O["nsss"][l][i] if b.kind == "s" else O["nssp"][l]
                    dma("sp", dst.rearrange("(q p) n -> p q n", p=128), stio[:, :, :], ["stio"], [("o_nss", l, b.kind, i)])
                if CUT == 9:
                    barrier()
                    return
                act(tmp2[:L, :], yb[:L, i, :], AF.Square, [("yb", i)], ["tmp2B", "nstB"], accum_out=nst[:L, 0:1])
                rsq(nst[:L, 1:2], nst[:L, 0:1], 1.0 / D_SSM, ["nstB"], ["nstB2"])
                stt("dve", ynb[:L, :], yb[:L, i, :], nst[:L, 1:2], goutB[:L, :], ALU.mult, ALU.mult,
                    [("yb", i), "nstB2", "goutB"], ["ynb"])
                psb, pbk = nextB()
                for m in range(4):
                    tr(psb[:, m * 128:m * 128 + L], ynb[:L, m * 128:(m + 1) * 128], ident_b[:L, :L], ["ynb", "ident_b"], [pbk])
                cp("act", mergedT[:, 4:8, off:off + L], psb[:, 0:512].rearrange("p (k c) -> p k c", c=128)[:, :, 0:L],
                   [pbk], [("mT", 4 + m) for m in range(4)])
            dbg("yb_%d_%s%d" % (l, b.kind, b.bi), yb[:L], [("yb", i) for i in range(nt)])
            barrier()

    def nsa_consts(ctx, dup, ncols_ee):
        C = {}
        C["ee"] = sb(ctx, "ee", [128, ncols_ee], BF16)
        dma("pool", C["ee"][:], I["c_eed" if dup else "c_ee"][:, 0:ncols_ee], [], ["ee"])
        C["dup"] = dup
        C["sels"] = sb(ctx, "sels", list(hc["c_sels"].shape), BF16)
        dma("pool", C["sels"][:], I["c_sels"], [], ["sels"])
        C["bd"] = sb(ctx, "bd", [128, 128], BF16)
        ms("dve", C["bd"][:], 0.0, ["bd"])
        ms("dve", C["bd"][0:64, 0:64], 1.0, ["bd"])
        ms("dve", C["bd"][64:128, 64:128], 1.0, ["bd"])
        return C

    def nsa_layer_params(ctx, l, Pm):
        Pm["qg2"] = sb(ctx, "qg2", [128, 1])
        Pm["kg02"] = sb(ctx, "kg02", [128, 1])
        for hb in (0, 64):
            dma("sp", Pm["qg2"][hb:hb + 64, :], I["q_norm"][l].rearrange("(d o) -> d o", o=1), [], ["qg2"],
                allow_slow_non_contiguous=True)
            dma("sp", Pm["kg02"][hb:hb + 64, :], I["k_norm"][l][0].rearrange("(d o) -> d o", o=1), [], ["kg02"],
                allow_slow_non_contiguous=True)
        ts("dve", Pm["qg2"][:], Pm["qg2"][:], 0.125, None, ALU.mult, None, ["qg2"], ["qg2"])
        Pm["w2k"] = sb(ctx, "w2k", [128, 128], BF16)
        for hb in (0, 64):
            dma("pool", Pm["w2k"][:, hb:hb + 64], I["cmp_w2"][l][0], [], ["w2k"])

    def cmp_topbot(Pm, src, g, nch, kvi, rkeys):
        ps, pk = nextF()
        for half in range(2):
            for j in range(16):
                mm(ps[:, half * 256:half * 256 + nch], Pm["w1"][:, kvi, half * 16 + j, :],
                   src[0:64, g, j:j + 16 * (nch - 1) + 1:16], j == 0, j == 15, [("w1", kvi)] + rkeys, [pk])
        return ps, pk

    def gelu_to(out_bf, x, n, wkk, R, W):
        t = wkk
        tt("dve", t[:, :n], x, x, ALU.mult, R, ["gl_t"])
        ts("dve", t[:, :n], t[:, :n], 0.044715, 1.0, ALU.mult, ALU.add, ["gl_t"], ["gl_t"])
        tt("dve", t[:, :n], t[:, :n], x, ALU.mult, R + ["gl_t"], ["gl_t"])
        act(t[:, :n], t[:, :n], AF.Sigmoid, ["gl_t"], ["gl_t"], scale=1.5957691216057308)
        tt("dve", out_bf, t[:, :n], x, ALU.mult, R + ["gl_t"], W)

    def cmp_k_finish(Pm, C, Gk, n, dst, wk, W):
        ps, pk = nextF()
        mm(ps[:, :n], Pm["w2k"][:, :], Gk, True, True, ["w2k", "Gk"], [pk])
        cp("act", wk["kc"][:, :n], ps[:, :n], [pk], ["kc"])
        act(wk["kcsq"][:, :n], ps[:, :n], AF.Square, [pk], ["kcsq"])
        ps2, pk2 = nextF()
        mm(ps2[:, :n], C["bd"][:, :], wk["kcsq"][:, :n], True, True, ["bd", "kcsq"], [pk2])
        rsq(wk["kcr"][:, :n], ps2[:, :n], 1.0 / HD, [pk2], ["kcr"])
        stt("dve", dst, wk["kc"][:, :n], Pm["kg02"][:, 0:1], wk["kcr"][:, :n], ALU.mult, ALU.mult,
            ["kc", "kcr", "kg02"], W)

    def attend(C, g, L, qrhs, qkeys, tiles, ps_o, okey, wk, imp=None):
        n = len(tiles)
        L4 = 4 * L
        for ti, t in enumerate(tiles):
            nk = t["nk"]
            ps, pk = nextF()
            mm(ps[:nk, :L4], t["KT"], qrhs, True, t.get("eblk") is None, t["kkeys"] + qkeys, [pk])
            if t.get("eblk") is not None:
                nj = t["nj"]
                eb = t["hb"] if C["dup"] else 0
                mm(ps[:nk, :L4], C["ee"][eb:eb + nj, t["eblk"] * 128:t["eblk"] * 128 + nk], wk["NBT"][eb:eb + nj, 0:L4],
                   False, True, ["ee", "NBT"], [pk])
            sS = wk["sS"][ti % 2]
            v3 = lambda ap: ap.rearrange("p (r q) -> p r q", q=L)
            stt("dve", v3(sS[:nk, :L4]), slope[:nk, 4 * g:4 * g + 4].unsqueeze(2).to_broadcast([nk, 4, L]),
                t["kcol"][:nk, :] if "kcol" in t else kp[:nk, t["delta"]:t["delta"] + 1], v3(ps[:nk, :L4]), ALU.mult, ALU.add,
                [pk, "slope", "kp", "cmpcol"], [("sS", ti % 2)])
            m = t.get("mask")
            if m == "causal" or m == "win":
                mt = maskC if m == "causal" else maskW
                tt("dve", v3(sS[:nk, :L4]), v3(sS[:nk, :L4]), mt[:nk, 0:L].unsqueeze(1).to_broadcast([nk, 4, L]), ALU.add,
                   [("sS", ti % 2), "maskCW"], [("sS", ti % 2)])
            elif m is not None:
                base, cm, qs = m
                T.op("pool", [("sS", ti % 2)], [("sS", ti % 2)],
                     lambda: G_.affine_select(out=sS[:nk, :L4], in_=sS[:nk, :L4], pattern=[[0, 4], [qs, L]],
                                              compare_op=ALU.is_ge, fill=fill_reg, base=base, channel_multiplier=cm))
            PT = wk["PT"][ti % 2]
            act(PT[:nk, :L4], sS[:nk, :L4], AF.Exp, [("sS", ti % 2)], [("PT", ti % 2)])
            for r in range(4):
                mm(ps_o[:L, r * 65:(r + 1) * 65], PT[:nk, r * L:(r + 1) * L], t["V"], ti == 0 and r == 0,
                   ti == n - 1 and r == 3, [("PT", ti % 2)] + t["vkeys"], [okey])
            if imp is not None:
                ps_i, ikey, nj = imp
                for r in range(4):
                    mm(ps_i[:L, r * nj:(r + 1) * nj], PT[:nk, r * L:(r + 1) * L], C["sels"][:nk, t["ct"], 0:nj],
                       ti == 0 and r == 0, ti == n - 1 and r == 3, [("PT", ti % 2), "sels"], [ikey])

    PGG = min(8, NPG)

    def prompt_res(ctx):
        ncolp = ((SEQ // 16 + 127) // 128) * 128
        r = {}
        r["KsT"] = sb(ctx, "rKsT", [128, 2, SEQ], BF16)
        r["Vs"] = sb(ctx, "rVs", [128, cfg.NTILE, 4, 65], BF16)
        r["KwT"] = sb(ctx, "rKwT", [128, 2, 8 * 128], BF16)
        r["Vw"] = sb(ctx, "rVw", [128, 8, 4, 65], BF16)
        r["KcT"] = sb(ctx, "rKcT", [128, 4, ncolp], BF16)
        r["Gv"] = sb(ctx, "rGv", [128, 4, ncolp], BF16)
        r["Vc"] = sb(ctx, "rVc", [128, ncolp // 128, 4, 65], BF16)
        r["ctop"] = sb(ctx, "rctop", [128, 2, 4])
        ms("dve", r["Vs"][:], 1.0, ["rinit"])
        ms("dve", r["Vw"][:], 1.0, ["rinit"])
        ms("dve", r["Vc"][:], 1.0, ["rinit"])
        ms("dve", r["Gv"][:], 0.0, ["rinit"])
        ms("dve", r["KcT"][:], 0.0, ["rinit"])
        ms("dve", r["ctop"][:], 0.0, ["rinit"])
        barrier()
        return r

    def sample_bufs(ctx):
        ncols = ((cfg.NCMP_S + 127) // 128) * 128
        S = {}
        S["KsT"] = sb(ctx, "sKsT", [128, 2, cfg.PAST + 8], BF16)
        S["Vs"] = sb(ctx, "sVs", [128, NPG + 1, 4, 65], BF16)
        S["KwT"] = sb(ctx, "sKwT", [128, 2, WIN + 8], BF16)
        S["Vw"] = sb(ctx, "sVw", [128, 5, 4, 65], BF16)
        S["KcT"] = sb(ctx, "sKcT", [128, 4, ncols], BF16)
        S["Gv"] = sb(ctx, "sGv", [128, 4, ncols], BF16)
        S["Vc"] = sb(ctx, "sVc", [128, ncols // 128, 4, 65], BF16)
        S["stage"] = sb(ctx, "sstage", [128, PGG, 512], BF16)
        S["XT"] = sb(ctx, "sXT", [64, 4, PGG * 128], BF16)
        S["wst"] = sb(ctx, "swst", [128, 4, 256], BF16)
        S["pti"] = sb(ctx, "spti", [128, NPG], I32)
        S["ptf"] = sb(ctx, "sptf", [128, NPG])
        S["idx"] = sb(ctx, "sidx", [128, NPG], I32)
        S["pcol"] = sb(ctx, "spcol", [128, 1])
        S["ctop"] = sb(ctx, "sctop", [128, 4])
        S["topS"] = sb(ctx, "stopS", [128, PGG * 8 + 1])
        ms("dve", S["Vs"][:], 1.0, ["sinit"])
        ms("dve", S["Vw"][:], 1.0, ["sinit"])
        ms("dve", S["Vc"][:], 1.0, ["sinit"])
        ms("dve", S["Gv"][:], 0.0, ["sinit"])
        ms("dve", S["KcT"][:], 0.0, ["sinit"])
        ts("dve", S["pcol"][:], kp[:, 0:1], 64.0, None, ALU.add, None, ["kp"], ["sinit"])
        barrier()
        return S

    def sample_seq_prepare(l, b, s_, Pm, C, S, kv, kst, wk, glt, hsum, Gk):
        L = b.L
        nch = PGG * 8
        dma("sp", S["pti"][:], I["pt"][s_].partition_broadcast(128), [], ["pti"])
        cp("dve", S["ptf"][:], S["pti"][:], ["pti"], ["ptf"])
        stt("dve", S["ptf"][:], S["ptf"][:], 128.0, S["pcol"][:, 0:1].to_broadcast([128, NPG]), ALU.mult, ALU.add,
            ["ptf"], ["ptf"])
        if l > 0:
            ts("dve", S["ptf"][:], S["ptf"][:], float(l * NPOOL * 128), None, ALU.add, None, ["ptf"], ["ptf"])
        cp("dve", S["idx"][:], S["ptf"][:], ["ptf"], ["idx"])
        for kvi, cache in ((0, I["ck"]), (1, I["cv"])):
            ms("dve", S["ctop"][:], 0.0, ["sctop"])
            for grp in range(NPG // PGG):
                for j in range(PGG):
                    pg = grp * PGG + j
                    T.dma("pool", S["stage"][:, j, :], cache.rearrange("l r c -> (l r) c"), ["idx"], [("stage", j)],
                          indirect=bass.IndirectOffsetOnAxis(ap=S["idx"][:, pg:pg + 1], axis=0))
                for j in range(PGG):
                    pg = grp * PGG + j
                    psb, pbk = nextB()
                    for g in range(4):
                        tr(psb[:64, g * 128:(g + 1) * 128], S["stage"][:, j, g * 64:(g + 1) * 64], ident_b[:, :],
                           [("stage", j), "ident_b"], [pbk])
                    if kvi == 0:
                        for gam in range(2):
                            tr(psb[:, 512 + gam * 128:512 + (gam + 1) * 128], S["stage"][:, j, 256 + gam * 128:256 + (gam + 1) * 128],
                               ident_b[:, :], [("stage", j), "ident_b"], [pbk])
                    cp("act", S["XT"][:, :, j * 128:(j + 1) * 128], psb[:64, 0:512].rearrange("p (g c) -> p g c", c=128),
                       [pbk], [("sXT", j)])
                    if kvi == 0:
                        cp("dve", S["KsT"][:, :, pg * 128:(pg + 1) * 128], psb[:, 512:768].rearrange("p (g c) -> p g c", c=128),
                           [pbk], ["sKsT"])
                    else:
                        cp("dve", S["Vs"][:, pg, :, 0:64], S["stage"][:, j, 256:512].rearrange("p (g d) -> p g d", d=HD),
                           [("stage", j)], ["sVs"])
                xk = [("sXT", j) for j in range(PGG)]
                col0 = grp * nch - 1
                lo = 1 if grp == 0 else 0
                for g in range(4):
                    ps, pk = cmp_topbot(Pm, S["XT"], g, nch, kvi, xk)
                    cp("dve", S["topS"][:, 0:1], S["ctop"][:, g:g + 1], ["sctop"], ["stopS"])
                    cp("act", S["topS"][:, 1:nch + 1], ps[:, 0:nch], [pk], ["stopS"])
                    cp("dve", S["ctop"][:, g:g + 1], S["topS"][:, nch:nch + 1], ["stopS"], ["sctop"])
                    stt("dve", hsum[:, 0:nch], S["topS"][:, 0:nch], Pm["posb"][:, kvi:kvi + 1], ps[:, 256:256 + nch], ALU.add, ALU.add,
                        ["stopS", pk, ("posb", kvi)], ["hsum"])
                    if kvi == 0:
                        gelu_to(Gk[:, 0:nch], hsum[:, 0:nch], nch, glt, ["hsum"], ["Gk"])
                        cmp_k_finish(Pm, C, Gk[:, lo:nch], nch - lo, S["KcT"][:, g, col0 + lo:col0 + nch], wk, [("sKcT", g)])
                    else:
                        gelu_to(S["Gv"][:, g, col0 + lo:col0 + nch], hsum[:, lo:nch], nch - lo, glt, ["hsum"], [("sGv", g)])
            if kvi == 1:
                for g in range(4):
                    for ct in range((cfg.NCMP_S + 127) // 128):
                        ps2, pk2 = nextF()
                        mm(ps2[:, 0:64], S["Gv"][:, g, ct * 128:(ct + 1) * 128], Pm["w2"][:, 1, :], True, True,
                           [("sGv", g), "w2"], [pk2])
                        cp("act", S["Vc"][:, ct, g, 0:64], ps2[:, 0:64], [pk2], [("sVc", ct, g)])
        kvk = [("kv", s_, c) for c in range(3)]
        psb, pbk = nextB()
        for gam in range(2):
            tr(psb[:, gam * 128:gam * 128 + L], kst[:L, 512 + gam * 128:512 + (gam + 1) * 128], ident_b[:L, :L], ["kst", "ident_b"], [pbk])
            tr(psb[:, 256 + gam * 128:256 + gam * 128 + L], kst[:L, 768 + gam * 128:768 + (gam + 1) * 128], ident_b[:L, :L],
               ["kst", "ident_b"], [pbk])
        v3 = psb[:, 0:512].rearrange("p (k c) -> p k c", c=128)
        cp("act", S["KsT"][:, :, cfg.PAST:cfg.PAST + L], v3[:, 0:2, 0:L], [pbk], ["sKsT"])
        cp("act", S["KwT"][:, :, WIN:WIN + L], v3[:, 2:4, 0:L], [pbk], ["sKwT"])
        cp("dve", S["Vs"][:L, NPG, :, 0:64], kv[:L, s_, 768:1024].rearrange("p (g d) -> p g d", d=HD), kvk, ["sVs"])
        cp("dve", S["Vw"][:L, 4, :, 0:64], kv[:L, s_, 1280:1536].rearrange("p (g d) -> p g d", d=HD), kvk, ["sVw"])
        dma("pool", S["wst"][:], I["cwk"][l][s_].rearrange("(t p) c -> p t c", p=128), [], ["wst"])
        psb, pbk = nextB()
        for t in range(4):
            for gam in range(2):
                tr(psb[:, (t * 2 + gam) * 128:(t * 2 + gam + 1) * 128], S["wst"][:, t, gam * 128:(gam + 1) * 128], ident_b[:, :],
                   ["wst", "ident_b"], [pbk])
        p4 = psb[:, :].rearrange("p (t g c) -> p t g c", g=2, c=128)
        for gam in range(2):
            cp("act", S["KwT"][:, gam, 0:WIN].rearrange("p (t c) -> p t c", c=128), p4[:, :, gam, :], [pbk], ["sKwT"])
        dma("pool", S["wst"][:], I["cwv"][l][s_].rearrange("(t p) c -> p t c", p=128), ["wst"], ["wst"])
        for t in range(4):
            cp("dve", S["Vw"][:, t, :, 0:64], S["wst"][:, t, :].rearrange("p (g d) -> p g d", d=HD), ["wst"], ["sVw"])

    def dense_tail(l, b, Pm, xT, mergedT):
        L, nt, TT = b.L, b.nt, b.TT
        with ExitStack() as ctx:
            ring_alloc(ctx)
            x_sb = sb(ctx, "x_sb2", [128, nt, D])
            gff = sb(ctx, "gff", [128, D])
            xn = sb(ctx, "xn2", [128, D], BF16)
            st = sb(ctx, "nst2", [128, 8])
            actT = sb(ctx, "actT", [128, FFT // 2, TT], BF16)
            sg = sb(ctx, "sgF", [128, TT])
            dma("sp", gff[:], I["norm_ffn"][l].partition_broadcast(128), [], ["gff"])
            for i in range(nt):
                dma("sp", x_sb[:L, i, :], x_src(l, b, i), [], [("x2", i)])

            def ev_o(i, c0, pc, ps, pk):
                tt("dve", x_sb[:L, i, c0:c0 + pc], x_sb[:L, i, c0:c0 + pc], ps[:L, :pc], ALU.add, [pk, ("x2", i)], [("x2", i)])
            tm_cols(b, col_panels(I["w_out"][l], 0, D, KT), mergedT, "mTx", ev_o)
            dbg("xmid_%d_%s%d" % (l, b.kind, b.bi), x_sb[:L], [("x2", i) for i in range(nt)])
            norm_T(b, x_sb, "x2", gff, "gff", xT, "hT", {"junk": xn, "xn": xn, "st": st})
            hk = xT_keys("hT", b)
            half_ff = D_FF // 2
            for half in range(2):
                f0 = half * half_ff
                plist = []
                for c in range(f0, f0 + half_ff, 256):
                    plist.append((c, Panel(None, KT, 512, srcs=[I["w_gate"][l][:, c:c + 256], I["w_up"][l][:, c:c + 256]])))

                def consume(j, p):
                    c = plist[j][0]
                    for m0 in range(0, 256, 128):
                        mi = (c - f0 + m0) // 128
                        psg, pgk = psF[2 + (mi % 2) * 2], "F%d" % (2 + (mi % 2) * 2)
                        psu, puk = psF[3 + (mi % 2) * 2], "F%d" % (3 + (mi % 2) * 2)
                        for kt in range(KT):
                            mm(psg[:, :TT], p.view[:, kt, m0:m0 + 128], xT[:, kt, :TT], kt == 0, kt == KT - 1, [p.key] + hk, [pgk])
                        for kt in range(KT):
                            mm(psu[:, :TT], p.view[:, kt, 256 + m0:256 + m0 + 128], xT[:, kt, :TT], kt == 0, kt == KT - 1, [p.key] + hk, [puk])
                        act(sg[:, :], psg[:, :TT], AF.Silu, [pgk], ["sgF"])
                        tt("dve", actT[:, mi, :], sg[:, :], psu[:, :TT], ALU.mult, ["sgF", puk], [("actT", mi)])
                run_panels([p for _, p in plist], consume)
                ak = [("actT", m) for m in range(FFT // 2)]
                dpan = []
                for c in range(0, D, 256):
                    dpan.append((c, Panel(I["w_down"][l][f0:f0 + half_ff, c:c + 256], FFT // 2, 256)))

                def consume_d(j, p):
                    c0 = dpan[j][0]
                    for i in range(nt):
                        ps, pk = nextF()
                        for k in range(FFT // 2):
                            mm(ps[:L, :256], actT[:, k, i * L:(i + 1) * L], p.view[:, k, :], k == 0, k == FFT // 2 - 1, [p.key] + ak, [pk])
                        tt("dve", x_sb[:L, i, c0:c0 + 256], x_sb[:L, i, c0:c0 + 256], ps[:L, :256], ALU.add, [pk, ("x2", i)], [("x2", i)])
                run_panels([p for _, p in dpan], consume_d)
            dbg("xout_%d_%s%d" % (l, b.kind, b.bi), x_sb[:L], [("x2", i) for i in range(nt)])
            for i in range(nt):
                dma("sp", x_dst(l, b, i), x_sb[:L, i, :], [("x2", i)], [("o_x", l, b.kind, b.bi, i)])
            barrier()

    def mixer_c(l, b, Pm, C, xT, mergedT, res):
        L, nt, TT = b.L, b.nt, b.TT
        prompt = b.kind == "p"
        with ExitStack() as ctx:
            QT = sb(ctx, "QT", [128, 8, TT], BF16)
            kv = sb(ctx, "kvC", [128, nt, KV_COLS])
            gs = sb(ctx, "gsC", [128, nt, 48])
            qf = sb(ctx, "qf", [128, TT])
            qsq = sb(ctx, "qsq", [128, TT], BF16)
            qr = sb(ctx, "qr", [128, TT])
            gout = sb(ctx, "goutC", [128, D_ATTN])
            dma("sp", gout[:], I["norm_out"][l][1024:2048].partition_broadcast(128), [], ["goutC"])
            wbase = I["w_in"][l]

            rctx = ExitStack()
            ring_alloc(rctx)
            qpan = []
            for gam_ in range(2):
                srcs_ = [wbase[:, OFF_C + (gam_ * 8 + gp_ * 4 + r_) * HD:OFF_C + (gam_ * 8 + gp_ * 4 + r_ + 1) * HD]
                         for r_ in range(4) for gp_ in range(2)]
                qpan.append((gam_, Panel(None, KT, 512, srcs=srcs_)))
            rk = xT_keys("xT", b)

            def q_consume(j, p):
                gam = j
                for r in range(4):
                    ps, pk = nextF()
                    for kt in range(KT):
                        mm(ps[:, :TT], p.view[:, kt, r * 128:(r + 1) * 128], xT[:, kt, :TT], kt == 0, kt == KT - 1, [p.key] + rk, [pk])
                    cp("act", qf[:, :], ps[:, :TT], [pk], ["qf"])
                    act(qsq[:, :], ps[:, :TT], AF.Square, [pk], ["qsq"])
                    ps2, pk2 = nextF()
                    mm(ps2[:, :TT], C["bd"][:, :], qsq[:, :], True, True, ["bd", "qsq"], [pk2])
                    rsq(qr[:, :], ps2[:, :TT], 1.0 / HD, [pk2], ["qr"])
                    stt("dve", QT[:, gam * 4 + r, :], qf[:, :], Pm["qg2"][:, 0:1], qr[:, :], ALU.mult, ALU.mult,
                        ["qf", "qr", "qg2"], [("QT", gam * 4 + r)])
            run_panels([p for _, p in qpan], lambda j, p: q_consume(j, p))

            if CUTK(b) == 20:
                barrier()
                rctx.close()
                return
            def ev_kv(i, c0, pc, ps, pk):
                o = c0 - (OFF_C + D_ATTN)
                cp("act", kv[:L, i, o:o + pc], ps[:L, :pc], [pk], [("kv", i, o // 512)])
            tm_cols(b, col_panels(wbase, OFF_C + D_ATTN, OFF_C + D_ATTN + KV_COLS, KT), xT, "xT", ev_kv)

            def ev_g(i, c0, pc, ps, pk):
                act(gs[:L, i, :], ps[:L, :pc], AF.Sigmoid, [pk], [("gs", i)])
            tm_cols(b, col_panels(wbase, OFF_C + D_ATTN + KV_COLS, OFF_C + COLS_C, KT), xT, "xT", ev_g)
            barrier()
            rctx.close()

            if CUTK(b) == 21:
                barrier()
                return
            kst = sb(ctx, "kst", [128, 1024], BF16)
            ksq = sb(ctx, "ksq", [128, 256])
            kss = sb(ctx, "kss", [128, 8])
            wk = {"sS": [sb(ctx, "sS%d" % k, [128, 512]) for k in range(2)],
                  "PT": [sb(ctx, "PT%d" % k, [128, 512], BF16) for k in range(2)],
                  "NBT": sb(ctx, "NBT", [128, 512], BF16),
                  "kc": sb(ctx, "kc", [128, 64]), "kcsq": sb(ctx, "kcsq", [128, 64], BF16),
                  "kcr": sb(ctx, "kcr", [128, 64])}
            glt = sb(ctx, "glt", [128, 64])
            hsum = sb(ctx, "hsum", [128, 64])
            Gk = sb(ctx, "Gk", [128, 64], BF16)
            yc = sb(ctx, "ycC", [128, D_ATTN])
            ycn = sb(ctx, "ycn", [128, D_ATTN], BF16)
            sm = sb(ctx, "smC", [128, 64])
            impS = sb(ctx, "impS", [128, 3, 128])
            mx = sb(ctx, "mxC", [128, 16])
            NBb = sb(ctx, "NBb", [128, 128], BF16)
            FB = sb(ctx, "FB", [128, 128])
            ms("dve", NBb[:], 0.0, ["NBb"])
            dbgt = sb(ctx, "dbgt", [128, 3, 260]) if any(n_.startswith("att_") for n_ in dbg_names) else None
            cmpcol = sb(ctx, "cmpcol", [128, 8])
            t1 = sb(ctx, "t1C", [128, 256])
            t2 = sb(ctx, "t2C", [128, 256])
            if not prompt:
                S = sample_bufs(ctx)
            if prompt:
                XkT = sb(ctx, "XkT", [64, 4, TT], BF16)
                XvT = sb(ctx, "XvT", [64, 4, TT], BF16)
                topS = sb(ctx, "topS", [128, 33])

            for i in range(nt):
                kvk = [("kv", i, c) for c in range(3)]
                gi = b.bi * 4 + i if prompt else None
                for br, c0 in ((1, 512), (2, 1024)):
                    kview = kv[:L, i, c0:c0 + 256]
                    k3 = kview.rearrange("p (g d) -> p g d", d=HD)
                    tt("dve", ksq[:L, :], kview, kview, ALU.mult, kvk, ["ksq"])
                    T.op("dve", ["ksq"], ["kss"], lambda: V.tensor_reduce(
                        out=kss[:L, 0:4], in_=ksq[:L, :].rearrange("p (g d) -> p g d", d=HD), axis=AX.X, op=ALU.add))
                    rsq(kss[:L, 4:8], kss[:L, 0:4], 1.0 / HD, ["kss"], ["kss2"])
                    tt("dve", k3, k3, kss[:L, 4:8].unsqueeze(2).to_broadcast([L, 4, HD]), ALU.mult, kvk + ["kss2"], kvk)
                    tt("dve", k3, k3, Pm["kg"][:L, br, :].unsqueeze(1).to_broadcast([L, 4, HD]), ALU.mult, kvk + ["kg"], kvk)
                if prompt:
                    t0 = b.bi * TB + i * 128
                    nkd, nvd = O["nkp"][l][t0:t0 + 128], O["nvp"][l][t0:t0 + 128]
                else:
                    nkd, nvd = O["nks"][l][i * 8:(i + 1) * 8], O["nvs"][l][i * 8:(i + 1) * 8]
                for (dst, dc, sc) in ((nkd, 0, 0), (nkd, 256, 512), (nvd, 0, 256), (nvd, 256, 768)):
                    dma("sp", dst[:, dc:dc + 256], kv[:L, i, sc:sc + 256], kvk, [("o_kv", l, b.kind, b.bi, i, dc, sc)])
                if prompt:
                    if t0 >= SEQ - WIN:
                        w0 = t0 - (SEQ - WIN)
                        dma("sp", O["nwkp"][l][w0:w0 + 128], kv[:L, i, 1024:1280], kvk, [("o_wk", l, i, b.bi)])
                        dma("sp", O["nwvp"][l][w0:w0 + 128], kv[:L, i, 1280:1536], kvk, [("o_wv", l, i, b.bi)])
                else:
                    dma("sp", O["nwks"][l][i][0:WIN - 8], I["cwk"][l][i][8:WIN], [], [("o_wk1", l, i)])
                    dma("sp", O["nwvs"][l][i][0:WIN - 8], I["cwv"][l][i][8:WIN], [], [("o_wv1", l, i)])
                    dma("sp", O["nwks"][l][i][WIN - 8:WIN], kv[:L, i, 1024:1280], kvk, [("o_wk2", l, i)])
                    dma("sp", O["nwvs"][l][i][WIN - 8:WIN], kv[:L, i, 1280:1536], kvk, [("o_wv2", l, i)])
                cp("act", kst[:L, 0:768], kv[:L, i, 0:768], kvk, ["kst"])
                cp("act", kst[:L, 768:1024], kv[:L, i, 1024:1280], kvk, ["kst"])
                if prompt:
                    slot = gi % 8
                    psb, pbk = nextB()
                    for gam in range(2):
                        tr(psb[:, gam * 128:gam * 128 + L], kst[:L, 512 + gam * 128:512 + (gam + 1) * 128], ident_b[:L, :L],
                           ["kst", "ident_b"], [pbk])
                        tr(psb[:, 256 + gam * 128:256 + gam * 128 + L], kst[:L, 768 + gam * 128:768 + (gam + 1) * 128],
                           ident_b[:L, :L], ["kst", "ident_b"], [pbk])
                    v3 = psb[:, 0:512].rearrange("p (k c) -> p k c", c=128)
                    cp("act", res["KsT"][:, :, gi * 128:gi * 128 + L], v3[:, 0:2, 0:L], [pbk], [("KsT", gi)])
                    cp("dve", res["KwT"][:, :, slot * 128:slot * 128 + L], v3[:, 2:4, 0:L], [pbk], [("KwT", slot)])
                    psb, pbk = nextB()
                    for g in range(4):
                        tr(psb[:64, g * 128:g * 128 + L], kst[:L, g * 64:(g + 1) * 64], ident_b[:L, :L], ["kst", "ident_b"], [pbk])
                        tr(psb[:64, 512 + g * 128:512 + g * 128 + L], kst[:L, 256 + g * 64:256 + (g + 1) * 64], ident_b[:L, :L],
                           ["kst", "ident_b"], [pbk])
                    v3 = psb[:64, :].rearrange("p (k c) -> p k c", c=128)
                    cp("act", XkT[:, :, i * 128:(i + 1) * 128], v3[:, 0:4, :], [pbk], [("XkT", i)])
                    cp("dve", XvT[:, :, i * 128:(i + 1) * 128], v3[:, 4:8, :], [pbk], [("XvT", i)])
                    cp("act", res["Vs"][:L, gi, :, 0:64], kv[:L, i, 768:1024].rearrange("p (g d) -> p g d", d=HD), kvk, [("Vs", gi)])
                    cp("dve", res["Vw"][:L, slot, :, 0:64], kv[:L, i, 1280:1536].rearrange("p (g d) -> p g d", d=HD), kvk, [("Vw", slot)])

            if CUTK(b) == 22:
                barrier()
                return
            if prompt:
                xk = [("XkT", i) for i in range(nt)]
                xv = [("XvT", i) for i in range(nt)]
                col0 = 32 * b.bi - 1
                lo = 1 if b.bi == 0 else 0
                for kvi, src, sk in ((0, XkT, xk), (1, XvT, xv)):
                    for g in range(4):
                        ps, pk = cmp_topbot(Pm, src, g, 32, kvi, sk)
                        cp("dve", topS[:, 0:1], res["ctop"][:, kvi, g:g + 1], ["ctop"], ["topS"])
                        cp("act", topS[:, 1:33], ps[:, 0:32], [pk], ["topS"])
                        cp("dve", res["ctop"][:, kvi, g:g + 1], topS[:, 32:33], ["topS"], ["ctop"])
                        stt("dve", hsum[:, 0:32], topS[:, 0:32], Pm["posb"][:, kvi:kvi + 1], ps[:, 256:288], ALU.add, ALU.add,
                            ["topS", pk, ("posb", kvi)], ["hsum"])
                        if kvi == 0:
                            gelu_to(Gk[:, 0:32], hsum[:, 0:32], 32, glt, ["hsum"], ["Gk"])
                            cmp_k_finish(Pm, C, Gk[:, lo:32], 32 - lo, res["KcT"][:, g, col0 + lo:col0 + 32], wk, [("KcT", g)])
                        else:
                            gelu_to(res["Gv"][:, g, col0 + lo:col0 + 32], hsum[:, lo:32], 32 - lo, glt, ["hsum"], [("Gv", g)])
                            for ct in range((col0 + lo) // 128, (col0 + 31) // 128 + 1):
                                ps2, pk2 = nextF()
                                mm(ps2[:, 0:64], res["Gv"][:, g, ct * 128:(ct + 1) * 128], Pm["w2"][:, 1, :], True, True,
                                   [("Gv", g), "w2"], [pk2])
                                cp("act", res["Vc"][:, ct, g, 0:64], ps2[:, 0:64], [pk2], [("Vc", ct, g)])

            if CUTK(b) == 23:
                barrier()
                return
            for i in range(nt):
                off = i * L
                gi = b.bi * 4 + i if prompt else None
                if (CUTK(b) == 29 and i == 1) or (CUTK(b) == 30 and i == 2):
                    barrier()
                    return
                if not prompt:
                    kvk = [("kv", i, c) for c in range(3)]
                    cp("act", kst[:L, 0:768], kv[:L, i, 0:768], kvk, ["kst"])
                    cp("act", kst[:L, 768:1024], kv[:L, i, 1024:1280], kvk, ["kst"])
                    sample_seq_prepare(l, b, i, Pm, C, S, kv, kst, wk, glt, hsum, Gk)
                if prompt:
                    nj = 64
                    cur0 = 2 * gi
                    ms("pool", FB[:, 0:nj], 0.0, ["FB"])
                    ms("pool", FB[:, 0:1], 1000.0, ["FB"])
                    ms("pool", FB[0:64, max(cur0 - 1, 0):cur0 + 1], 1000.0, ["FB"])
                    ms("pool", FB[64:128, cur0:cur0 + 2], 1000.0, ["FB"])
                    if cur0 + 1 < nj:
                        ms("pool", FB[0:64, cur0 + 1:nj], -1e30, ["FB"])
                    if cur0 + 2 < nj:
                        ms("pool", FB[64:128, cur0 + 2:nj], -1e30, ["FB"])
                    ksel = 16
                else:
                    nj = cfg.NSEL_S
                    ms("pool", FB[:L, 0:nj], 0.0, ["FB"])
                    ms("pool", FB[:L, 0:1], 1000.0, ["FB"])
                    ms("pool", FB[:L, nj - 1:nj], 1000.0, ["FB"])
                    ksel = 15
                nct = (8 * gi + 6) // 128 + 1 if prompt else (cfg.NCMP_S + 127) // 128
                for ct in range(nct):
                    coff = 2048 * ct - (128 * gi if prompt else cfg.PAST)
                    ts("dve", cmpcol[:, ct:ct + 1], kp[:, 0:1], 16.0, 991.0 + coff, ALU.mult, ALU.add, ["kp"], ["cmpcol"])
                for g in range(4):
                    if CUTK(b) == 35 and i == CUTI and g == CUTG:
                        barrier()
                        return
                    gam, hb = g // 2, (g % 2) * 64
                    qrhs = QT[hb:hb + 64, gam * 4:gam * 4 + 4, off:off + L]
                    qkeys = [("QT", gam * 4 + r) for r in range(4)]
                    tiles = []
                    if prompt:
                        nmax = 8 * gi + 6
                        navail = 32 * (b.bi + 1) - 1
                        for ct in range(nmax // 128 + 1):
                            nk = min(128, navail - 128 * ct)
                            tiles.append(dict(KT=res["KcT"][hb:hb + 64, g, ct * 128:ct * 128 + nk], nk=nk,
                                              V=res["Vc"][:nk, ct, g, :], delta=0, ct=ct, kcol=cmpcol[:, ct:ct + 1],
                                              mask=(128 * gi - 2048 * ct - 31, -16, 1),
                                              kkeys=[("KcT", g)], vkeys=[("Vc", ct, g)]))
                    else:
                        for ct in range((cfg.NCMP_S + 127) // 128):
                            nk = min(128, cfg.NCMP_S - 128 * ct)
                            tiles.append(dict(KT=S["KcT"][hb:hb + 64, g, ct * 128:ct * 128 + nk], nk=nk,
                                              V=S["Vc"][:nk, ct, g, :], delta=0, ct=ct, kcol=cmpcol[:, ct:ct + 1],
                                              mask=(cfg.PAST - 2048 * ct - 31, -16, 1),
                                              kkeys=[("sKcT", g)], vkeys=[("sVc", ct, g)]))
                    attend(C, g, L, qrhs, qkeys, tiles, psF[2], "F2", wk, imp=(psF[5], "F5", nj))
                    if CUTK(b) == 24 and i == CUTI:
                        barrier()
                        return
                    oc3 = psF[2][:L, 0:260].rearrange("p (r e) -> p r e", e=65)
                    ts("dve", sm[:L, 0:4], oc3[:, :, 64], 1e-37, None, ALU.max, None, ["F2"], ["sm_rc"])
                    T.op("dve", ["sm_rc"], ["sm_rc"], lambda: V.reciprocal(out=sm[:L, 0:4], in_=sm[:L, 0:4]))
                    i3 = psF[5][:L, 0:4 * nj].rearrange("p (r j) -> p r j", j=nj)
                    imp_ = impS[:L, 0, 0:nj]
                    ts("dve", imp_, i3[:, 0, :], sm[:L, 0:1], None, ALU.mult, None, ["F5", "sm_rc"], ["imp"])
                    for r in range(1, 4):
                        stt("dve", imp_, i3[:, r, :], sm[:L, r:r + 1], imp_, ALU.mult, ALU.add, ["F5", "sm_rc", "imp"], ["imp"])
                    tt("dve", imp_, imp_, FB[:L, 0:nj], ALU.add, ["imp", "FB"], ["imp"])
                    w1_, w2_ = impS[:L, 1, 0:nj], impS[:L, 2, 0:nj]
                    T.op("dve", ["imp"], ["mx"], lambda: V.max(out=mx[:L, 0:8], in_=imp_))
                    T.op("dve", ["imp", "mx"], ["impw1"], lambda: V.match_replace(out=w1_, in_to_replace=mx[:L, 0:8], in_values=imp_, imm_value=-2e30))
                    T.op("dve", ["impw1"], ["mx"], lambda: V.max(out=mx[:L, 8:16], in_=w1_))
                    if ksel == 15:
                        ms("dve", mx[:L, 15:16], -2e30, ["mx"])
                    T.op("dve", ["impw1", "mx"], ["impw2"], lambda: V.match_replace(out=w2_, in_to_replace=mx[:L, 8:16], in_values=w1_, imm_value=-2e30))
                    ts("dve", NBb[:L, 0:nj], w2_, -1.5e30, NEG, ALU.is_ge, ALU.mult, ["impw2"], ["NBb"])
                    njt = nj
                    if C["dup"]:
                        ts("dve", NBb[:L, 64:64 + nj], w2_, -1.5e30, NEG, ALU.is_ge, ALU.mult, ["impw2"], ["NBb"])
                        njt = 128
                    psb, pbk = nextB()
                    tr(psb[:njt, 0:L], NBb[:L, 0:njt], ident_b[:L, :L], ["NBb", "ident_b"], [pbk])
                    cp("act", wk["NBT"][:njt, 0:4 * L].rearrange("p (r q) -> p r q", q=L),
                       psb[:njt, 0:L].unsqueeze(1).to_broadcast([njt, 4, L]), [pbk], ["NBT"])
                    if CUTK(b) == 25 and i == CUTI:
                        barrier()
                        return
                    tiles = []
                    if prompt:
                        for t in range(gi + 1):
                            tiles.append(dict(KT=res["KsT"][hb:hb + 64, gam, t * 128:(t + 1) * 128], nk=128,
                                              V=res["Vs"][:, t, g, :], delta=gi - t, eblk=t, nj=nj, hb=hb,
                                              mask="causal" if t == gi else None,
                                              kkeys=[("KsT", t)], vkeys=[("Vs", t)]))
                    else:
                        for pg in range(cfg.NPG):
                            tiles.append(dict(KT=S["KsT"][hb:hb + 64, gam, pg * 128:(pg + 1) * 128], nk=128,
                                              V=S["Vs"][:, pg, g, :], delta=cfg.NPG - pg, eblk=pg, nj=nj, hb=hb,
                                              mask=None, kkeys=["sKsT"], vkeys=["sVs"]))
                        tiles.append(dict(KT=S["KsT"][hb:hb + 64, gam, cfg.PAST:cfg.PAST + 8], nk=8,
                                          V=S["Vs"][:8, cfg.NPG, g, :], delta=0, eblk=None,
                                          mask="causal", kkeys=["sKsT"], vkeys=["sVs"]))
                    attend(C, g, L, qrhs, qkeys, tiles, psF[3], "F3", wk)
                    if CUTK(b) == 26 and i == CUTI:
                        barrier()
                        return
                    tiles = []
                    if prompt:
                        for t in range(max(0, gi - 4), gi + 1):
                            sl = t % 8
                            m = "causal" if t == gi else ("win" if t == gi - 4 else None)
                            tiles.append(dict(KT=res["KwT"][hb:hb + 64, gam, sl * 128:(sl + 1) * 128], nk=128,
                                              V=res["Vw"][:, sl, g, :], delta=gi - t, mask=m,
                                              kkeys=[("KwT", sl)], vkeys=[("Vw", sl)]))
                    else:
                        for t in range(4):
                            tiles.append(dict(KT=S["KwT"][hb:hb + 64, gam, t * 128:(t + 1) * 128], nk=128,
                                              V=S["Vw"][:, t, g, :], delta=4 - t, mask="win" if t == 0 else None,
                                              kkeys=["sKwT"], vkeys=["sVw"]))
                        tiles.append(dict(KT=S["KwT"][hb:hb + 64, gam, WIN:WIN + 8], nk=8, V=S["Vw"][:8, 4, g, :],
                                          delta=0, mask="causal", kkeys=["sKwT"], vkeys=["sVw"]))
                    attend(C, g, L, qrhs, qkeys, tiles, psF[4], "F4", wk)
                    if CUTK(b) == 27 and i == CUTI:
                        barrier()
                        return
                    dn = "att_%d_%s%d_%d_%d" % (l, b.kind, b.bi, i, g)
                    if dn in dbg_names:
                        for x, (bank, bkey) in enumerate(((2, "F2"), (3, "F3"), (4, "F4"))):
                            cp("act", dbgt[:L, x, :], psF[bank][:L, 0:260], [bkey], [("dbgt", x)])
                        dbg(dn, dbgt[:L], [("dbgt", x) for x in range(3)])
                        dbg("imp_" + dn, impS[:L, :, 0:nj], ["imp", "impw1", "impw2"])
                    g3 = gs[:L, i, g * 12:(g + 1) * 12].rearrange("p (r x) -> p r x", x=3)
                    for x, (bank, bkey) in enumerate(((2, "F2"), (3, "F3"), (4, "F4"))):
                        o3 = psF[bank][:L, 0:260].rearrange("p (r e) -> p r e", e=65)
                        ts("dve", sm[:L, 8 + 4 * x:12 + 4 * x], o3[:, :, 64], 1e-37, None, ALU.max, None, [bkey], [("rd", x)])
                        T.op("dve", [("rd", x)], [("rd", x)], lambda: V.reciprocal(out=sm[:L, 8 + 4 * x:12 + 4 * x], in_=sm[:L, 8 + 4 * x:12 + 4 * x]))
                        tt("dve", sm[:L, 8 + 4 * x:12 + 4 * x], sm[:L, 8 + 4 * x:12 + 4 * x], g3[:, :, x], ALU.mult,
                           [("rd", x), ("gs", i)], [("rd", x)])
                        dst = (t1 if x != 1 else t2)[:L, :].rearrange("p (r d) -> p r d", d=HD)
                        if x == 2:
                            dst = t2[:L, :].rearrange("p (r d) -> p r d", d=HD)
                        tt("dve", dst, o3[:, :, 0:64], sm[:L, 8 + 4 * x:12 + 4 * x].unsqueeze(2).to_broadcast([L, 4, HD]), ALU.mult,
                           [bkey, ("rd", x)], ["t2C" if x else "t1C"])
                        if x >= 1:
                            out_ = yc[:L, g * 256:(g + 1) * 256] if x == 2 else t1[:L, :]
                            tt("dve", out_, t1[:L, :], t2[:L, :], ALU.add, ["t1C", "t2C"], [("yc", g)] if x == 2 else ["t1C"])
                if CUTK(b) == 28 and i == CUTI:
                    barrier()
                    return
                yk = [("yc", g) for g in range(4)]
                act(ycn[:L, :], yc[:L, :], AF.Square, yk, ["ycn", "ycss"], accum_out=sm[:L, 32:33])
                if CUTK(b) == 31 and i == CUTI:
                    barrier()
                    return
                rsq(sm[:L, 33:34], sm[:L, 32:33], 1.0 / D_ATTN, ["ycss"], ["ycr"])
                stt("dve", ycn[:L, :], yc[:L, :], sm[:L, 33:34], gout[:L, :], ALU.mult, ALU.mult, yk + ["ycr", "goutC", "ycn"], ["ycn"])
                if CUTK(b) == 32 and i == CUTI:
                    barrier()
                    return
                psb, pbk = nextB()
                for m in range(8):
                    tr(psb[:, m * 128:m * 128 + L], ycn[:L, m * 128:(m + 1) * 128], ident_b[:L, :L], ["ycn", "ident_b"], [pbk])
                cp("act", mergedT[:, 8:16, off:off + L], psb[:, :].rearrange("p (k c) -> p k c", c=128)[:, :, 0:L],
                   [pbk], [("mT", 8 + m) for m in range(8)])
                if CUTK(b) == 33 and i == CUTI:
                    barrier()
                    return
                dbg("yc_%d_%s%d_%d" % (l, b.kind, b.bi, i), yc[:L, :], yk)
            barrier()

    for l in range(DEPTH):
        with ExitStack() as lctx:
            Pm = load_layer_params(lctx, l)
            nsa_layer_params(lctx, l, Pm)
            carry = {"A": sb(lctx, "carryA", [128, 4, 2]), "B": sb(lctx, "carryB", [128, 8, 3]),
                     "hT": sb(lctx, "hT", [128, D_SSM]), "hT_bf": sb(lctx, "hT_bf", [128, D_SSM], BF16)}
            barrier()
            for grp in ("p", "s"):
                with ExitStack() as gctx:
                    res = prompt_res(gctx) if (grp == "p" and stage >= 3) else None
                    blocks = [Blk(cfg, l, "p", i) for i in range(cfg.NBLK)] if grp == "p" else [Blk(cfg, l, "s", 0)]
                    for b in blocks:
                        with ExitStack() as bctx:
                            L, nt = b.L, b.nt
                            xT = sb(bctx, "xT", [128, KT, b.TT], BF16)
                            mergedT = sb(bctx, "mergedT", [128, KT, b.TT], BF16)
                            with ExitStack() as c0:
                                x_sb = sb(c0, "x_sb", [128, nt, D])
                                gmix = sb(c0, "gmix", [128, D])
                                xn_ = sb(c0, "xn", [128, D], BF16)
                                wk = {"junk": xn_, "xn": xn_, "st": sb(c0, "nst", [128, 8])}
                                dma("sp", gmix[:], I["norm_mix"][l].partition_broadcast(128), [], ["gmix"])
                                for i in range(nt):
                                    dma("sp", x_sb[:L, i, :], x_src(l, b, i), [], [("x", i)])
                                norm_T(b, x_sb, "x", gmix, "gmix", xT, "xT", wk)
                                barrier()
                            dbg("xT_%d_%s%d" % (l, b.kind, b.bi), xT[:], xT_keys("xT", b))
                            mixer_a(l, b, Pm, xT, mergedT, carry)
                            if stage <= 1:
                                continue
                            mixer_b(l, b, Pm, xT, mergedT, carry)
                            if stage <= 2:
                                continue
                            with ExitStack() as nctx:
                                C = nsa_consts(nctx, (64 if b.kind == "p" else cfg.NSEL_S) <= 64, SEQ if b.kind == "p" else cfg.PAST)
                                mixer_c(l, b, Pm, C, xT, mergedT, res)
                            if stage <= 3:
                                continue
                            dense_tail(l, b, Pm, xT, mergedT)
                    barrier()
            barrier()
        if stage <= 3:
            break
    T.finish()
    es.close()
    return nc, I, O, DBG, hc


N_CORES = 8
_WEIGHTS = ["norm_mix", "w_in", "conv_a_w", "conv_b_w", "conv_b_bias", "dt_bias", "a_log", "d_skip", "q_norm",
            "k_norm", "cmp_pos", "cmp_w1", "cmp_w2", "norm_out", "w_out", "norm_ffn", "w_gate", "w_up", "w_down"]
_PROG = {}


def _core_inputs(inp, c, cfg, hc):
    NS = cfg.NS
    f = lambda a: np.ascontiguousarray(np.asarray(a, dtype=np.float32))
    m = {}
    m["xp"] = f(inp["x_prompt"][c % 2])
    m["xs"] = f(inp["x_sample"][NS * c:NS * (c + 1)]).reshape(NS * 8, D)
    m["ck"] = f(inp["cache_k"]).reshape(DEPTH, -1, 512)
    m["cv"] = f(inp["cache_v"]).reshape(DEPTH, -1, 512)
    m["cwk"] = f(inp["cache_win_k"][:, NS * c:NS * (c + 1)]).reshape(DEPTH, NS, WIN, 256)
    m["cwv"] = f(inp["cache_win_v"][:, NS * c:NS * (c + 1)]).reshape(DEPTH, NS, WIN, 256)
    m["sca"] = f(inp["state_conv_a"][:, NS * c:NS * (c + 1)])
    m["scb"] = f(inp["state_conv_b"][:, NS * c:NS * (c + 1)])
    m["ssm"] = f(inp["state_ssm"][:, NS * c:NS * (c + 1)]).reshape(DEPTH, NS, D_SSM, SSM_N)
    m["pt"] = np.ascontiguousarray(np.asarray(inp["page_table"][NS * c:NS * (c + 1)], dtype=np.int32))
    for k in _WEIGHTS:
        m[k] = f(inp[k])
    m.update(hc)
    return m


def kernel(**inputs):
    seq = int(inputs["x_prompt"].shape[1])
    nb = int(inputs["x_sample"].shape[0])
    past = int(inputs["page_table"].shape[1]) * 128
    npool = int(inputs["cache_k"].shape[1])
    ns = nb // N_CORES
    key = (seq, past, ns, npool)
    if key not in _PROG:
        cfg = Cfg(seq=seq, past=past, ns=ns, npool=npool)
        _PROG[key] = (cfg,) + build(cfg)
    cfg, nc, I, O, DBG, hc = _PROG[key]
    in_maps = [_core_inputs(inputs, c, cfg, hc) for c in range(N_CORES)]
    res = run_bass_kernel_spmd(nc, in_maps, core_ids=list(range(N_CORES)))
    R_ = res.results
    B = int(inputs["x_prompt"].shape[0])
    pstack = lambda nm: np.stack([R_[b][nm] for b in range(B)], axis=1)
    scat = lambda nm: np.concatenate([R_[c][nm] for c in range(N_CORES)], axis=1)
    y_prompt = np.stack([R_[b]["yp"] for b in range(B)], axis=0)
    y_sample = np.concatenate([R_[c]["ys"] for c in range(N_CORES)], axis=0).reshape(nb, 8, D)
    outs = (
        y_prompt, y_sample,
        pstack("nkp").reshape(DEPTH, B, seq, 2, NG, HD), pstack("nvp").reshape(DEPTH, B, seq, 2, NG, HD),
        pstack("nwkp").reshape(DEPTH, B, WIN, NG, HD), pstack("nwvp").reshape(DEPTH, B, WIN, NG, HD),
        pstack("ncap"), pstack("ncbp"), pstack("nssp").reshape(DEPTH, B, SSM_H, HD, SSM_N),
        scat("nks").reshape(DEPTH, nb, 8, 2, NG, HD), scat("nvs").reshape(DEPTH, nb, 8, 2, NG, HD),
        scat("nwks").reshape(DEPTH, nb, WIN, NG, HD), scat("nwvs").reshape(DEPTH, nb, WIN, NG, HD),
        scat("ncas"), scat("ncbs"), scat("nsss").reshape(DEPTH, nb, SSM_H, HD, SSM_N),
    )
    return tuple(np.ascontiguousarray(o, dtype=np.float32) for o in outs)
```

```python
import math
from contextlib import ExitStack

import numpy as np
import concourse.bass as bass
import concourse.mybir as mybir
from concourse.bass_utils import run_bass_kernel_spmd

F32 = mybir.dt.float32
BF16 = mybir.dt.bfloat16
I32 = mybir.dt.int32
U32 = mybir.dt.uint32
AF = mybir.ActivationFunctionType
ALU = mybir.AluOpType
AX = mybir.AxisListType

D = 2048
KT = D // 128
DEPTH = 2
HD = 64
D_CONV = 512
D_SSM = 512
D_ATTN = 1024
SSM_H = 8
SSM_N = 128
XBC = 1024
NH = 16
NG = 4
RPG = 4
D_FF = 5632
FFT = D_FF // 128
COLS_A = 1536
COLS_B = 512 + 1024 + 8
KV_COLS = 1536
COLS_C = 1024 + 1536 + 48
D_IN = COLS_A + COLS_B + COLS_C
OFF_B = COLS_A
OFF_C = COLS_A + COLS_B
EPS = 1e-6
NEG = -30000.0
TB = 512
WIN = 512
ND = 12


class Trk:
    def __init__(self, nc, es):
        self.nc = nc
        self.eng = {"pe": nc.tensor, "act": nc.scalar, "dve": nc.vector,
                    "pool": nc.gpsimd, "sp": nc.sync}
        self.sem = {k: es.enter_context(nc.semaphore("s_" + k)) for k in self.eng}
        self.cnt = {k: 0 for k in self.eng}
        self.seen = {k: {} for k in self.eng}
        self.res = {}
        self.dq = ("sp", "pool", "act")
        self.dsem = {q: [es.enter_context(nc.semaphore("d_%s%d" % (q, i))) for i in range(ND)]
                     for q in self.dq}
        self.dval = {q: [0] * ND for q in self.dq}
        self.dnext = {q: 0 for q in self.dq}
        self.nwait = 0
        self.excl = set(["F%d" % i for i in range(8)] + ["B%d" % i for i in range(8)])

    def _wait(self, e, tok, raw):
        if tok is None:
            return
        key, h, v, src = tok
        if src == e and e == "pe":
            return
        if self.seen[e].get(key, 0) >= v:
            return
        self.eng[e].wait_ge(h, v)
        self.nwait += 1
        self.seen[e][key] = v

    def _deps(self, e, reads, writes):
        for k in reads:
            r = self.res.get(k)
            if r is not None:
                self._wait(e, r[0], True)
                if isinstance(k, str) and k in self.excl:
                    for t in r[1].values():
                        if t[3] != e:
                            self._wait(e, t, False)
        for k in writes:
            r = self.res.get(k)
            if r is not None:
                self._wait(e, r[0], False)
                for t in r[1].values():
                    self._wait(e, t, False)

    def _record(self, tok, reads, writes):
        for k in reads:
            r = self.res.setdefault(k, [None, {}])
            r[1][tok[0]] = tok
        for k in writes:
            self.res[k] = [tok, {}]

    def op(self, e, reads, writes, emit, sig=True):
        self._deps(e, reads, writes)
        ins = emit()
        if sig:
            self.cnt[e] += 1
            ins.then_inc(self.sem[e], 1)
            tok = (e, self.sem[e], self.cnt[e], e)
        else:
            tok = (e, self.sem[e], self.cnt[e] + 1, e)
        self._record(tok, reads, writes)
        return tok

    def dma(self, q, out, in_, reads, writes, indirect=None, **kw):
        self._deps(q, reads, writes)
        i = self.dnext[q]
        self.dnext[q] = (i + 1) % ND
        h = self.dsem[q][i]
        v = self.dval[q][i]
        if v > 0:
            self._wait(q, ((q, i), h, v, None), True)
        if indirect is not None:
            ins = self.eng[q].indirect_dma_start(out=out, out_offset=None, in_=in_,
                                                 in_offset=indirect, **kw)
        else:
            ins = self.eng[q].dma_start(out=out, in_=in_, **kw)
        ins.then_inc(h, 16)
        self.dval[q][i] = v + 16
        tok = ((q, i), h, v + 16, None)
        self._record(tok, reads, writes)
        return tok

    def all_tokens(self):
        toks = []
        for e in self.eng:
            if self.cnt[e] > 0:
                toks.append((e, self.sem[e], self.cnt[e], e))
        for q in self.dq:
            for i in range(ND):
                if self.dval[q][i] > 0:
                    toks.append(((q, i), self.dsem[q][i], self.dval[q][i], None))
        return toks

    def barrier(self, scratch):
        for t in self.all_tokens():
            if t[3] != "pool":
                self._wait("pool", t, True)
        if self.cnt["pool"] > 0:
            self._wait("pool", ("pool", self.sem["pool"], self.cnt["pool"], "pool"), True)
        self.cnt["pool"] += 1
        self.nc.gpsimd.memset(scratch, 0.0).then_inc(self.sem["pool"], 1)
        tok = ("pool", self.sem["pool"], self.cnt["pool"], "pool")
        for e in self.eng:
            if e != "pool":
                self._wait(e, tok, True)
        self.res = {}

    def finish(self):
        for t in self.all_tokens():
            if t[3] != "sp":
                self._wait("sp", t, True)


class Cfg:
    def __init__(self, seq=4096, past=8192, ns=4, npool=2560):
        self.SEQ = seq
        self.PAST = past
        self.NS = ns
        self.NPOOL = npool
        self.NBLK = seq // TB
        self.NPG = past // 128
        self.NTILE = seq // 128
        self.NCMP_P = seq // 16
        self.NCMP_S = past // 16 - 1
        self.NSEL_S = past // 64


def host_consts(cfg):
    c = {}
    c["c_ident"] = np.eye(128, dtype=np.float32)
    t = np.arange(128)
    c["c_tri"] = (t[:, None] <= t[None, :]).astype(np.float32)
    c["c_negtri"] = np.where(t[None, :] < t[:, None], NEG, 0.0).astype(np.float32)
    nd = max(cfg.NTILE, cfg.NPG) + 2
    c["c_kp"] = (t[:, None] - 64 - 128 * np.arange(nd)[None, :]).astype(np.float32)
    slopes = np.exp2(-8.0 * np.arange(1, NH + 1, dtype=np.float32) / NH).astype(np.float32)
    c["c_slope"] = np.broadcast_to(slopes[None, :, None], (128, NH, 128)).astype(np.float32).copy()
    nkk = max(cfg.SEQ, cfg.PAST)
    kk = np.arange(nkk)
    c["c_ee"] = (kk[None, :] // 64 == np.arange(128)[:, None]).astype(np.float32)
    c["c_eed"] = (kk[None, :] // 64 == (np.arange(128) % 64)[:, None]).astype(np.float32)
    ncol_p = ((cfg.NCMP_P + 127) // 128) * 128
    col = np.arange(ncol_p)
    sp = ((col[:, None] - 1) // 4 == np.arange(128)[None, :]) & (col[:, None] >= 1)
    c["c_selp"] = sp.astype(np.float32).reshape(ncol_p // 128, 128, 128).transpose(1, 0, 2).copy()
    ncol_s = ((cfg.NCMP_S + 127) // 128) * 128
    col = np.arange(ncol_s)
    ss = (col[:, None] // 4 == np.arange(128)[None, :])
    c["c_sels"] = ss.astype(np.float32).reshape(ncol_s // 128, 128, 128).transpose(1, 0, 2).copy()
    return c


import os
CUT = int(os.environ.get("KCUT", "0"))


CUTKIND = os.environ.get("KCUTK", "ps")
CUTI = int(os.environ.get("KCUTI", "0"))
CUTG = int(os.environ.get("KCUTG", "1"))


def CUTK(b):
    v = os.environ.get("KCUTP" if b.kind == "p" else "KCUTS")
    if v is not None:
        return int(v)
    return CUT if b.kind in CUTKIND else 0


class Cut(Exception):
    pass


def cutpoint(n):
    if CUT == n:
        raise Cut()


class Blk:
    def __init__(self, cfg, l, kind, bi):
        self.l = l
        self.kind = kind
        self.bi = bi
        if kind == "p":
            self.nt = TB // 128
            self.L = 128
            self.nseq = 1
            self.Lseq = TB
        else:
            self.nt = cfg.NS
            self.L = 8
            self.nseq = cfg.NS
            self.Lseq = 8
        self.TT = self.nt * self.L


def build(cfg, stage=99, dbg_names=()):
    nc = bass.Bass("TRN2", target_bir_lowering=False)
    es = ExitStack()
    SEQ, NS, NPOOL, NPG = cfg.SEQ, cfg.NS, cfg.NPOOL, cfg.NPG

    def dram(name, shape, dtype=F32, kind="ExternalInput"):
        return nc.dram_tensor(name, list(shape), dtype, kind=kind).ap()

    I = {}
    I["xp"] = dram("xp", [SEQ, D])
    I["xs"] = dram("xs", [NS * 8, D])
    I["ck"] = dram("ck", [DEPTH, NPOOL * 128, 512])
    I["cv"] = dram("cv", [DEPTH, NPOOL * 128, 512])
    I["cwk"] = dram("cwk", [DEPTH, NS, WIN, 256])
    I["cwv"] = dram("cwv", [DEPTH, NS, WIN, 256])
    I["sca"] = dram("sca", [DEPTH, NS, 2, D_CONV])
    I["scb"] = dram("scb", [DEPTH, NS, 3, XBC])
    I["ssm"] = dram("ssm", [DEPTH, NS, D_SSM, SSM_N])
    I["pt"] = dram("pt", [NS, NPG], I32)
    wshapes = {
        "norm_mix": [DEPTH, D], "w_in": [DEPTH, D, D_IN], "conv_a_w": [DEPTH, 3, D_CONV],
        "conv_b_w": [DEPTH, 4, XBC], "conv_b_bias": [DEPTH, XBC], "dt_bias": [DEPTH, SSM_H],
        "a_log": [DEPTH, SSM_H], "d_skip": [DEPTH, SSM_H], "q_norm": [DEPTH, HD],
        "k_norm": [DEPTH, 3, HD], "cmp_pos": [DEPTH, 2, 32, HD], "cmp_w1": [DEPTH, 2, 2048, 128],
        "cmp_w2": [DEPTH, 2, 128, HD], "norm_out": [DEPTH, D], "w_out": [DEPTH, D, D],
        "norm_ffn": [DEPTH, D], "w_gate": [DEPTH, D, D_FF], "w_up": [DEPTH, D, D_FF],
        "w_down": [DEPTH, D_FF, D],
    }
    for k, s in wshapes.items():
        I[k] = dram(k, s)
    hc = host_consts(cfg)
    for k, v in hc.items():
        I[k] = dram(k, list(v.shape))

    O = {}

    def out(name, shape):
        O[name] = dram(name, shape, kind="ExternalOutput")

    out("yp", [SEQ, D])
    out("ys", [NS * 8, D])
    out("nkp", [DEPTH, SEQ, 512])
    out("nvp", [DEPTH, SEQ, 512])
    out("nwkp", [DEPTH, WIN, 256])
    out("nwvp", [DEPTH, WIN, 256])
    out("ncap", [DEPTH, 2, D_CONV])
    out("ncbp", [DEPTH, 3, XBC])
    out("nssp", [DEPTH, D_SSM, SSM_N])
    out("nks", [DEPTH, NS * 8, 512])
    out("nvs", [DEPTH, NS * 8, 512])
    out("nwks", [DEPTH, NS, WIN, 256])
    out("nwvs", [DEPTH, NS, WIN, 256])
    out("ncas", [DEPTH, NS, 2, D_CONV])
    out("ncbs", [DEPTH, NS, 3, XBC])
    out("nsss", [DEPTH, NS, D_SSM, SSM_N])
    x1p = nc.dram_tensor("x1p", [SEQ, D], F32, kind="Internal").ap()
    x1s = nc.dram_tensor("x1s", [NS * 8, D], F32, kind="Internal").ap()
    DBG = {}

    T = Trk(nc, es)
    V, S_, G_, PE = nc.vector, nc.scalar, nc.gpsimd, nc.tensor

    uniq = {"n": 0}

    def sb(ctx, name, shape, dtype=F32):
        uniq["n"] += 1
        return ctx.enter_context(nc.sbuf_tensor("%s_%d" % (name, uniq["n"]), list(shape), dtype))

    def mm(out_, lhsT, rhs, start, stop, R, W):
        return T.op("pe", R, W, lambda: PE.matmul(out_, lhsT, rhs, start=start, stop=stop), sig=stop)

    def tr(out_, in_, ident, R, W):
        return T.op("pe", R, W, lambda: PE.transpose(out_, in_, ident))

    def trf(out_, in_, R, W):
        K_ = in_.shape[0]
        return T.op("pe", R, W, lambda: PE.matmul(out_, in_, ident_f[:K_, :K_], start=True, stop=True))

    def tr_hilo(out_, in_, hl, R, W):
        cp("act", hl[:, 0, :], in_, R, ["hl_hi"])
        tt("dve", hl[:, 1, :], in_, hl[:, 0, :], ALU.subtract, R + ["hl_hi"], ["hl_lo"])
        psb, pbk = nextB()
        tr(psb[:, 0:128], hl[:, 0, :], ident_b[:, :], ["hl_hi", "ident_b"], [pbk])
        tr(psb[:, 128:256], hl[:, 1, :], ident_b[:, :], ["hl_lo", "ident_b"], [pbk])
        cp("act", out_, psb[:, 0:128], [pbk], W)
        tt("dve", out_, out_, psb[:, 128:256], ALU.add, [pbk] + W, W)

    def act(out_, in_, func, R, W, **kw):
        return T.op("act", R, W, lambda: S_.activation(out=out_, in_=in_, func=func, **kw))

    def tt(e, out_, in0, in1, op, R, W):
        return T.op(e, R, W, lambda: T.eng[e].tensor_tensor(out=out_, in0=in0, in1=in1, op=op))

    def ts(e, out_, in0, s1, s2, op0, op1, R, W):
        if op1 is None:
            return T.op(e, R, W, lambda: T.eng[e].tensor_scalar(out=out_, in0=in0, scalar1=s1, scalar2=None, op0=op0))
        return T.op(e, R, W, lambda: T.eng[e].tensor_scalar(out=out_, in0=in0, scalar1=s1, scalar2=s2, op0=op0, op1=op1))

    def stt(e, out_, in0, scalar, in1, op0, op1, R, W):
        return T.op(e, R, W, lambda: T.eng[e].scalar_tensor_tensor(out=out_, in0=in0, scalar=scalar, in1=in1, op0=op0, op1=op1))

    def cp(e, out_, in_, R, W):
        if e == "act":
            return T.op("act", R, W, lambda: S_.copy(out=out_, in_=in_))
        return T.op(e, R, W, lambda: T.eng[e].tensor_copy(out=out_, in_=in_))

    def rsq(out_, in_, scale, R, W):
        act(out_, in_, AF.Sqrt, R, W, scale=scale, bias=EPS)
        return T.op("dve", W, W, lambda: V.reciprocal(out=out_, in_=out_))

    def ms(e, ap, val, W):
        return T.op(e, [], W, lambda: T.eng[e].memset(ap, val))

    def dma(q, out_, in_, R, W, **kw):
        return T.dma(q, out_, in_, R, W, **kw)

    psF = [es.enter_context(nc.psum_tensor("psF%d" % i, [128, 512], F32)) for i in range(6)]
    psB = [es.enter_context(nc.psum_tensor("psB%d" % i, [128, 1024], BF16)) for i in range(2)]
    rr = {"F": 0, "B": 0}

    def nextF(banks=(0, 1)):
        i = banks[rr["F"] % len(banks)]
        rr["F"] += 1
        return psF[i], "F%d" % i

    def nextB():
        i = rr["B"] % 2
        rr["B"] += 1
        return psB[i], "B%d" % i

    ident_f = sb(es, "ident_f", [128, 128])
    ident_b = sb(es, "ident_b", [128, 128], BF16)
    kp = sb(es, "kp", [128, hc["c_kp"].shape[1]])
    slope = sb(es, "slope", [128, NH])
    ones_b = sb(es, "ones_b", [128, 128], BF16)
    ones_f = sb(es, "ones_f", [128, 128])
    bsc = sb(es, "bsc", [128, 8])
    dma("sp", ident_f[:], I["c_ident"], [], ["ident_f"])
    dma("pool", ident_b[:], I["c_ident"], [], ["ident_b"])
    tri_b = sb(es, "tri_b", [128, 128], BF16)
    negtri_b = sb(es, "negtri_b", [128, 128], BF16)
    dma("pool", tri_b[:], I["c_tri"], [], ["tri_b"])
    dma("pool", negtri_b[:], I["c_negtri"], [], ["negtri_b"])
    dma("sp", kp[:], I["c_kp"], [], ["kp"])
    dma("sp", slope[:], I["c_slope"][:, :, 0], [], ["slope"], allow_slow_non_contiguous=True)
    maskC = sb(es, "maskC", [128, 128])
    maskW = sb(es, "maskW", [128, 128])
    dma("sp", maskC[:], I["c_negtri"], [], ["maskCW"])
    dma("sp", maskW[:], I["c_negtri"].rearrange("a b -> b a"), [], ["maskCW"], allow_slow_non_contiguous=True)
    ms("dve", ones_b[:], 1.0, ["ones_b"])
    ms("dve", ones_f[:], 1.0, ["ones_f"])

    fill_reg = G_.to_reg(NEG)

    def barrier():
        T.barrier(bsc[:, 0:1])

    def dbg(name, ap, R):
        if name not in dbg_names:
            return
        shape = list(ap.shape)
        DBG[name] = dram("dbg_" + name, shape, ap.dtype, kind="ExternalOutput")
        dma("sp", DBG[name], ap, R, [("dbg", name)])

    NRING = 2
    ring = [None] * NRING
    pstate = {"n": 0}

    def ring_alloc(ctx):
        for i_ in range(NRING):
            ring[i_] = sb(ctx, "ring%d" % i_, [128, 8192], BF16)

    class Panel:
        def __init__(self, src, kt, pc, srcs=None):
            self.src, self.kt, self.pc = src, kt, pc
            self.srcs = srcs
            self.slot = None

        def issue(self):
            if self.slot is not None:
                return
            self.slot = pstate["n"] % NRING
            pstate["n"] += 1
            v = ring[self.slot][:, 0:self.kt * self.pc].rearrange("p (k c) -> p k c", c=self.pc)
            self.view = v
            self.key = ("ring", self.slot)
            self.keys = []
            if self.srcs is None:
                hk_ = max(1, self.kt // 2)
                for n_, k0 in enumerate(range(0, self.kt, hk_)):
                    k1 = min(self.kt, k0 + hk_)
                    key_ = ("ring", self.slot, n_)
                    self.keys.append(key_)
                    dma("pool", v[:, k0:k1, :], self.src[k0 * 128:k1 * 128, :].rearrange("(k p) c -> p k c", p=128),
                        [], [key_])
            else:
                w_ = self.pc // len(self.srcs)
                for n_, src_ in enumerate(self.srcs):
                    key_ = ("ring", self.slot, n_)
                    self.keys.append(key_)
                    dma("pool", v[:, :, n_ * w_:(n_ + 1) * w_], src_.rearrange("(k p) c -> p k c", p=128), [], [key_])

    def run_panels(panels, consume):
        for j, p in enumerate(panels):
            p.issue()
            if j + 1 < len(panels) and NRING > 1:
                panels[j + 1].issue()
            consume(j, p)

    def col_panels(w_ap, c0, c1, kt, width=512):
        ps_ = []
        c = c0
        while c < c1:
            pc = min(width, c1 - c)
            ps_.append((c, Panel(w_ap[:, c:c + pc], kt, pc)))
            c += pc
        return ps_

    def load_layer_params(ctx, l):
        Pm = {}
        Pm["g_out_fm"] = sb(ctx, "g_out_fm", [128, 4])
        dma("sp", Pm["g_out_fm"][:], I["norm_out"][l][0:512].rearrange("(m p) -> p m", p=128),
            [], ["g_out_fm"], allow_slow_non_contiguous=True)
        Pm["wA"] = sb(ctx, "wA", [128, 4, 3])
        for m in range(4):
            dma("sp", Pm["wA"][:, m, :], I["conv_a_w"][l][:, m * 128:(m + 1) * 128].rearrange("k p -> p k"),
                [], ["wA"], allow_slow_non_contiguous=True)
        Pm["wB"] = sb(ctx, "wB", [128, 8, 4])
        for m in range(8):
            dma("sp", Pm["wB"][:, m, :], I["conv_b_w"][l][:, m * 128:(m + 1) * 128].rearrange("k p -> p k"),
                [], ["wB"], allow_slow_non_contiguous=True)
        Pm["bB"] = sb(ctx, "bB", [128, 8])
        dma("sp", Pm["bB"][:], I["conv_b_bias"][l].rearrange("(m p) -> p m", p=128),
            [], ["bB"], allow_slow_non_contiguous=True)
        for nm in ("dt_bias", "a_log", "d_skip"):
            Pm[nm] = sb(ctx, nm, [128, SSM_H])
            dma("sp", Pm[nm][:], I[nm][l].partition_broadcast(128), [], [nm])
        Pm["negA"] = sb(ctx, "negA", [128, SSM_H])
        act(Pm["negA"][:], Pm["a_log"][:], AF.Exp, ["a_log"], ["negA"])
        ts("dve", Pm["negA"][:], Pm["negA"][:], -1.0, None, ALU.mult, None, ["negA"], ["negA"])
        Pm["dskip"] = sb(ctx, "dskip", [128, D_SSM])
        cp("dve", Pm["dskip"][:].rearrange("p (h d) -> p h d", d=HD),
           Pm["d_skip"][:].unsqueeze(2).to_broadcast([128, SSM_H, HD]), ["d_skip"], ["dskip"])
        Pm["qg"] = sb(ctx, "qg", [64, 1])
        dma("sp", Pm["qg"][:], I["q_norm"][l].rearrange("(d o) -> d o", o=1), [], ["qg"],
            allow_slow_non_contiguous=True)
        ts("dve", Pm["qg"][:], Pm["qg"][:], 0.125, None, ALU.mult, None, ["qg"], ["qg"])
        Pm["kg"] = sb(ctx, "kg", [128, 3, HD])
        dma("sp", Pm["kg"][:], I["k_norm"][l].partition_broadcast(128), [], ["kg"])
        Pm["kg0"] = sb(ctx, "kg0", [64, 1])
        dma("sp", Pm["kg0"][:], I["k_norm"][l][0].rearrange("(d o) -> d o", o=1), [], ["kg0"],
            allow_slow_non_contiguous=True)
        Pm["w1"] = sb(ctx, "w1", [64, 2, 32, 128], BF16)
        for kv in range(2):
            dma("pool", Pm["w1"][:, kv], I["cmp_w1"][l][kv].rearrange("(j d) h -> d j h", d=HD),
                [], [("w1", kv)])
        Pm["posT"] = sb(ctx, "posT", [64, 2, 32], BF16)
        for kv in range(2):
            dma("pool", Pm["posT"][:, kv, :], I["cmp_pos"][l][kv].rearrange("j d -> d j"), [], ["posT"],
                allow_slow_non_contiguous=True)
        Pm["w2"] = sb(ctx, "w2", [128, 2, HD], BF16)
        dma("pool", Pm["w2"][:], I["cmp_w2"][l].rearrange("v h d -> h v d"), [], ["w2"])
        Pm["posb"] = sb(ctx, "posb", [128, 2])
        for kv in range(2):
            ps, pk = nextF()
            for j in range(32):
                mm(ps[:, 0:1], Pm["w1"][:, kv, j, :], Pm["posT"][:, kv, j:j + 1], j == 0, j == 31,
                   [("w1", kv), "posT"], [pk])
            cp("dve", Pm["posb"][:, kv:kv + 1], ps[:, 0:1], [pk], [("posb", kv)])
        return Pm

    def x_src(l, b, i):
        if b.kind == "p":
            base = b.bi * TB + i * 128
            src = I["xp"] if l == 0 else x1p
            return src[base:base + 128, :]
        src = I["xs"] if l == 0 else x1s
        return src[i * 8:(i + 1) * 8, :]

    def x_dst(l, b, i):
        if b.kind == "p":
            base = b.bi * TB + i * 128
            dst = x1p if l == 0 else O["yp"]
            return dst[base:base + 128, :]
        dst = x1s if l == 0 else O["ys"]
        return dst[i * 8:(i + 1) * 8, :]

    def norm_T(b, x_sb, xkey, gain, gkey, xT, tkey, wk):
        L, nt = b.L, b.nt
        for i in range(nt):
            act(wk["junk"][:L, :], x_sb[:L, i, :], AF.Square, [(xkey, i)], ["xn", "nst"],
                accum_out=wk["st"][:L, 0:1])
            rsq(wk["st"][:L, 1:2], wk["st"][:L, 0:1], 1.0 / D, ["nst"], ["nst2"])
            stt("dve", wk["xn"][:L, :], x_sb[:L, i, :], wk["st"][:L, 1:2], gain[:L, :], ALU.mult, ALU.mult,
                [(xkey, i), "nst2", gkey], ["xn"])
            for h in range(2):
                ps, pk = nextB()
                for k in range(8):
                    kt = h * 8 + k
                    tr(ps[:, k * 128:k * 128 + L], wk["xn"][:L, kt * 128:(kt + 1) * 128], ident_b[:L, :L],
                       ["xn", "ident_b"], [pk])
                src = ps[:, :].rearrange("p (k c) -> p k c", c=128)[:, :, 0:L]
                e = "act" if h == 0 else "dve"
                cp(e, xT[:, h * 8:(h + 1) * 8, i * L:(i + 1) * L], src, [pk], [(tkey, i, h)])

    def xT_keys(tkey, b):
        return [(tkey, i, h) for i in range(b.nt) for h in range(2)]

    def fm_cols(b, panels, xT, tkey, mw, evac):
        rk = xT_keys(tkey, b)

        def consume(j, cp_):
            c0, p = panels[j]
            for m0 in range(0, p.pc, mw):
                ps, pk = nextF()
                for kt in range(p.kt):
                    mm(ps[:mw, :b.TT], p.view[:, kt, m0:m0 + mw], xT[:, kt, :b.TT], kt == 0, kt == p.kt - 1,
                       p.keys + rk, [pk])
                evac(c0 + m0, ps, pk)
        run_panels([p for _, p in panels], consume)

    def tm_cols(b, panels, xT, tkey, evac):
        def consume(j, cp_):
            c0, p = panels[j]
            for i in range(b.nt):
                ps, pk = nextF()
                for kt in range(p.kt):
                    mm(ps[:b.L, :p.pc], xT[:, kt, i * b.L:(i + 1) * b.L], p.view[:, kt, :], kt == 0, kt == p.kt - 1,
                       p.keys + [(tkey, i, 0), (tkey, i, 1)], [pk])
                evac(i, c0, p.pc, ps, pk)
        run_panels([p for _, p in panels], consume)

    def mixer_a(l, b, Pm, xT, mergedT, carry):
        L, nt, TT, nseq, Ls = b.L, b.nt, b.TT, b.nseq, b.Lseq
        with ExitStack() as ctx:
            ring_alloc(ctx)
            uA = sb(ctx, "uA", [128, 12, TT])
            ext = sb(ctx, "extA", [128, 4, nseq, 2 + Ls])
            acc = sb(ctx, "accA", [128, TT])
            sq = sb(ctx, "sqA", [128, 4, TT], BF16)
            ya = sb(ctx, "yA", [128, 4, TT])
            rstd = sb(ctx, "rstdA", [128, TT])
            wcol = I["w_in"][l][:, 0:COLS_A]

            def evac(c, ps, pk):
                m = c // 128
                cp("act", uA[:, m, :], ps[:, :TT], [pk], [("uA", m)])
            fm_cols(b, col_panels(wcol, 0, COLS_A, KT), xT, "xT", 128, evac)
            if b.kind == "p":
                if b.bi == 0:
                    ms("dve", ext[:, :, :, 0:2], 0.0, ["extA_pre"])
                else:
                    cp("dve", ext[:, :, 0, 0:2], carry["A"][:, :, :], ["carryA"], ["extA_pre"])
            else:
                for m in range(4):
                    for s_ in range(nseq):
                        dma("sp", ext[:, m, s_, 0:2],
                            I["sca"][l][s_][:, m * 128:(m + 1) * 128].rearrange("t c -> c t"),
                            [], [("extA_prem", m, s_)], allow_slow_non_contiguous=True)
            pre_keys = ["extA_pre"] + ([("extA_prem", m, s_) for m in range(4) for s_ in range(nseq)] if b.kind == "s" else [])
            for m in range(4):
                v3 = lambda ap: ap.rearrange("p (s t) -> p s t", t=Ls)
                tt("dve", ext[:, m, :, 2:2 + Ls], v3(uA[:, 4 + m, :]), v3(uA[:, 8 + m, :]), ALU.mult,
                   [("uA", 4 + m), ("uA", 8 + m)] + pre_keys, [("extA", m)])
                a3 = v3(acc[:, :])
                ts("dve", a3, ext[:, m, :, 0:Ls], Pm["wA"][:, m, 0:1], None, ALU.mult, None,
                   [("extA", m), "wA"] + pre_keys, ["accA"])
                stt("dve", a3, ext[:, m, :, 1:1 + Ls], Pm["wA"][:, m, 1:2], a3, ALU.mult, ALU.add,
                    [("extA", m), "accA"], ["accA"])
                stt("dve", a3, ext[:, m, :, 2:2 + Ls], Pm["wA"][:, m, 2:3], a3, ALU.mult, ALU.add,
                    [("extA", m), "accA"], ["accA"])
                tt("dve", ya[:, m, :], acc[:, :], uA[:, m, :], ALU.mult, ["accA", ("uA", m)], [("yA", m)])
                act(sq[:, m, :], ya[:, m, :], AF.Square, [("yA", m)], [("sqA", m)])
            if b.kind == "p":
                cp("dve", carry["A"][:, :, :], ext[:, :, 0, Ls:Ls + 2], [("extA", m) for m in range(4)], ["carryA"])
                if b.bi == cfg.NBLK - 1:
                    for m in range(4):
                        dma("sp", O["ncap"][l][:, m * 128:(m + 1) * 128].rearrange("t c -> c t"),
                            ext[:, m, 0, Ls:Ls + 2], [("extA", m)], [("o_ncap", l, m)], allow_slow_non_contiguous=True)
            else:
                for m in range(4):
                    for s_ in range(nseq):
                        dma("sp", O["ncas"][l][s_][:, m * 128:(m + 1) * 128].rearrange("t c -> c t"),
                            ext[:, m, s_, Ls:Ls + 2], [("extA", m)], [("o_ncas", l, m, s_)], allow_slow_non_contiguous=True)
            ps, pk = nextF()
            for m in range(4):
                mm(ps[:, :TT], ones_b[:, :], sq[:, m, :], m == 0, m == 3, [("sqA", m), "ones_b"], [pk])
            rsq(rstd[:, :], ps[:, :TT], 1.0 / D_CONV, [pk], ["rstdA"])
            for m in range(4):
                stt("dve", mergedT[:, m, :TT], ya[:, m, :], Pm["g_out_fm"][:, m:m + 1], rstd[:, :], ALU.mult, ALU.mult,
                    [("yA", m), "rstdA", "g_out_fm"], [("mT", m)])
            dbg("yA_%d_%s%d" % (l, b.kind, b.bi), ya[:], [("yA", m) for m in range(4)])
            barrier()

    def mixer_b(l, b, Pm, xT, mergedT, carry):
        L, nt, TT, nseq, Ls = b.L, b.nt, b.TT, b.nseq, b.Lseq
        with ExitStack() as ctx:
            ring_alloc(ctx)
            goutB = sb(ctx, "goutB", [128, D_SSM])
            dma("sp", goutB[:], I["norm_out"][l][512:1024].partition_broadcast(128), [], ["goutB"])
            z = sb(ctx, "zB", [128, nt, D_SSM])
            dtr = sb(ctx, "dtr", [128, nt, SSM_H])
            dtv = sb(ctx, "dtv", [128, nt, SSM_H])
            la = sb(ctx, "la", [128, nt, SSM_H])
            ext = sb(ctx, "extB", [128, 8, nseq, 3 + Ls])
            acc = sb(ctx, "accB", [128, TT])
            xcb = sb(ctx, "xcb", [128, 8, TT], BF16)
            BT = xcb[:, 4:6, :]
            CT = xcb[:, 6:8, :]
            lahl = sb(ctx, "lahl", [128, 2, SSM_H], BF16)
            la_bch = sb(ctx, "la_bch", [128, 2, SSM_H, 128], BF16)
            sthl = sb(ctx, "sthl", [128, 2, 128], BF16)
            xs_bf = sb(ctx, "xs_bf", [128, nt, D_SSM], BF16)
            Btok = sb(ctx, "Btok", [128, nt, 256], BF16)
            yb = sb(ctx, "yb", [128, nt, D_SSM])
            cum = sb(ctx, "cumB", [128, 16])
            sm = sb(ctx, "smB", [128, 5, SSM_H])
            decT = [sb(ctx, "decT%d" % k, [128, 128]) for k in range(2)]
            scT = sb(ctx, "scT", [128, SSM_H, 128], BF16)
            tmp = sb(ctx, "tmpB", [128, D_SSM])
            tmp2 = sb(ctx, "tmp2B", [128, D_SSM])
            xw = sb(ctx, "xwB", [128, D_SSM], BF16)
            ynb = sb(ctx, "ynb", [128, D_SSM], BF16)
            stio = sb(ctx, "stio", [128, 4, 128])
            nst = sb(ctx, "nstB", [128, 4])
            hT, hT_bf = carry["hT"], carry["hT_bf"]
            wbase = I["w_in"][l]
            v3 = lambda ap: ap.rearrange("p (s t) -> p s t", t=Ls)

            def ev_z(i, c0, pc, ps, pk):
                cp("act", z[:L, i, :], ps[:L, :pc], [pk], [("zB", i)])
            tm_cols(b, col_panels(wbase, OFF_B, OFF_B + 512, KT), xT, "xT", ev_z)

            def ev_x(c, ps, pk):
                m = (c - (OFF_B + 512)) // 128
                cp("act", ext[:, m, :, 3:3 + Ls], v3(ps[:, :TT]), [pk], [("extB", m)])
            fm_cols(b, col_panels(wbase, OFF_B + 512, OFF_B + 512 + XBC, KT), xT, "xT", 128, ev_x)

            def ev_dt(i, c0, pc, ps, pk):
                cp("act", dtr[:L, i, :], ps[:L, :pc], [pk], [("dtr", i)])
            tm_cols(b, col_panels(wbase, OFF_B + 512 + XBC, OFF_B + 512 + XBC + SSM_H, KT), xT, "xT", ev_dt)

            if CUT == 1:
                barrier()
                return
            if b.kind == "p":
                if b.bi == 0:
                    ms("dve", ext[:, :, :, 0:3], 0.0, ["extB_pre"])
                else:
                    cp("dve", ext[:, :, 0, 0:3], carry["B"][:, :, :], ["carryB"], ["extB_pre"])
                pre_keys = ["extB_pre"]
            else:
                pre_keys = []
                for m in range(8):
                    for s_ in range(nseq):
                        dma("sp", ext[:, m, s_, 0:3],
                            I["scb"][l][s_][:, m * 128:(m + 1) * 128].rearrange("t c -> c t"),
                            [], [("extB_prem", m, s_)], allow_slow_non_contiguous=True)
                        pre_keys.append(("extB_prem", m, s_))
            for m in range(8):
                a3 = v3(acc[:, :])
                ts("dve", a3, ext[:, m, :, 0:Ls], Pm["wB"][:, m, 0:1], None, ALU.mult, None,
                   [("extB", m), "wB"] + pre_keys, ["accB"])
                for k in range(1, 4):
                    stt("dve", a3, ext[:, m, :, k:k + Ls], Pm["wB"][:, m, k:k + 1], a3, ALU.mult, ALU.add,
                        [("extB", m), "accB"], ["accB"])
                act(xcb[:, m, :], acc[:, :], AF.Silu, ["accB", "bB"], [("xcb", m)], bias=Pm["bB"][:, m:m + 1])
            ek = [("extB", m) for m in range(8)]
            if b.kind == "p":
                cp("dve", carry["B"][:, :, :], ext[:, :, 0, Ls:Ls + 3], ek, ["carryB"])
                if b.bi == cfg.NBLK - 1:
                    for m in range(8):
                        dma("sp", O["ncbp"][l][:, m * 128:(m + 1) * 128].rearrange("t c -> c t"),
                            ext[:, m, 0, Ls:Ls + 3], ek, [("o_ncbp", l, m)], allow_slow_non_contiguous=True)
            else:
                for m in range(8):
                    for s_ in range(nseq):
                        dma("sp", O["ncbs"][l][s_][:, m * 128:(m + 1) * 128].rearrange("t c -> c t"),
                            ext[:, m, s_, Ls:Ls + 3], ek, [("o_ncbs", l, m, s_)], allow_slow_non_contiguous=True)
            if CUT == 2:
                barrier()
                return
            if CUT == 10:
                barrier()
                return
            for i in range(nt):
                if CUT == 11 and i == 1:
                    barrier()
                    return
                psb, pbk = nextB()
                for m in range(4):
                    tr(psb[:L, m * 128:(m + 1) * 128], xcb[:, m, i * L:(i + 1) * L], ident_b[:, :],
                       [("xcb", m), "ident_b"], [pbk])
                for m in range(2):
                    tr(psb[:L, 512 + m * 128:512 + (m + 1) * 128], BT[:, m, i * L:(i + 1) * L], ident_b[:, :],
                       [("xcb", 4 + m), "ident_b"], [pbk])
                cp("dve", xs_bf[:L, i, :], psb[:L, 0:512], [pbk], [("xs_bf", i)])
                cp("act", Btok[:L, i, :], psb[:L, 512:768], [pbk], [("Btok", i)])
            if CUT == 3:
                barrier()
                return
            for i in range(nt):
                tt("dve", dtv[:L, i, :], dtr[:L, i, :], Pm["dt_bias"][:L, :], ALU.add, [("dtr", i), "dt_bias"], [("dtv", i)])
                act(dtv[:L, i, :], dtv[:L, i, :], AF.Exp, [("dtv", i)], [("dtv", i)])
                act(dtv[:L, i, :], dtv[:L, i, :], AF.Ln, [("dtv", i)], [("dtv", i)], bias=1.0)
                tt("dve", la[:L, i, :], dtv[:L, i, :], Pm["negA"][:L, :], ALU.mult, [("dtv", i), "negA"], [("la", i)])

            if CUT == 4:
                barrier()
                return
            for i in range(nt):
                off = i * L
                first = (b.kind == "p" and b.bi == 0 and i == 0)
                if b.kind == "s":
                    dma("sp", stio[:, :, :], I["ssm"][l][i].rearrange("(q p) n -> p q n", p=128), [], ["stio"])
                    for q in range(4):
                        tr_hilo(hT[:, q * 128:(q + 1) * 128], stio[:, q, :], sthl, ["stio"], ["hT"])
                    cp("act", hT_bf[:, :], hT[:, :], ["hT"], ["hT_bf"])
                elif first:
                    ms("dve", hT[:, :], 0.0, ["hT"])
                    ms("dve", hT_bf[:, :], 0.0, ["hT_bf"])
                psc, pck = psF[2], "F2"
                cp("act", lahl[:L, 0, :], la[:L, i, :], [("la", i)], ["lahi"])
                tt("dve", lahl[:L, 1, :], la[:L, i, :], lahl[:L, 0, :], ALU.subtract, [("la", i), "lahi"], ["lalo"])
                for k in range(2):
                    mm(psc[:L, 0:8], tri_b[:L, :L], lahl[:L, k, :], k == 0, k == 1, ["tri_b", "lahi", "lalo"], [pck])
                for k in range(2):
                    mm(psc[:, 8:16], ones_b[:L, :], lahl[:L, k, :], k == 0, k == 1, ["ones_b", "lahi", "lalo"], [pck])
                cp("dve", cum[:, :], psc[:, 0:16], [pck], ["cumB"])
                negcum, expcum, w_, dec, t5 = (sm[:, k, :] for k in range(5))
                ts("dve", negcum[:L, :], cum[:L, 0:8], -1.0, None, ALU.mult, None, ["cumB"], ["negcum"])
                act(expcum[:L, :], cum[:L, 0:8], AF.Exp, ["cumB"], ["expcum"])
                act(dec[:, :], cum[:, 8:16], AF.Exp, ["cumB"], ["decB"])
                tt("dve", t5[:L, :], cum[:L, 8:16], cum[:L, 0:8], ALU.subtract, ["cumB"], ["t5B"])
                act(t5[:L, :], t5[:L, :], AF.Exp, ["t5B"], ["t5B"])
                tt("dve", w_[:L, :], t5[:L, :], dtv[:L, i, :], ALU.mult, ["t5B", ("dtv", i)], ["wB_"])
                for k in range(2):
                    cp("dve", la_bch[:L, k, :, :L], lahl[:L, k, :].unsqueeze(2).to_broadcast([L, SSM_H, L]),
                       ["lahi", "lalo"], [("la_bch", k)])
                if CUT == 5:
                    barrier()
                    return
                psg, pgk = psF[3], "F3"
                for g in range(2):
                    mm(psg[:L, g * 128:g * 128 + L], BT[:, g, off:off + L], CT[:, g, off:off + L], True, True,
                       [("xcb", 4 + g), ("xcb", 6 + g)], [pgk])
                for h in range(SSM_H):
                    g = h // 4
                    psr, prk = nextF()
                    for k in range(2):
                        mm(psr[:L, :L], la_bch[:L, k, h, :L], tri_b[:L, :L], k == 0, False, [("la_bch", k), "tri_b"], [prk])
                    mm(psr[:L, :L], ident_b[:L, :L], negtri_b[:L, :L], False, True, ["ident_b", "negtri_b"], [prk])
                    dT = decT[h % 2]
                    act(dT[:L, :L], psr[:L, :L], AF.Exp, [prk, "negcum"], [("decT", h % 2)], bias=negcum[:L, h:h + 1])
                    stt("dve", scT[:L, h, :L], psg[:L, g * 128:g * 128 + L], dtv[:L, i, h:h + 1], dT[:L, :L],
                        ALU.mult, ALU.mult, [pgk, ("dtv", i), ("decT", h % 2)], [("scT", h)])
                if CUT == 6:
                    barrier()
                    return
                psy, pyk = psF[4], "F4"
                psy2, py2k = psF[5], "F5"
                for h in range(SSM_H):
                    g = h // 4
                    hs = slice(h * HD, (h + 1) * HD)
                    mm(psy[:L, hs], scT[:L, h, :L], xs_bf[:L, i, hs], True, True, [("scT", h), ("xs_bf", i)], [pyk])
                    mm(psy2[:L, hs], CT[:, g, off:off + L], hT_bf[:, hs], True, True, [("xcb", 6 + g), "hT_bf"], [py2k])
                h3 = lambda ap: ap.rearrange("p (h d) -> p h d", d=HD)
                tt("dve", h3(tmp[:L, :]), h3(psy2[:L, :]), expcum[:L, :].unsqueeze(2).to_broadcast([L, SSM_H, HD]),
                   ALU.mult, [py2k, "expcum"], ["tmpB"])
                tt("dve", yb[:L, i, :], tmp[:L, :], psy[:L, :], ALU.add, ["tmpB", pyk], [("yb", i)])
                tt("dve", tmp2[:L, :], xs_bf[:L, i, :], Pm["dskip"][:L, :], ALU.mult, [("xs_bf", i), "dskip"], ["tmp2B"])
                tt("dve", yb[:L, i, :], yb[:L, i, :], tmp2[:L, :], ALU.add, [("yb", i), "tmp2B"], [("yb", i)])
                act(tmp[:L, :], z[:L, i, :], AF.Silu, [("zB", i)], ["tmpB"])
                tt("dve", yb[:L, i, :], yb[:L, i, :], tmp[:L, :], ALU.mult, [("yb", i), "tmpB"], [("yb", i)])
                if CUT == 7:
                    barrier()
                    return
                tt("dve", h3(xw[:L, :]), h3(xs_bf[:L, i, :]), w_[:L, :].unsqueeze(2).to_broadcast([L, SSM_H, HD]),
                   ALU.mult, [("xs_bf", i), "wB_"], ["xwB"])
                psc2, pc2k = psF[2], "F2"
                for g in range(2):
                    mm(psc2[:, g * 256:(g + 1) * 256], Btok[:L, i, g * 128:(g + 1) * 128], xw[:L, g * 256:(g + 1) * 256],
                       True, True, [("Btok", i), "xwB"], [pc2k])
                tt("dve", h3(hT[:, :]), h3(hT[:, :]), dec[:, :].unsqueeze(2).to_broadcast([128, SSM_H, HD]), ALU.mult,
                   ["hT", "decB"], ["hT"])
                tt("dve", hT[:, :], hT[:, :], psc2[:, :], ALU.add, ["hT", pc2k], ["hT"])
                cp("act", hT_bf[:, :], hT[:, :], ["hT"], ["hT_bf"])
                if CUT == 8:
                    barrier()
                    return
                last = (b.kind == "p" and b.bi == cfg.NBLK - 1 and i == nt - 1)
                if b.kind == "s" or last:
                    for q in range(4):
                        tr_hilo(stio[:, q, :], hT[:, q * 128:(q + 1) * 128], sthl, ["hT"], ["stio"])
                    dst = O["nsss"][l][i] if b.kind == "s" else O["nssp"][l]
                    dma("sp", dst.rearrange("(q p) n -> p q n", p=128), stio[:, :, :], ["stio"], [("o_nss", l, b.kind, i)])
                if CUT == 9:
                    barrier()
                    return
                act(tmp2[:L, :], yb[:L, i, :], AF.Square, [("yb", i)], ["tmp2B", "nstB"], accum_out=nst[:L, 0:1])
                rsq(nst[:L, 1:2], nst[:L, 0:1], 1.0 / D_SSM, ["nstB"], ["nstB2"])
                stt("dve", ynb[:L, :], yb[:L, i, :], nst[:L, 1:2], goutB[:L, :], ALU.mult, ALU.mult,
                    [("yb", i), "nstB2", "goutB"], ["ynb"])
                psb, pbk = nextB()
                for m in range(4):
                    tr(psb[:, m * 128:m * 128 + L], ynb[:L, m * 128:(m + 1) * 128], ident_b[:L, :L], ["ynb", "ident_b"], [pbk])
                cp("act", mergedT[:, 4:8, off:off + L], psb[:, 0:512].rearrange("p (k c) -> p k c", c=128)[:, :, 0:L],
                   [pbk], [("mT", 4 + m) for m in range(4)])
            dbg("yb_%d_%s%d" % (l, b.kind, b.bi), yb[:L], [("yb", i) for i in range(nt)])
            barrier()

    def nsa_consts(ctx, dup, ncols_ee):
        C = {}
        C["ee"] = sb(ctx, "ee", [128, ncols_ee], BF16)
        dma("pool", C["ee"][:], I["c_eed" if dup else "c_ee"][:, 0:ncols_ee], [], ["ee"])
        C["dup"] = dup
        C["sels"] = sb(ctx, "sels", list(hc["c_sels"].shape), BF16)
        dma("pool", C["sels"][:], I["c_sels"], [], ["sels"])
        C["bd"] = sb(ctx, "bd", [128, 128], BF16)
        ms("dve", C["bd"][:], 0.0, ["bd"])
        ms("dve", C["bd"][0:64, 0:64], 1.0, ["bd"])
        ms("dve", C["bd"][64:128, 64:128], 1.0, ["bd"])
        return C

    def nsa_layer_params(ctx, l, Pm):
        Pm["qg2"] = sb(ctx, "qg2", [128, 1])
        Pm["kg02"] = sb(ctx, "kg02", [128, 1])
        for hb in (0, 64):
            dma("sp", Pm["qg2"][hb:hb + 64, :], I["q_norm"][l].rearrange("(d o) -> d o", o=1), [], ["qg2"],
                allow_slow_non_contiguous=True)
            dma("sp", Pm["kg02"][hb:hb + 64, :], I["k_norm"][l][0].rearrange("(d o) -> d o", o=1), [], ["kg02"],
                allow_slow_non_contiguous=True)
        ts("dve", Pm["qg2"][:], Pm["qg2"][:], 0.125, None, ALU.mult, None, ["qg2"], ["qg2"])
        Pm["w2k"] = sb(ctx, "w2k", [128, 128], BF16)
        for hb in (0, 64):
            dma("pool", Pm["w2k"][:, hb:hb + 64], I["cmp_w2"][l][0], [], ["w2k"])

    def cmp_topbot(Pm, src, g, nch, kvi, rkeys):
        ps, pk = nextF()
        for half in range(2):
            for j in range(16):
                mm(ps[:, half * 256:half * 256 + nch], Pm["w1"][:, kvi, half * 16 + j, :],
                   src[0:64, g, j:j + 16 * (nch - 1) + 1:16], j == 0, j == 15, [("w1", kvi)] + rkeys, [pk])
        return ps, pk

    def gelu_to(out_bf, x, n, wkk, R, W):
        t = wkk
        tt("dve", t[:, :n], x, x, ALU.mult, R, ["gl_t"])
        ts("dve", t[:, :n], t[:, :n], 0.044715, 1.0, ALU.mult, ALU.add, ["gl_t"], ["gl_t"])
        tt("dve", t[:, :n], t[:, :n], x, ALU.mult, R + ["gl_t"], ["gl_t"])
        act(t[:, :n], t[:, :n], AF.Sigmoid, ["gl_t"], ["gl_t"], scale=1.5957691216057308)
        tt("dve", out_bf, t[:, :n], x, ALU.mult, R + ["gl_t"], W)

    def cmp_k_finish(Pm, C, Gk, n, dst, wk, W):
        ps, pk = nextF()
        mm(ps[:, :n], Pm["w2k"][:, :], Gk, True, True, ["w2k", "Gk"], [pk])
        cp("act", wk["kc"][:, :n], ps[:, :n], [pk], ["kc"])
        act(wk["kcsq"][:, :n], ps[:, :n], AF.Square, [pk], ["kcsq"])
        ps2, pk2 = nextF()
        mm(ps2[:, :n], C["bd"][:, :], wk["kcsq"][:, :n], True, True, ["bd", "kcsq"], [pk2])
        rsq(wk["kcr"][:, :n], ps2[:, :n], 1.0 / HD, [pk2], ["kcr"])
        stt("dve", dst, wk["kc"][:, :n], Pm["kg02"][:, 0:1], wk["kcr"][:, :n], ALU.mult, ALU.mult,
            ["kc", "kcr", "kg02"], W)

    def attend(C, g, L, qrhs, qkeys, tiles, ps_o, okey, wk, imp=None):
        n = len(tiles)
        L4 = 4 * L
        for ti, t in enumerate(tiles):
            nk = t["nk"]
            ps, pk = nextF()
            mm(ps[:nk, :L4], t["KT"], qrhs, True, t.get("eblk") is None, t["kkeys"] + qkeys, [pk])
            if t.get("eblk") is not None:
                nj = t["nj"]
                eb = t["hb"] if C["dup"] else 0
                mm(ps[:nk, :L4], C["ee"][eb:eb + nj, t["eblk"] * 128:t["eblk"] * 128 + nk], wk["NBT"][eb:eb + nj, 0:L4],
                   False, True, ["ee", "NBT"], [pk])
            sS = wk["sS"][ti % 2]
            v3 = lambda ap: ap.rearrange("p (r q) -> p r q", q=L)
            stt("dve", v3(sS[:nk, :L4]), slope[:nk, 4 * g:4 * g + 4].unsqueeze(2).to_broadcast([nk, 4, L]),
                t["kcol"][:nk, :] if "kcol" in t else kp[:nk, t["delta"]:t["delta"] + 1], v3(ps[:nk, :L4]), ALU.mult, ALU.add,
                [pk, "slope", "kp", "cmpcol"], [("sS", ti % 2)])
            m = t.get("mask")
            if m == "causal" or m == "win":
                mt = maskC if m == "causal" else maskW
                tt("dve", v3(sS[:nk, :L4]), v3(sS[:nk, :L4]), mt[:nk, 0:L].unsqueeze(1).to_broadcast([nk, 4, L]), ALU.add,
                   [("sS", ti % 2), "maskCW"], [("sS", ti % 2)])
            elif m is not None:
                base, cm, qs = m
                T.op("pool", [("sS", ti % 2)], [("sS", ti % 2)],
                     lambda: G_.affine_select(out=sS[:nk, :L4], in_=sS[:nk, :L4], pattern=[[0, 4], [qs, L]],
                                              compare_op=ALU.is_ge, fill=fill_reg, base=base, channel_multiplier=cm))
            PT = wk["PT"][ti % 2]
            act(PT[:nk, :L4], sS[:nk, :L4], AF.Exp, [("sS", ti % 2)], [("PT", ti % 2)])
            for r in range(4):
                mm(ps_o[:L, r * 65:(r + 1) * 65], PT[:nk, r * L:(r + 1) * L], t["V"], ti == 0 and r == 0,
                   ti == n - 1 and r == 3, [("PT", ti % 2)] + t["vkeys"], [okey])
            if imp is not None:
                ps_i, ikey, nj = imp
                for r in range(4):
                    mm(ps_i[:L, r * nj:(r + 1) * nj], PT[:nk, r * L:(r + 1) * L], C["sels"][:nk, t["ct"], 0:nj],
                       ti == 0 and r == 0, ti == n - 1 and r == 3, [("PT", ti % 2), "sels"], [ikey])

    PGG = min(8, NPG)

    def prompt_res(ctx):
        ncolp = ((SEQ // 16 + 127) // 128) * 128
        r = {}
        r["KsT"] = sb(ctx, "rKsT", [128, 2, SEQ], BF16)
        r["Vs"] = sb(ctx, "rVs", [128, cfg.NTILE, 4, 65], BF16)
        r["KwT"] = sb(ctx, "rKwT", [128, 2, 8 * 128], BF16)
        r["Vw"] = sb(ctx, "rVw", [128, 8, 4, 65], BF16)
        r["KcT"] = sb(ctx, "rKcT", [128, 4, ncolp], BF16)
        r["Gv"] = sb(ctx, "rGv", [128, 4, ncolp], BF16)
        r["Vc"] = sb(ctx, "rVc", [128, ncolp // 128, 4, 65], BF16)
        r["ctop"] = sb(ctx, "rctop", [128, 2, 4])
        ms("dve", r["Vs"][:], 1.0, ["rinit"])
        ms("dve", r["Vw"][:], 1.0, ["rinit"])
        ms("dve", r["Vc"][:], 1.0, ["rinit"])
        ms("dve", r["Gv"][:], 0.0, ["rinit"])
        ms("dve", r["KcT"][:], 0.0, ["rinit"])
        ms("dve", r["ctop"][:], 0.0, ["rinit"])
        barrier()
        return r

    def sample_bufs(ctx):
        ncols = ((cfg.NCMP_S + 127) // 128) * 128
        S = {}
        S["KsT"] = sb(ctx, "sKsT", [128, 2, cfg.PAST + 8], BF16)
        S["Vs"] = sb(ctx, "sVs", [128, NPG + 1, 4, 65], BF16)
        S["KwT"] = sb(ctx, "sKwT", [128, 2, WIN + 8], BF16)
        S["Vw"] = sb(ctx, "sVw", [128, 5, 4, 65], BF16)
        S["KcT"] = sb(ctx, "sKcT", [128, 4, ncols], BF16)
        S["Gv"] = sb(ctx, "sGv", [128, 4, ncols], BF16)
        S["Vc"] = sb(ctx, "sVc", [128, ncols // 128, 4, 65], BF16)
        S["stage"] = sb(ctx, "sstage", [128, PGG, 512], BF16)
        S["XT"] = sb(ctx, "sXT", [64, 4, PGG * 128], BF16)
        S["wst"] = sb(ctx, "swst", [128, 4, 256], BF16)
        S["pti"] = sb(ctx, "spti", [128, NPG], I32)
        S["ptf"] = sb(ctx, "sptf", [128, NPG])
        S["idx"] = sb(ctx, "sidx", [128, NPG], I32)
        S["pcol"] = sb(ctx, "spcol", [128, 1])
        S["ctop"] = sb(ctx, "sctop", [128, 4])
        S["topS"] = sb(ctx, "stopS", [128, PGG * 8 + 1])
        ms("dve", S["Vs"][:], 1.0, ["sinit"])
        ms("dve", S["Vw"][:], 1.0, ["sinit"])
        ms("dve", S["Vc"][:], 1.0, ["sinit"])
        ms("dve", S["Gv"][:], 0.0, ["sinit"])
        ms("dve", S["KcT"][:], 0.0, ["sinit"])
        ts("dve", S["pcol"][:], kp[:, 0:1], 64.0, None, ALU.add, None, ["kp"], ["sinit"])
        barrier()
        return S

    def sample_seq_prepare(l, b, s_, Pm, C, S, kv, kst, wk, glt, hsum, Gk):
        L = b.L
        nch = PGG * 8
        dma("sp", S["pti"][:], I["pt"][s_].partition_broadcast(128), [], ["pti"])
        cp("dve", S["ptf"][:], S["pti"][:], ["pti"], ["ptf"])
        stt("dve", S["ptf"][:], S["ptf"][:], 128.0, S["pcol"][:, 0:1].to_broadcast([128, NPG]), ALU.mult, ALU.add,
            ["ptf"], ["ptf"])
        if l > 0:
            ts("dve", S["ptf"][:], S["ptf"][:], float(l * NPOOL * 128), None, ALU.add, None, ["ptf"], ["ptf"])
        cp("dve", S["idx"][:], S["ptf"][:], ["ptf"], ["idx"])
        for kvi, cache in ((0, I["ck"]), (1, I["cv"])):
            ms("dve", S["ctop"][:], 0.0, ["sctop"])
            for grp in range(NPG // PGG):
                for j in range(PGG):
                    pg = grp * PGG + j
                    T.dma("pool", S["stage"][:, j, :], cache.rearrange("l r c -> (l r) c"), ["idx"], [("stage", j)],
                          indirect=bass.IndirectOffsetOnAxis(ap=S["idx"][:, pg:pg + 1], axis=0))
                for j in range(PGG):
                    pg = grp * PGG + j
                    psb, pbk = nextB()
                    for g in range(4):
                        tr(psb[:64, g * 128:(g + 1) * 128], S["stage"][:, j, g * 64:(g + 1) * 64], ident_b[:, :],
                           [("stage", j), "ident_b"], [pbk])
                    if kvi == 0:
                        for gam in range(2):
                            tr(psb[:, 512 + gam * 128:512 + (gam + 1) * 128], S["stage"][:, j, 256 + gam * 128:256 + (gam + 1) * 128],
                               ident_b[:, :], [("stage", j), "ident_b"], [pbk])
                    cp("act", S["XT"][:, :, j * 128:(j + 1) * 128], psb[:64, 0:512].rearrange("p (g c) -> p g c", c=128),
                       [pbk], [("sXT", j)])
                    if kvi == 0:
                        cp("dve", S["KsT"][:, :, pg * 128:(pg + 1) * 128], psb[:, 512:768].rearrange("p (g c) -> p g c", c=128),
                           [pbk], ["sKsT"])
                    else:
                        cp("dve", S["Vs"][:, pg, :, 0:64], S["stage"][:, j, 256:512].rearrange("p (g d) -> p g d", d=HD),
                           [("stage", j)], ["sVs"])
                xk = [("sXT", j) for j in range(PGG)]
                col0 = grp * nch - 1
                lo = 1 if grp == 0 else 0
                for g in range(4):
                    ps, pk = cmp_topbot(Pm, S["XT"], g, nch, kvi, xk)
                    cp("dve", S["topS"][:, 0:1], S["ctop"][:, g:g + 1], ["sctop"], ["stopS"])
                    cp("act", S["topS"][:, 1:nch + 1], ps[:, 0:nch], [pk], ["stopS"])
                    cp("dve", S["ctop"][:, g:g + 1], S["topS"][:, nch:nch + 1], ["stopS"], ["sctop"])
                    stt("dve", hsum[:, 0:nch], S["topS"][:, 0:nch], Pm["posb"][:, kvi:kvi + 1], ps[:, 256:256 + nch], ALU.add, ALU.add,
                        ["stopS", pk, ("posb", kvi)], ["hsum"])
                    if kvi == 0:
                        gelu_to(Gk[:, 0:nch], hsum[:, 0:nch], nch, glt, ["hsum"], ["Gk"])
                        cmp_k_finish(Pm, C, Gk[:, lo:nch], nch - lo, S["KcT"][:, g, col0 + lo:col0 + nch], wk, [("sKcT", g)])
                    else:
                        gelu_to(S["Gv"][:, g, col0 + lo:col0 + nch], hsum[:, lo:nch], nch - lo, glt, ["hsum"], [("sGv", g)])
            if kvi == 1:
                for g in range(4):
                    for ct in range((cfg.NCMP_S + 127) // 128):
                        ps2, pk2 = nextF()
                        mm(ps2[:, 0:64], S["Gv"][:, g, ct * 128:(ct + 1) * 128], Pm["w2"][:, 1, :], True, True,
                           [("sGv", g), "w2"], [pk2])
                        cp("act", S["Vc"][:, ct, g, 0:64], ps2[:, 0:64], [pk2], [("sVc", ct, g)])
        kvk = [("kv", s_, c) for c in range(3)]
        psb, pbk = nextB()
        for gam in range(2):
            tr(psb[:, gam * 128:gam * 128 + L], kst[:L, 512 + gam * 128:512 + (gam + 1) * 128], ident_b[:L, :L], ["kst", "ident_b"], [pbk])
            tr(psb[:, 256 + gam * 128:256 + gam * 128 + L], kst[:L, 768 + gam * 128:768 + (gam + 1) * 128], ident_b[:L, :L],
               ["kst", "ident_b"], [pbk])
        v3 = psb[:, 0:512].rearrange("p (k c) -> p k c", c=128)
        cp("act", S["KsT"][:, :, cfg.PAST:cfg.PAST + L], v3[:, 0:2, 0:L], [pbk], ["sKsT"])
        cp("act", S["KwT"][:, :, WIN:WIN + L], v3[:, 2:4, 0:L], [pbk], ["sKwT"])
        cp("dve", S["Vs"][:L, NPG, :, 0:64], kv[:L, s_, 768:1024].rearrange("p (g d) -> p g d", d=HD), kvk, ["sVs"])
        cp("dve", S["Vw"][:L, 4, :, 0:64], kv[:L, s_, 1280:1536].rearrange("p (g d) -> p g d", d=HD), kvk, ["sVw"])
        dma("pool", S["wst"][:], I["cwk"][l][s_].rearrange("(t p) c -> p t c", p=128), [], ["wst"])
        psb, pbk = nextB()
        for t in range(4):
            for gam in range(2):
                tr(psb[:, (t * 2 + gam) * 128:(t * 2 + gam + 1) * 128], S["wst"][:, t, gam * 128:(gam + 1) * 128], ident_b[:, :],
                   ["wst", "ident_b"], [pbk])
        p4 = psb[:, :].rearrange("p (t g c) -> p t g c", g=2, c=128)
        for gam in range(2):
            cp("act", S["KwT"][:, gam, 0:WIN].rearrange("p (t c) -> p t c", c=128), p4[:, :, gam, :], [pbk], ["sKwT"])
        dma("pool", S["wst"][:], I["cwv"][l][s_].rearrange("(t p) c -> p t c", p=128), ["wst"], ["wst"])
        for t in range(4):
            cp("dve", S["Vw"][:, t, :, 0:64], S["wst"][:, t, :].rearrange("p (g d) -> p g d", d=HD), ["wst"], ["sVw"])

    def dense_tail(l, b, Pm, xT, mergedT):
        L, nt, TT = b.L, b.nt, b.TT
        with ExitStack() as ctx:
            ring_alloc(ctx)
            x_sb = sb(ctx, "x_sb2", [128, nt, D])
            gff = sb(ctx, "gff", [128, D])
            xn = sb(ctx, "xn2", [128, D], BF16)
            st = sb(ctx, "nst2", [128, 8])
            actT = sb(ctx, "actT", [128, FFT // 2, TT], BF16)
            sg = sb(ctx, "sgF", [128, TT])
            dma("sp", gff[:], I["norm_ffn"][l].partition_broadcast(128), [], ["gff"])
            for i in range(nt):
                dma("sp", x_sb[:L, i, :], x_src(l, b, i), [], [("x2", i)])

            def ev_o(i, c0, pc, ps, pk):
                tt("dve", x_sb[:L, i, c0:c0 + pc], x_sb[:L, i, c0:c0 + pc], ps[:L, :pc], ALU.add, [pk, ("x2", i)], [("x2", i)])
            tm_cols(b, col_panels(I["w_out"][l], 0, D, KT), mergedT, "mTx", ev_o)
            dbg("xmid_%d_%s%d" % (l, b.kind, b.bi), x_sb[:L], [("x2", i) for i in range(nt)])
            norm_T(b, x_sb, "x2", gff, "gff", xT, "hT", {"junk": xn, "xn": xn, "st": st})
            hk = xT_keys("hT", b)
            half_ff = D_FF // 2
            for half in range(2):
                f0 = half * half_ff
                plist = []
                for c in range(f0, f0 + half_ff, 256):
                    plist.append((c, Panel(None, KT, 512, srcs=[I["w_gate"][l][:, c:c + 256], I["w_up"][l][:, c:c + 256]])))

                def consume(j, p):
                    c = plist[j][0]
                    for m0 in range(0, 256, 128):
                        mi = (c - f0 + m0) // 128
                        psg, pgk = psF[2 + (mi % 2) * 2], "F%d" % (2 + (mi % 2) * 2)
                        psu, puk = psF[3 + (mi % 2) * 2], "F%d" % (3 + (mi % 2) * 2)
                        for kt in range(KT):
                            mm(psg[:, :TT], p.view[:, kt, m0:m0 + 128], xT[:, kt, :TT], kt == 0, kt == KT - 1, p.keys + hk, [pgk])
                        for kt in range(KT):
                            mm(psu[:, :TT], p.view[:, kt, 256 + m0:256 + m0 + 128], xT[:, kt, :TT], kt == 0, kt == KT - 1, p.keys + hk, [puk])
                        act(sg[:, :], psg[:, :TT], AF.Silu, [pgk], ["sgF"])
                        tt("dve", actT[:, mi, :], sg[:, :], psu[:, :TT], ALU.mult, ["sgF", puk], [("actT", mi)])
                run_panels([p for _, p in plist], consume)
                ak = [("actT", m) for m in range(FFT // 2)]
                dpan = []
                for c in range(0, D, 256):
                    dpan.append((c, Panel(I["w_down"][l][f0:f0 + half_ff, c:c + 256], FFT // 2, 256)))

                def consume_d(j, p):
                    c0 = dpan[j][0]
                    for i in range(nt):
                        ps, pk = nextF()
                        for k in range(FFT // 2):
                            mm(ps[:L, :256], actT[:, k, i * L:(i + 1) * L], p.view[:, k, :], k == 0, k == FFT // 2 - 1, p.keys + ak, [pk])
                        tt("dve", x_sb[:L, i, c0:c0 + 256], x_sb[:L, i, c0:c0 + 256], ps[:L, :256], ALU.add, [pk, ("x2", i)], [("x2", i)])
                run_panels([p for _, p in dpan], consume_d)
            dbg("xout_%d_%s%d" % (l, b.kind, b.bi), x_sb[:L], [("x2", i) for i in range(nt)])
            for i in range(nt):
                dma("sp", x_dst(l, b, i), x_sb[:L, i, :], [("x2", i)], [("o_x", l, b.kind, b.bi, i)])
            barrier()

    def mixer_c(l, b, Pm, C, xT, mergedT, res):
        L, nt, TT = b.L, b.nt, b.TT
        prompt = b.kind == "p"
        with ExitStack() as ctx:
            QT = sb(ctx, "QT", [128, 8, TT], BF16)
            kv = sb(ctx, "kvC", [128, nt, KV_COLS])
            gs = sb(ctx, "gsC", [128, nt, 48])
            qf = sb(ctx, "qf", [128, TT])
            qsq = sb(ctx, "qsq", [128, TT], BF16)
            qr = sb(ctx, "qr", [128, TT])
            gout = sb(ctx, "goutC", [128, D_ATTN])
            dma("sp", gout[:], I["norm_out"][l][1024:2048].partition_broadcast(128), [], ["goutC"])
            wbase = I["w_in"][l]

            rctx = ExitStack()
            ring_alloc(rctx)
            qpan = []
            for gam_ in range(2):
                srcs_ = [wbase[:, OFF_C + (gam_ * 8 + gp_ * 4 + r_) * HD:OFF_C + (gam_ * 8 + gp_ * 4 + r_ + 1) * HD]
                         for r_ in range(4) for gp_ in range(2)]
                qpan.append((gam_, Panel(None, KT, 512, srcs=srcs_)))
            rk = xT_keys("xT", b)

            def q_consume(j, p):
                gam = j
                for r in range(4):
                    ps, pk = nextF()
                    for kt in range(KT):
                        mm(ps[:, :TT], p.view[:, kt, r * 128:(r + 1) * 128], xT[:, kt, :TT], kt == 0, kt == KT - 1, p.keys + rk, [pk])
                    cp("act", qf[:, :], ps[:, :TT], [pk], ["qf"])
                    act(qsq[:, :], ps[:, :TT], AF.Square, [pk], ["qsq"])
                    ps2, pk2 = nextF()
                    mm(ps2[:, :TT], C["bd"][:, :], qsq[:, :], True, True, ["bd", "qsq"], [pk2])
                    rsq(qr[:, :], ps2[:, :TT], 1.0 / HD, [pk2], ["qr"])
                    stt("dve", QT[:, gam * 4 + r, :], qf[:, :], Pm["qg2"][:, 0:1], qr[:, :], ALU.mult, ALU.mult,
                        ["qf", "qr", "qg2"], [("QT", gam * 4 + r)])
            run_panels([p for _, p in qpan], lambda j, p: q_consume(j, p))

            if CUTK(b) == 20:
                barrier()
                rctx.close()
                return
            def ev_kv(i, c0, pc, ps, pk):
                o = c0 - (OFF_C + D_ATTN)
                cp("act", kv[:L, i, o:o + pc], ps[:L, :pc], [pk], [("kv", i, o // 512)])
            tm_cols(b, col_panels(wbase, OFF_C + D_ATTN, OFF_C + D_ATTN + KV_COLS, KT), xT, "xT", ev_kv)

            def ev_g(i, c0, pc, ps, pk):
                act(gs[:L, i, :], ps[:L, :pc], AF.Sigmoid, [pk], [("gs", i)])
            tm_cols(b, col_panels(wbase, OFF_C + D_ATTN + KV_COLS, OFF_C + COLS_C, KT), xT, "xT", ev_g)
            barrier()
            rctx.close()

            if CUTK(b) == 21:
                barrier()
                return
            kst = sb(ctx, "kst", [128, 1024], BF16)
            ksq = sb(ctx, "ksq", [128, 256])
            kss = sb(ctx, "kss", [128, 8])
            wk = {"sS": [sb(ctx, "sS%d" % k, [128, 512]) for k in range(2)],
                  "PT": [sb(ctx, "PT%d" % k, [128, 512], BF16) for k in range(2)],
                  "NBT": sb(ctx, "NBT", [128, 512], BF16),
                  "kc": sb(ctx, "kc", [128, 64]), "kcsq": sb(ctx, "kcsq", [128, 64], BF16),
                  "kcr": sb(ctx, "kcr", [128, 64])}
            glt = sb(ctx, "glt", [128, 64])
            hsum = sb(ctx, "hsum", [128, 64])
            Gk = sb(ctx, "Gk", [128, 64], BF16)
            yc = sb(ctx, "ycC", [128, D_ATTN])
            ycn = sb(ctx, "ycn", [128, D_ATTN], BF16)
            sm = sb(ctx, "smC", [128, 64])
            impS = sb(ctx, "impS", [128, 3, 128])
            mx = sb(ctx, "mxC", [128, 16])
            NBb = sb(ctx, "NBb", [128, 128], BF16)
            FB = sb(ctx, "FB", [128, 128])
            ms("dve", NBb[:], 0.0, ["NBb"])
            dbgt = sb(ctx, "dbgt", [128, 3, 260]) if any(n_.startswith("att_") for n_ in dbg_names) else None
            cmpcol = sb(ctx, "cmpcol", [128, 8])
            t1 = sb(ctx, "t1C", [128, 256])
            t2 = sb(ctx, "t2C", [128, 256])
            if not prompt:
                S = sample_bufs(ctx)
            if prompt:
                XkT = sb(ctx, "XkT", [64, 4, TT], BF16)
                XvT = sb(ctx, "XvT", [64, 4, TT], BF16)
                topS = sb(ctx, "topS", [128, 33])

            for i in range(nt):
                kvk = [("kv", i, c) for c in range(3)]
                gi = b.bi * 4 + i if prompt else None
                for br, c0 in ((1, 512), (2, 1024)):
                    kview = kv[:L, i, c0:c0 + 256]
                    k3 = kview.rearrange("p (g d) -> p g d", d=HD)
                    tt("dve", ksq[:L, :], kview, kview, ALU.mult, kvk, ["ksq"])
                    T.op("dve", ["ksq"], ["kss"], lambda: V.tensor_reduce(
                        out=kss[:L, 0:4], in_=ksq[:L, :].rearrange("p (g d) -> p g d", d=HD), axis=AX.X, op=ALU.add))
                    rsq(kss[:L, 4:8], kss[:L, 0:4], 1.0 / HD, ["kss"], ["kss2"])
                    tt("dve", k3, k3, kss[:L, 4:8].unsqueeze(2).to_broadcast([L, 4, HD]), ALU.mult, kvk + ["kss2"], kvk)
                    tt("dve", k3, k3, Pm["kg"][:L, br, :].unsqueeze(1).to_broadcast([L, 4, HD]), ALU.mult, kvk + ["kg"], kvk)
                if prompt:
                    t0 = b.bi * TB + i * 128
                    nkd, nvd = O["nkp"][l][t0:t0 + 128], O["nvp"][l][t0:t0 + 128]
                else:
                    nkd, nvd = O["nks"][l][i * 8:(i + 1) * 8], O["nvs"][l][i * 8:(i + 1) * 8]
                for (dst, dc, sc) in ((nkd, 0, 0), (nkd, 256, 512), (nvd, 0, 256), (nvd, 256, 768)):
                    dma("sp", dst[:, dc:dc + 256], kv[:L, i, sc:sc + 256], kvk, [("o_kv", l, b.kind, b.bi, i, dc, sc)])
                if prompt:
                    if t0 >= SEQ - WIN:
                        w0 = t0 - (SEQ - WIN)
                        dma("sp", O["nwkp"][l][w0:w0 + 128], kv[:L, i, 1024:1280], kvk, [("o_wk", l, i, b.bi)])
                        dma("sp", O["nwvp"][l][w0:w0 + 128], kv[:L, i, 1280:1536], kvk, [("o_wv", l, i, b.bi)])
                else:
                    dma("sp", O["nwks"][l][i][0:WIN - 8], I["cwk"][l][i][8:WIN], [], [("o_wk1", l, i)])
                    dma("sp", O["nwvs"][l][i][0:WIN - 8], I["cwv"][l][i][8:WIN], [], [("o_wv1", l, i)])
                    dma("sp", O["nwks"][l][i][WIN - 8:WIN], kv[:L, i, 1024:1280], kvk, [("o_wk2", l, i)])
                    dma("sp", O["nwvs"][l][i][WIN - 8:WIN], kv[:L, i, 1280:1536], kvk, [("o_wv2", l, i)])
                cp("act", kst[:L, 0:768], kv[:L, i, 0:768], kvk, ["kst"])
                cp("act", kst[:L, 768:1024], kv[:L, i, 1024:1280], kvk, ["kst"])
                if prompt:
                    slot = gi % 8
                    psb, pbk = nextB()
                    for gam in range(2):
                        tr(psb[:, gam * 128:gam * 128 + L], kst[:L, 512 + gam * 128:512 + (gam + 1) * 128], ident_b[:L, :L],
                           ["kst", "ident_b"], [pbk])
                        tr(psb[:, 256 + gam * 128:256 + gam * 128 + L], kst[:L, 768 + gam * 128:768 + (gam + 1) * 128],
                           ident_b[:L, :L], ["kst", "ident_b"], [pbk])
                    v3 = psb[:, 0:512].rearrange("p (k c) -> p k c", c=128)
                    cp("act", res["KsT"][:, :, gi * 128:gi * 128 + L], v3[:, 0:2, 0:L], [pbk], [("KsT", gi)])
                    cp("dve", res["KwT"][:, :, slot * 128:slot * 128 + L], v3[:, 2:4, 0:L], [pbk], [("KwT", slot)])
                    psb, pbk = nextB()
                    for g in range(4):
                        tr(psb[:64, g * 128:g * 128 + L], kst[:L, g * 64:(g + 1) * 64], ident_b[:L, :L], ["kst", "ident_b"], [pbk])
                        tr(psb[:64, 512 + g * 128:512 + g * 128 + L], kst[:L, 256 + g * 64:256 + (g + 1) * 64], ident_b[:L, :L],
                           ["kst", "ident_b"], [pbk])
                    v3 = psb[:64, :].rearrange("p (k c) -> p k c", c=128)
                    cp("act", XkT[:, :, i * 128:(i + 1) * 128], v3[:, 0:4, :], [pbk], [("XkT", i)])
                    cp("dve", XvT[:, :, i * 128:(i + 1) * 128], v3[:, 4:8, :], [pbk], [("XvT", i)])
                    cp("act", res["Vs"][:L, gi, :, 0:64], kv[:L, i, 768:1024].rearrange("p (g d) -> p g d", d=HD), kvk, [("Vs", gi)])
                    cp("dve", res["Vw"][:L, slot, :, 0:64], kv[:L, i, 1280:1536].rearrange("p (g d) -> p g d", d=HD), kvk, [("Vw", slot)])

            if CUTK(b) == 22:
                barrier()
                return
            if prompt:
                xk = [("XkT", i) for i in range(nt)]
                xv = [("XvT", i) for i in range(nt)]
                col0 = 32 * b.bi - 1
                lo = 1 if b.bi == 0 else 0
                for kvi, src, sk in ((0, XkT, xk), (1, XvT, xv)):
                    for g in range(4):
                        ps, pk = cmp_topbot(Pm, src, g, 32, kvi, sk)
                        cp("dve", topS[:, 0:1], res["ctop"][:, kvi, g:g + 1], ["ctop"], ["topS"])
                        cp("act", topS[:, 1:33], ps[:, 0:32], [pk], ["topS"])
                        cp("dve", res["ctop"][:, kvi, g:g + 1], topS[:, 32:33], ["topS"], ["ctop"])
                        stt("dve", hsum[:, 0:32], topS[:, 0:32], Pm["posb"][:, kvi:kvi + 1], ps[:, 256:288], ALU.add, ALU.add,
                            ["topS", pk, ("posb", kvi)], ["hsum"])
                        if kvi == 0:
                            gelu_to(Gk[:, 0:32], hsum[:, 0:32], 32, glt, ["hsum"], ["Gk"])
                            cmp_k_finish(Pm, C, Gk[:, lo:32], 32 - lo, res["KcT"][:, g, col0 + lo:col0 + 32], wk, [("KcT", g)])
                        else:
                            gelu_to(res["Gv"][:, g, col0 + lo:col0 + 32], hsum[:, lo:32], 32 - lo, glt, ["hsum"], [("Gv", g)])
                            for ct in range((col0 + lo) // 128, (col0 + 31) // 128 + 1):
                                ps2, pk2 = nextF()
                                mm(ps2[:, 0:64], res["Gv"][:, g, ct * 128:(ct + 1) * 128], Pm["w2"][:, 1, :], True, True,
                                   [("Gv", g), "w2"], [pk2])
                                cp("act", res["Vc"][:, ct, g, 0:64], ps2[:, 0:64], [pk2], [("Vc", ct, g)])

            if CUTK(b) == 23:
                barrier()
                return
            for i in range(nt):
                off = i * L
                gi = b.bi * 4 + i if prompt else None
                if (CUTK(b) == 29 and i == 1) or (CUTK(b) == 30 and i == 2):
                    barrier()
                    return
                if not prompt:
                    kvk = [("kv", i, c) for c in range(3)]
                    cp("act", kst[:L, 0:768], kv[:L, i, 0:768], kvk, ["kst"])
                    cp("act", kst[:L, 768:1024], kv[:L, i, 1024:1280], kvk, ["kst"])
                    sample_seq_prepare(l, b, i, Pm, C, S, kv, kst, wk, glt, hsum, Gk)
                if prompt:
                    nj = 64
                    cur0 = 2 * gi
                    ms("pool", FB[:, 0:nj], 0.0, ["FB"])
                    ms("pool", FB[:, 0:1], 1000.0, ["FB"])
                    ms("pool", FB[0:64, max(cur0 - 1, 0):cur0 + 1], 1000.0, ["FB"])
                    ms("pool", FB[64:128, cur0:cur0 + 2], 1000.0, ["FB"])
                    if cur0 + 1 < nj:
                        ms("pool", FB[0:64, cur0 + 1:nj], -1e30, ["FB"])
                    if cur0 + 2 < nj:
                        ms("pool", FB[64:128, cur0 + 2:nj], -1e30, ["FB"])
                    ksel = 16
                else:
                    nj = cfg.NSEL_S
                    ms("pool", FB[:L, 0:nj], 0.0, ["FB"])
                    ms("pool", FB[:L, 0:1], 1000.0, ["FB"])
                    ms("pool", FB[:L, nj - 1:nj], 1000.0, ["FB"])
                    ksel = 15
                nct = (8 * gi + 6) // 128 + 1 if prompt else (cfg.NCMP_S + 127) // 128
                for ct in range(nct):
                    coff = 2048 * ct - (128 * gi if prompt else cfg.PAST)
                    ts("dve", cmpcol[:, ct:ct + 1], kp[:, 0:1], 16.0, 991.0 + coff, ALU.mult, ALU.add, ["kp"], ["cmpcol"])
                for g in range(4):
                    if CUTK(b) == 35 and i == CUTI and g == CUTG:
                        barrier()
                        return
                    gam, hb = g // 2, (g % 2) * 64
                    qrhs = QT[hb:hb + 64, gam * 4:gam * 4 + 4, off:off + L]
                    qkeys = [("QT", gam * 4 + r) for r in range(4)]
                    tiles = []
                    if prompt:
                        nmax = 8 * gi + 6
                        navail = 32 * (b.bi + 1) - 1
                        for ct in range(nmax // 128 + 1):
                            nk = min(128, navail - 128 * ct)
                            tiles.append(dict(KT=res["KcT"][hb:hb + 64, g, ct * 128:ct * 128 + nk], nk=nk,
                                              V=res["Vc"][:nk, ct, g, :], delta=0, ct=ct, kcol=cmpcol[:, ct:ct + 1],
                                              mask=(128 * gi - 2048 * ct - 31, -16, 1),
                                              kkeys=[("KcT", g)], vkeys=[("Vc", ct, g)]))
                    else:
                        for ct in range((cfg.NCMP_S + 127) // 128):
                            nk = min(128, cfg.NCMP_S - 128 * ct)
                            tiles.append(dict(KT=S["KcT"][hb:hb + 64, g, ct * 128:ct * 128 + nk], nk=nk,
                                              V=S["Vc"][:nk, ct, g, :], delta=0, ct=ct, kcol=cmpcol[:, ct:ct + 1],
                                              mask=(cfg.PAST - 2048 * ct - 31, -16, 1),
                                              kkeys=[("sKcT", g)], vkeys=[("sVc", ct, g)]))
                    attend(C, g, L, qrhs, qkeys, tiles, psF[2], "F2", wk, imp=(psF[5], "F5", nj))
                    if CUTK(b) == 24 and i == CUTI:
                        barrier()
                        return
                    oc3 = psF[2][:L, 0:260].rearrange("p (r e) -> p r e", e=65)
                    ts("dve", sm[:L, 0:4], oc3[:, :, 64], 1e-37, None, ALU.max, None, ["F2"], ["sm_rc"])
                    T.op("dve", ["sm_rc"], ["sm_rc"], lambda: V.reciprocal(out=sm[:L, 0:4], in_=sm[:L, 0:4]))
                    i3 = psF[5][:L, 0:4 * nj].rearrange("p (r j) -> p r j", j=nj)
                    imp_ = impS[:L, 0, 0:nj]
                    ts("dve", imp_, i3[:, 0, :], sm[:L, 0:1], None, ALU.mult, None, ["F5", "sm_rc"], ["imp"])
                    for r in range(1, 4):
                        stt("dve", imp_, i3[:, r, :], sm[:L, r:r + 1], imp_, ALU.mult, ALU.add, ["F5", "sm_rc", "imp"], ["imp"])
                    tt("dve", imp_, imp_, FB[:L, 0:nj], ALU.add, ["imp", "FB"], ["imp"])
                    w1_, w2_ = impS[:L, 1, 0:nj], impS[:L, 2, 0:nj]
                    T.op("dve", ["imp"], ["mx"], lambda: V.max(out=mx[:L, 0:8], in_=imp_))
                    T.op("dve", ["imp", "mx"], ["impw1"], lambda: V.match_replace(out=w1_, in_to_replace=mx[:L, 0:8], in_values=imp_, imm_value=-2e30))
                    T.op("dve", ["impw1"], ["mx"], lambda: V.max(out=mx[:L, 8:16], in_=w1_))
                    if ksel == 15:
                        ms("dve", mx[:L, 15:16], -2e30, ["mx"])
                    T.op("dve", ["impw1", "mx"], ["impw2"], lambda: V.match_replace(out=w2_, in_to_replace=mx[:L, 8:16], in_values=w1_, imm_value=-2e30))
                    ts("dve", NBb[:L, 0:nj], w2_, -1.5e30, NEG, ALU.is_ge, ALU.mult, ["impw2"], ["NBb"])
                    njt = nj
                    if C["dup"]:
                        ts("dve", NBb[:L, 64:64 + nj], w2_, -1.5e30, NEG, ALU.is_ge, ALU.mult, ["impw2"], ["NBb"])
                        njt = 128
                    psb, pbk = nextB()
                    tr(psb[:njt, 0:L], NBb[:L, 0:njt], ident_b[:L, :L], ["NBb", "ident_b"], [pbk])
                    cp("act", wk["NBT"][:njt, 0:4 * L].rearrange("p (r q) -> p r q", q=L),
                       psb[:njt, 0:L].unsqueeze(1).to_broadcast([njt, 4, L]), [pbk], ["NBT"])
                    if CUTK(b) == 25 and i == CUTI:
                        barrier()
                        return
                    tiles = []
                    if prompt:
                        for t in range(gi + 1):
                            tiles.append(dict(KT=res["KsT"][hb:hb + 64, gam, t * 128:(t + 1) * 128], nk=128,
                                              V=res["Vs"][:, t, g, :], delta=gi - t, eblk=t, nj=nj, hb=hb,
                                              mask="causal" if t == gi else None,
                                              kkeys=[("KsT", t)], vkeys=[("Vs", t)]))
                    else:
                        for pg in range(cfg.NPG):
                            tiles.append(dict(KT=S["KsT"][hb:hb + 64, gam, pg * 128:(pg + 1) * 128], nk=128,
                                              V=S["Vs"][:, pg, g, :], delta=cfg.NPG - pg, eblk=pg, nj=nj, hb=hb,
                                              mask=None, kkeys=["sKsT"], vkeys=["sVs"]))
                        tiles.append(dict(KT=S["KsT"][hb:hb + 64, gam, cfg.PAST:cfg.PAST + 8], nk=8,
                                          V=S["Vs"][:8, cfg.NPG, g, :], delta=0, eblk=None,
                                          mask="causal", kkeys=["sKsT"], vkeys=["sVs"]))
                    attend(C, g, L, qrhs, qkeys, tiles, psF[3], "F3", wk)
                    if CUTK(b) == 26 and i == CUTI:
                        barrier()
                        return
                    tiles = []
                    if prompt:
                        for t in range(max(0, gi - 4), gi + 1):
                            sl = t % 8
                            m = "causal" if t == gi else ("win" if t == gi - 4 else None)
                            tiles.append(dict(KT=res["KwT"][hb:hb + 64, gam, sl * 128:(sl + 1) * 128], nk=128,
                                              V=res["Vw"][:, sl, g, :], delta=gi - t, mask=m,
                                              kkeys=[("KwT", sl)], vkeys=[("Vw", sl)]))
                    else:
                        for t in range(4):
                            tiles.append(dict(KT=S["KwT"][hb:hb + 64, gam, t * 128:(t + 1) * 128], nk=128,
                                              V=S["Vw"][:, t, g, :], delta=4 - t, mask="win" if t == 0 else None,
                                              kkeys=["sKwT"], vkeys=["sVw"]))
                        tiles.append(dict(KT=S["KwT"][hb:hb + 64, gam, WIN:WIN + 8], nk=8, V=S["Vw"][:8, 4, g, :],
                                          delta=0, mask="causal", kkeys=["sKwT"], vkeys=["sVw"]))
                    attend(C, g, L, qrhs, qkeys, tiles, psF[4], "F4", wk)
                    if CUTK(b) == 27 and i == CUTI:
                        barrier()
                        return
                    dn = "att_%d_%s%d_%d_%d" % (l, b.kind, b.bi, i, g)
                    if dn in dbg_names:
                        for x, (bank, bkey) in enumerate(((2, "F2"), (3, "F3"), (4, "F4"))):
                            cp("act", dbgt[:L, x, :], psF[bank][:L, 0:260], [bkey], [("dbgt", x)])
                        dbg(dn, dbgt[:L], [("dbgt", x) for x in range(3)])
                        dbg("imp_" + dn, impS[:L, :, 0:nj], ["imp", "impw1", "impw2"])
                    g3 = gs[:L, i, g * 12:(g + 1) * 12].rearrange("p (r x) -> p r x", x=3)
                    for x, (bank, bkey) in enumerate(((2, "F2"), (3, "F3"), (4, "F4"))):
                        o3 = psF[bank][:L, 0:260].rearrange("p (r e) -> p r e", e=65)
                        ts("dve", sm[:L, 8 + 4 * x:12 + 4 * x], o3[:, :, 64], 1e-37, None, ALU.max, None, [bkey], [("rd", x)])
                        T.op("dve", [("rd", x)], [("rd", x)], lambda: V.reciprocal(out=sm[:L, 8 + 4 * x:12 + 4 * x], in_=sm[:L, 8 + 4 * x:12 + 4 * x]))
                        tt("dve", sm[:L, 8 + 4 * x:12 + 4 * x], sm[:L, 8 + 4 * x:12 + 4 * x], g3[:, :, x], ALU.mult,
                           [("rd", x), ("gs", i)], [("rd", x)])
                        dst = (t1 if x != 1 else t2)[:L, :].rearrange("p (r d) -> p r d", d=HD)
                        if x == 2:
                            dst = t2[:L, :].rearrange("p (r d) -> p r d", d=HD)
                        tt("dve", dst, o3[:, :, 0:64], sm[:L, 8 + 4 * x:12 + 4 * x].unsqueeze(2).to_broadcast([L, 4, HD]), ALU.mult,
                           [bkey, ("rd", x)], ["t2C" if x else "t1C"])
                        if x >= 1:
                            out_ = yc[:L, g * 256:(g + 1) * 256] if x == 2 else t1[:L, :]
                            tt("dve", out_, t1[:L, :], t2[:L, :], ALU.add, ["t1C", "t2C"], [("yc", g)] if x == 2 else ["t1C"])
                if CUTK(b) == 28 and i == CUTI:
                    barrier()
                    return
                yk = [("yc", g) for g in range(4)]
                act(ycn[:L, :], yc[:L, :], AF.Square, yk, ["ycn", "ycss"], accum_out=sm[:L, 32:33])
                if CUTK(b) == 31 and i == CUTI:
                    barrier()
                    return
                rsq(sm[:L, 33:34], sm[:L, 32:33], 1.0 / D_ATTN, ["ycss"], ["ycr"])
                stt("dve", ycn[:L, :], yc[:L, :], sm[:L, 33:34], gout[:L, :], ALU.mult, ALU.mult, yk + ["ycr", "goutC", "ycn"], ["ycn"])
                if CUTK(b) == 32 and i == CUTI:
                    barrier()
                    return
                psb, pbk = nextB()
                for m in range(8):
                    tr(psb[:, m * 128:m * 128 + L], ycn[:L, m * 128:(m + 1) * 128], ident_b[:L, :L], ["ycn", "ident_b"], [pbk])
                cp("act", mergedT[:, 8:16, off:off + L], psb[:, :].rearrange("p (k c) -> p k c", c=128)[:, :, 0:L],
                   [pbk], [("mT", 8 + m) for m in range(8)])
                if CUTK(b) == 33 and i == CUTI:
                    barrier()
                    return
                dbg("yc_%d_%s%d_%d" % (l, b.kind, b.bi, i), yc[:L, :], yk)
            barrier()

    for l in range(DEPTH):
        with ExitStack() as lctx:
            Pm = load_layer_params(lctx, l)
            nsa_layer_params(lctx, l, Pm)
            carry = {"A": sb(lctx, "carryA", [128, 4, 2]), "B": sb(lctx, "carryB", [128, 8, 3]),
                     "hT": sb(lctx, "hT", [128, D_SSM]), "hT_bf": sb(lctx, "hT_bf", [128, D_SSM], BF16)}
            barrier()
            for grp in ("p", "s"):
                with ExitStack() as gctx:
                    res = prompt_res(gctx) if (grp == "p" and stage >= 3) else None
                    blocks = [Blk(cfg, l, "p", i) for i in range(cfg.NBLK)] if grp == "p" else [Blk(cfg, l, "s", 0)]
                    for b in blocks:
                        with ExitStack() as bctx:
                            L, nt = b.L, b.nt
                            xT = sb(bctx, "xT", [128, KT, b.TT], BF16)
                            mergedT = sb(bctx, "mergedT", [128, KT, b.TT], BF16)
                            with ExitStack() as c0:
                                x_sb = sb(c0, "x_sb", [128, nt, D])
                                gmix = sb(c0, "gmix", [128, D])
                                xn_ = sb(c0, "xn", [128, D], BF16)
                                wk = {"junk": xn_, "xn": xn_, "st": sb(c0, "nst", [128, 8])}
                                dma("sp", gmix[:], I["norm_mix"][l].partition_broadcast(128), [], ["gmix"])
                                for i in range(nt):
                                    dma("sp", x_sb[:L, i, :], x_src(l, b, i), [], [("x", i)])
                                norm_T(b, x_sb, "x", gmix, "gmix", xT, "xT", wk)
                                barrier()
                            dbg("xT_%d_%s%d" % (l, b.kind, b.bi), xT[:], xT_keys("xT", b))
                            mixer_a(l, b, Pm, xT, mergedT, carry)
                            if stage <= 1:
                                continue
                            mixer_b(l, b, Pm, xT, mergedT, carry)
                            if stage <= 2:
                                continue
                            with ExitStack() as nctx:
                                C = nsa_consts(nctx, (64 if b.kind == "p" else cfg.NSEL_S) <= 64, SEQ if b.kind == "p" else cfg.PAST)
                                mixer_c(l, b, Pm, C, xT, mergedT, res)
                            if stage <= 3:
                                continue
                            dense_tail(l, b, Pm, xT, mergedT)
                    barrier()
            barrier()
        if stage <= 3:
            break
    T.finish()
    es.close()
    return nc, I, O, DBG, hc


N_CORES = 8
_WEIGHTS = ["norm_mix", "w_in", "conv_a_w", "conv_b_w", "conv_b_bias", "dt_bias", "a_log", "d_skip", "q_norm",
            "k_norm", "cmp_pos", "cmp_w1", "cmp_w2", "norm_out", "w_out", "norm_ffn", "w_gate", "w_up", "w_down"]
_PROG = {}


def _core_inputs(inp, c, cfg, hc):
    NS = cfg.NS
    f = lambda a: np.ascontiguousarray(np.asarray(a, dtype=np.float32))
    m = {}
    m["xp"] = f(inp["x_prompt"][c % 2])
    m["xs"] = f(inp["x_sample"][NS * c:NS * (c + 1)]).reshape(NS * 8, D)
    m["ck"] = f(inp["cache_k"]).reshape(DEPTH, -1, 512)
    m["cv"] = f(inp["cache_v"]).reshape(DEPTH, -1, 512)
    m["cwk"] = f(inp["cache_win_k"][:, NS * c:NS * (c + 1)]).reshape(DEPTH, NS, WIN, 256)
    m["cwv"] = f(inp["cache_win_v"][:, NS * c:NS * (c + 1)]).reshape(DEPTH, NS, WIN, 256)
    m["sca"] = f(inp["state_conv_a"][:, NS * c:NS * (c + 1)])
    m["scb"] = f(inp["state_conv_b"][:, NS * c:NS * (c + 1)])
    m["ssm"] = f(inp["state_ssm"][:, NS * c:NS * (c + 1)]).reshape(DEPTH, NS, D_SSM, SSM_N)
    m["pt"] = np.ascontiguousarray(np.asarray(inp["page_table"][NS * c:NS * (c + 1)], dtype=np.int32))
    for k in _WEIGHTS:
        m[k] = f(inp[k])
    m.update(hc)
    return m


def kernel(**inputs):
    seq = int(inputs["x_prompt"].shape[1])
    nb = int(inputs["x_sample"].shape[0])
    past = int(inputs["page_table"].shape[1]) * 128
    npool = int(inputs["cache_k"].shape[1])
    ns = nb // N_CORES
    key = (seq, past, ns, npool)
    if key not in _PROG:
        cfg = Cfg(seq=seq, past=past, ns=ns, npool=npool)
        _PROG[key] = (cfg,) + build(cfg)
    cfg, nc, I, O, DBG, hc = _PROG[key]
    in_maps = [_core_inputs(inputs, c, cfg, hc) for c in range(N_CORES)]
    res = run_bass_kernel_spmd(nc, in_maps, core_ids=list(range(N_CORES)))
    R_ = res.results
    B = int(inputs["x_prompt"].shape[0])
    pstack = lambda nm: np.stack([R_[b][nm] for b in range(B)], axis=1)
    scat = lambda nm: np.concatenate([R_[c][nm] for c in range(N_CORES)], axis=1)
    y_prompt = np.stack([R_[b]["yp"] for b in range(B)], axis=0)
    y_sample = np.concatenate([R_[c]["ys"] for c in range(N_CORES)], axis=0).reshape(nb, 8, D)
    outs = (
        y_prompt, y_sample,
        pstack("nkp").reshape(DEPTH, B, seq, 2, NG, HD), pstack("nvp").reshape(DEPTH, B, seq, 2, NG, HD),
        pstack("nwkp").reshape(DEPTH, B, WIN, NG, HD), pstack("nwvp").reshape(DEPTH, B, WIN, NG, HD),
        pstack("ncap"), pstack("ncbp"), pstack("nssp").reshape(DEPTH, B, SSM_H, HD, SSM_N),
        scat("nks").reshape(DEPTH, nb, 8, 2, NG, HD), scat("nvs").reshape(DEPTH, nb, 8, 2, NG, HD),
        scat("nwks").reshape(DEPTH, nb, WIN, NG, HD), scat("nwvs").reshape(DEPTH, nb, WIN, NG, HD),
        scat("ncas"), scat("ncbs"), scat("nsss").reshape(DEPTH, nb, SSM_H, HD, SSM_N),
    )
    return tuple(np.ascontiguousarray(o, dtype=np.float32) for o in outs)
```
